# Optimizing a Trainium2 kernel written in Bass

```python
import math
import jax, jax.numpy as jnp
from jax import lax
import numpy as np

D_MODEL = 1024
BATCH = 8
SEQ = 4096
DEPTH = 2

GRID_W = 64
CTX_LEN = 256
EPS = 1e-6
ROPE_BASE = 10000.0
D_FF = 4 * D_MODEL

RET_HEADS = 4
RET_DIM = 128
RET_W = RET_HEADS * RET_DIM
RET_CHUNK = 128

NA_HEADS = 8
NA_DIM = 64
NA_W = NA_HEADS * NA_DIM
NA_WIN_R = 8
NA_WIN_C = 16
NA_QCB = 16
NA_KCB = 32

EVEN_IN = 4 * RET_W + 3 * NA_W
MIX_W = RET_W + NA_W

SSM_INNER = 2 * D_MODEL
SSM_HEAD_DIM = 64
SSM_HEADS = SSM_INNER // SSM_HEAD_DIM
SSM_GROUPS = 4
SSM_HPG = SSM_HEADS // SSM_GROUPS
SSM_STATE = 128
SSM_CONV = 7
SSM_CHUNK = 128
SSM_GN = SSM_GROUPS * SSM_STATE
SSM_CONV_CH = SSM_INNER + 2 * SSM_GN
ODD_IN = SSM_INNER + SSM_CONV_CH + 2 * SSM_HEADS

kernel_name = 'hybrid_retention_natten_ssd_dit_block'


def _rmsnorm(x, g):
    xf = x.astype(jnp.float32)
    y = xf * lax.rsqrt(jnp.mean(xf * xf, axis=-1, keepdims=True) + EPS)
    return (y * g.astype(jnp.float32)).astype(x.dtype)


def _head_norm(y):
    mu = jnp.mean(y, axis=-1, keepdims=True)
    var = jnp.mean(jnp.square(y - mu), axis=-1, keepdims=True)
    return (y - mu) * lax.rsqrt(var + EPS)


def _mlp(h, w1, w2):
    return jnp.square(jax.nn.relu(h @ w1)) @ w2


def _flip(t):
    return jnp.flip(t, axis=1)


def _rope_2d(t, rpos, cpos):
    half = t.shape[-1] // 2
    nf = half // 2
    inv = ROPE_BASE ** (-jnp.arange(nf, dtype=jnp.float32) / nf)

    def rot(u, pos):
        ang = pos.astype(jnp.float32)[:, None] * inv[None, :]
        cos = jnp.cos(ang)[None, :, None, :]
        sin = jnp.sin(ang)[None, :, None, :]
        u1, u2 = u[..., :nf], u[..., nf:]
        return jnp.concatenate([u1 * cos - u2 * sin, u1 * sin + u2 * cos], axis=-1)

    return jnp.concatenate([rot(t[..., :half], rpos), rot(t[..., half:], cpos)], axis=-1)


def _chunks(t, size):
    b, l = t.shape[:2]
    return t.reshape(b, l // size, size, *t.shape[2:]).swapaxes(0, 1)


def _unchunk(t):
    n, b, cs = t.shape[:3]
    return t.swapaxes(0, 1).reshape(b, n * cs, *t.shape[3:])


def _retention_scan(q, k, v, log_g, s0, with_y):
    idx = jnp.arange(RET_CHUNK, dtype=jnp.float32)
    diff = idx[:, None] - idx[None, :]
    dmat = jnp.where(diff[None] >= 0, jnp.exp(jnp.maximum(diff, 0.0)[None] * log_g[:, None, None]), 0.0)
    q_dec = jnp.exp((idx[:, None] + 1.0) * log_g[None, :])
    k_dec = jnp.exp((RET_CHUNK - 1.0 - idx)[:, None] * log_g[None, :])
    c_dec = jnp.exp(RET_CHUNK * log_g)

    def body(s, inp):
        qc, kc, vc = inp
        s_new = c_dec[None, :, None, None] * s + jnp.einsum('bjhd,bjhe->bhde', kc * k_dec[None, :, :, None], vc)
        if not with_y:
            return s_new, None
        att = jnp.einsum('bihd,bjhd->bhij', qc, kc) * dmat[None]
        y = jnp.einsum('bhij,bjhe->bihe', att, vc) + jnp.einsum('bihd,bhde->bihe', qc, s) * q_dec[None, :, :, None]
        return s_new, y

    s, ys = lax.scan(body, s0, (_chunks(q, RET_CHUNK), _chunks(k, RET_CHUNK), _chunks(v, RET_CHUNK)))
    return (_unchunk(ys) if with_y else None), s


def _retention_bidir(q, k, v, log_g, s0_f, s0_b, with_y):
    y_f, s_f = _retention_scan(q, k, v, log_g[0], s0_f, with_y)
    y_b, s_b = _retention_scan(_flip(q), _flip(k), _flip(v), log_g[1], s0_b, with_y)
    y = y_f + _flip(y_b) if with_y else None
    return y, s_f, s_b


def _na_col_tables():
    nqb = GRID_W // NA_QCB
    qcol = np.arange(GRID_W).reshape(nqb, NA_QCB)
    kstart = np.clip(np.arange(nqb) * NA_QCB - NA_WIN_C // 2, 0, GRID_W - NA_KCB)
    kcol = kstart[:, None] + np.arange(NA_KCB)[None, :]
    cstart = np.clip(qcol - NA_WIN_C // 2, 0, GRID_W - NA_WIN_C)
    kk = kcol[:, None, :]
    cmask = (kk >= cstart[:, :, None]) & (kk < cstart[:, :, None] + NA_WIN_C)
    dcol = np.clip(kk - qcol[:, :, None], -(NA_WIN_C - 1), NA_WIN_C - 1) + (NA_WIN_C - 1)
    return kcol, cmask, dcol


def _na_latent(q, k, v, kc, vc, rpb, rows):
    b, _, nh, dh = q.shape
    wr = min(NA_WIN_R, rows)
    nqb = GRID_W // NA_QCB
    kcol, cmask_np, dcol = _na_col_tables()
    scale = dh ** -0.5
    qg = q.reshape(b, rows, nqb, NA_QCB, nh, dh)
    kg = k.reshape(b, rows, GRID_W, nh, dh)
    vg = v.reshape(b, rows, GRID_W, nh, dh)
    rpb_c = rpb[:, :, dcol]
    cmask = jnp.asarray(cmask_np)[:, :, None, :]

    def one_row(r):
        rs = jnp.clip(r - wr // 2, 0, rows - wr)
        kr = lax.dynamic_slice_in_dim(kg, rs, wr, axis=1)[:, :, kcol]
        vr = lax.dynamic_slice_in_dim(vg, rs, wr, axis=1)[:, :, kcol]
        qr = lax.dynamic_index_in_dim(qg, r, axis=1, keepdims=False)
        s_loc = jnp.einsum('bnqhd,brnkhd->bhnqrk', qr, kr).astype(jnp.float32) * scale
        dr = rs + jnp.arange(wr) - r + (NA_WIN_R - 1)
        bias = jnp.transpose(rpb_c[:, dr], (0, 2, 3, 1, 4)).astype(jnp.float32)
        s_loc = jnp.where(cmask, s_loc + bias, -jnp.inf).reshape(b, nh, nqb, NA_QCB, wr * NA_KCB)
        s_ctx = jnp.einsum('bnqhd,bchd->bhnqc', qr, kc).astype(jnp.float32) * scale
        p = jax.nn.softmax(jnp.concatenate([s_loc, s_ctx], axis=-1), axis=-1)
        p_loc = p[..., :wr * NA_KCB].reshape(b, nh, nqb, NA_QCB, wr, NA_KCB)
        p_ctx = p[..., wr * NA_KCB:]
        o = (jnp.einsum('bhnqrk,brnkhd->bnqhd', p_loc, vr.astype(jnp.float32))
             + jnp.einsum('bhnqc,bchd->bnqhd', p_ctx, vc.astype(jnp.float32)))
        return o.reshape(b, GRID_W, nh, dh).astype(q.dtype)

    out = lax.map(one_row, jnp.arange(rows))
    return jnp.transpose(out, (1, 0, 2, 3, 4)).reshape(b, rows * GRID_W, nh, dh)


def _ctx_attn(q, k, v):
    s = jnp.einsum('bqhd,bkhd->bhqk', q, k).astype(jnp.float32) * (q.shape[-1] ** -0.5)
    p = jax.nn.softmax(s, axis=-1)
    return jnp.einsum('bhqk,bkhd->bqhd', p, v.astype(jnp.float32)).astype(q.dtype)


def _dwconv(x, w, bias):
    ch = x.shape[-1]
    y = lax.conv_general_dilated(x, w.astype(x.dtype)[:, None, :], window_strides=(1,),
                                 padding=[(SSM_CONV // 2, SSM_CONV // 2)],
                                 dimension_numbers=('NWC', 'WIO', 'NWC'), feature_group_count=ch)
    return y + bias.astype(x.dtype)


def _ssd_scan(x, dt, a, bm, cm, h0, with_y):
    tri = jnp.tril(jnp.ones((SSM_CHUNK, SSM_CHUNK), dtype=bool))[None, :, :, None, None]

    def body(h, inp):
        xc, dtc, bc, cc = inp
        acum = jnp.cumsum(dtc * a, axis=1)
        last = acum[:, -1]
        w_end = jnp.exp(last[:, None] - acum) * dtc
        h_new = jnp.exp(last)[..., None, None] * h + jnp.einsum('bjgh,bjgn,bjghp->bghpn', w_end, bc, xc)
        if not with_y:
            return h_new, None
        lmat = jnp.exp(jnp.where(tri, acum[:, :, None] - acum[:, None, :], -jnp.inf))
        cb = jnp.einsum('bign,bjgn->bijg', cc, bc)
        y = jnp.einsum('bijgh,bjghp->bighp', lmat * cb[..., None] * dtc[:, None], xc)
        y = y + jnp.einsum('bign,bghpn->bighp', cc, h) * jnp.exp(acum)[..., None]
        return h_new, y

    h, ys = lax.scan(body, h0, (_chunks(x, SSM_CHUNK), _chunks(dt, SSM_CHUNK), _chunks(bm, SSM_CHUNK), _chunks(cm, SSM_CHUNK)))
    return (_unchunk(ys) if with_y else None), h


def _ssd_bidir(x, dt, a, bm, cm, h0_f, h0_b, with_y):
    y_f, h_f = _ssd_scan(x, dt[:, :, 0], a[0], bm, cm, h0_f, with_y)
    y_b, h_b = _ssd_scan(_flip(x), _flip(dt[:, :, 1]), a[1], _flip(bm), _flip(cm), h0_b, with_y)
    y = y_f + _flip(y_b) if with_y else None
    return y, h_f, h_b


def _even_mixer(h, hc, w_in, w_out, decay_logit, rpb, rows, ctx_out):
    b, l, _ = h.shape
    cuts = [RET_W, 2 * RET_W, 3 * RET_W, 4 * RET_W, 4 * RET_W + NA_W, 4 * RET_W + 2 * NA_W]
    rq, rk, rv, rg, nq, nk, nv = jnp.split(h @ w_in, cuts, axis=-1)
    crq, crk, crv, crg, cnq, cnk, cnv = jnp.split(hc @ w_in, cuts, axis=-1)
    pos = jnp.arange(l)
    rpos = pos // GRID_W
    cpos = pos % GRID_W

    def ret_heads(t):
        return t.reshape(t.shape[0], t.shape[1], RET_HEADS, RET_DIM).astype(jnp.float32)

    def na_heads(t):
        return t.reshape(t.shape[0], t.shape[1], NA_HEADS, NA_DIM)

    def ret_finish(y, g):
        return (_head_norm(y) * jax.nn.silu(ret_heads(g))).reshape(y.shape[0], y.shape[1], RET_W).astype(h.dtype)

    kscale = RET_DIM ** -0.5
    log_g = jax.nn.log_sigmoid(decay_logit.astype(jnp.float32))
    s0 = jnp.zeros((b, RET_HEADS, RET_DIM, RET_DIM), jnp.float32)
    yc_ret, s_f, s_b = _retention_bidir(ret_heads(crq), ret_heads(crk) * kscale, ret_heads(crv), log_g, s0, s0, ctx_out)
    y_ret, _, _ = _retention_bidir(_rope_2d(ret_heads(rq), rpos, cpos), _rope_2d(ret_heads(rk), rpos, cpos) * kscale,
                                   ret_heads(rv), log_g, s_f, s_b, True)
    ck, cv = na_heads(cnk), na_heads(cnv)
    y_na = _na_latent(na_heads(nq), na_heads(nk), na_heads(nv), ck, cv, rpb, rows).reshape(b, l, NA_W)
    out = jnp.concatenate([ret_finish(y_ret, rg), y_na], axis=-1) @ w_out
    if not ctx_out:
        return out, None
    y_na_c = _ctx_attn(na_heads(cnq), ck, cv).reshape(b, hc.shape[1], NA_W)
    out_c = jnp.concatenate([ret_finish(yc_ret, crg), y_na_c], axis=-1) @ w_out
    return out, out_c


def _odd_mixer(h, hc, w_in, conv_w, conv_b, dt_bias, a_log, d_skip, norm_w, w_out, ctx_out):
    def prep(t):
        bsz, l = t.shape[:2]
        z, xbc, dt = jnp.split(t @ w_in, [SSM_INNER, SSM_INNER + SSM_CONV_CH], axis=-1)
        xbc = jax.nn.silu(_dwconv(xbc, conv_w, conv_b))
        xs, bm, cm = jnp.split(xbc, [SSM_INNER, SSM_INNER + SSM_GN], axis=-1)
        xs = xs.reshape(bsz, l, SSM_GROUPS, SSM_HPG, SSM_HEAD_DIM).astype(jnp.float32)
        bm = bm.reshape(bsz, l, SSM_GROUPS, SSM_STATE).astype(jnp.float32)
        cm = cm.reshape(bsz, l, SSM_GROUPS, SSM_STATE).astype(jnp.float32)
        dt = jax.nn.softplus(dt.astype(jnp.float32) + dt_bias.reshape(-1).astype(jnp.float32))
        return z, xs, bm, cm, dt.reshape(bsz, l, 2, SSM_GROUPS, SSM_HPG)

    def finish(y, xs, z):
        bsz, l = y.shape[:2]
        y = (y + d_skip.reshape(SSM_GROUPS, SSM_HPG, 1).astype(jnp.float32) * xs).reshape(bsz, l, SSM_INNER)
        y = _rmsnorm(y * jax.nn.silu(z.astype(jnp.float32)), norm_w)
        return y.astype(h.dtype) @ w_out

    a = -jnp.exp(a_log.astype(jnp.float32)).reshape(2, SSM_GROUPS, SSM_HPG)
    z, xs, bm, cm, dt = prep(h)
    cz, cxs, cbm, ccm, cdt = prep(hc)
    h0 = jnp.zeros((h.shape[0], SSM_GROUPS, SSM_HPG, SSM_HEAD_DIM, SSM_STATE), jnp.float32)
    yc, h_f, h_b = _ssd_bidir(cxs, cdt, a, cbm, ccm, h0, h0, ctx_out)
    y, _, _ = _ssd_bidir(xs, dt, a, bm, cm, h_f, h_b, True)
    out = finish(y, xs, z)
    if not ctx_out:
        return out, None
    return out, finish(yc, cxs, cz)


def setup_inputs(seed: int = 0) -> dict:
    key = jax.random.key(seed)
    ks = jax.random.split(key, 24)
    n_even = (DEPTH + 1) // 2
    n_odd = DEPTH // 2

    def nrm(k, shape, s):
        return jax.random.normal(k, shape, jnp.float32) * s

    gamma = 1.0 - 2.0 ** (-5.0 - np.arange(RET_HEADS, dtype=np.float32))
    logit0 = jnp.asarray(np.log(gamma / (1.0 - gamma)).astype(np.float32))
    dt0 = jnp.exp(jax.random.uniform(ks[17], (n_odd, 2, SSM_HEADS), jnp.float32, math.log(1e-3), math.log(1e-1)))
    return {
        'x': nrm(ks[0], (BATCH, SEQ, D_MODEL), 1.0),
        'c': nrm(ks[1], (BATCH, D_MODEL), 1.0),
        'ctx': nrm(ks[2], (BATCH, CTX_LEN, D_MODEL), 1.0),
        'c_ctx': nrm(ks[3], (D_MODEL,), 1.0),
        'w_mod': nrm(ks[4], (DEPTH, D_MODEL, 6 * D_MODEL), 0.5 * D_MODEL ** -0.5),
        'b_mod': nrm(ks[5], (DEPTH, 6 * D_MODEL), 0.01),
        'norm_mix': 1.0 + nrm(ks[6], (DEPTH, D_MODEL), 0.02),
        'norm_mlp': 1.0 + nrm(ks[7], (DEPTH, D_MODEL), 0.02),
        'w_mlp_in': nrm(ks[8], (DEPTH, D_MODEL, D_FF), D_MODEL ** -0.5),
        'w_mlp_out': nrm(ks[9], (DEPTH, D_FF, D_MODEL), D_FF ** -0.5),
        'w_in_even': nrm(ks[10], (n_even, D_MODEL, EVEN_IN), D_MODEL ** -0.5),
        'w_out_even': nrm(ks[11], (n_even, MIX_W, D_MODEL), MIX_W ** -0.5),
        'ret_decay_logit': logit0[None, None, :] + nrm(ks[12], (n_even, 2, RET_HEADS), 0.01),
        'na_rpb': nrm(ks[13], (n_even, NA_HEADS, 2 * NA_WIN_R - 1, 2 * NA_WIN_C - 1), 0.02),
        'w_in_odd': nrm(ks[14], (n_odd, D_MODEL, ODD_IN), D_MODEL ** -0.5),
        'conv_w': nrm(ks[15], (n_odd, SSM_CONV, SSM_CONV_CH), SSM_CONV ** -0.5),
        'conv_b': nrm(ks[16], (n_odd, SSM_CONV_CH), 0.01),
        'dt_bias': dt0 + jnp.log(-jnp.expm1(-dt0)),
        'a_log': jnp.log(jax.random.uniform(ks[18], (n_odd, 2, SSM_HEADS), jnp.float32, 1.0, 16.0)),
        'd_skip': 1.0 + nrm(ks[19], (n_odd, SSM_HEADS), 0.02),
        'ssm_norm': 1.0 + nrm(ks[20], (n_odd, SSM_INNER), 0.02),
        'w_out_odd': nrm(ks[21], (n_odd, SSM_INNER, D_MODEL), SSM_INNER ** -0.5),
        'norm_final': 1.0 + nrm(ks[22], (D_MODEL,), 0.02),
    }


def reference(x, c, ctx, c_ctx, w_mod, b_mod, norm_mix, norm_mlp, w_mlp_in, w_mlp_out,
              w_in_even, w_out_even, ret_decay_logit, na_rpb,
              w_in_odd, conv_w, conv_b, dt_bias, a_log, d_skip, ssm_norm, w_out_odd, norm_final):
    rows = x.shape[1] // GRID_W
    hctx = ctx
    for layer in range(DEPTH):
        last = layer == DEPTH - 1
        j = layer // 2
        mod = jax.nn.silu(c) @ w_mod[layer] + b_mod[layer]
        mod_c = jax.nn.silu(c_ctx) @ w_mod[layer] + b_mod[layer]
        sh1, sc1, g1, sh2, sc2, g2 = jnp.split(mod[:, None, :], 6, axis=-1)
        csh1, csc1, cg1, csh2, csc2, cg2 = jnp.split(mod_c, 6, axis=-1)
        h = _rmsnorm(x, norm_mix[layer]) * (1.0 + sc1) + sh1
        hc = _rmsnorm(hctx, norm_mix[layer]) * (1.0 + csc1) + csh1
        if layer % 2 == 0:
            y, yc = _even_mixer(h, hc, w_in_even[j], w_out_even[j], ret_decay_logit[j], na_rpb[j], rows, not last)
        else:
            y, yc = _odd_mixer(h, hc, w_in_odd[j], conv_w[j], conv_b[j], dt_bias[j], a_log[j], d_skip[j],
                               ssm_norm[j], w_out_odd[j], not last)
        x = x + g1 * y
        x = x + g2 * _mlp(_rmsnorm(x, norm_mlp[layer]) * (1.0 + sc2) + sh2, w_mlp_in[layer], w_mlp_out[layer])
        if not last:
            hctx = hctx + cg1 * yc
            hctx = hctx + cg2 * _mlp(_rmsnorm(hctx, norm_mlp[layer]) * (1.0 + csc2) + csh2,
                                     w_mlp_in[layer], w_mlp_out[layer])
    return _rmsnorm(x, norm_final)
```

```python
import numpy as np
import ml_dtypes
import concourse.bass as bass
import concourse.mybir as mybir
from concourse.bass_utils import run_bass_kernel_spmd

F32 = mybir.dt.float32
BF16 = mybir.dt.bfloat16
AF = mybir.ActivationFunctionType
ALU = mybir.AluOpType
AX = mybir.AxisListType


class Buf:
    __slots__ = ("name", "w", "r")

    def __init__(self, name=""):
        self.name = name
        self.w = None
        self.r = {}


class Ctx:
    ENG = ("pe", "act", "dve", "pool", "sp")

    def __init__(self, nc, n_dma_sems=36):
        self.nc = nc
        self.eng = {"pe": nc.tensor, "act": nc.scalar, "dve": nc.vector,
                    "pool": nc.gpsimd, "sp": nc.sync}
        self.sem = {}
        self.cnt = {}
        for e in self.ENG:
            self.sem[e] = nc.alloc_semaphore("sem_" + e)
            self.cnt[e] = 0
        self.dma_sems = []
        for i in range(n_dma_sems):
            k = "dma%d" % i
            self.sem[k] = nc.alloc_semaphore(k)
            self.cnt[k] = 0
            self.dma_sems.append(k)
        self.dma_rr = 0
        self.grp_sems = []
        for i in range(20):
            k = "grp%d" % i
            self.sem[k] = nc.alloc_semaphore(k)
            self.cnt[k] = 0
            self.grp_sems.append(k)
        self.grp_rr = 0
        self.seen = {e: {} for e in self.ENG}
        self.nwait = 0
        self.nins = 0

    def _wait(self, e, key, val):
        if val <= 0:
            return
        if self.seen[e].get(key, 0) >= val:
            return
        self.eng[e].wait_ge(self.sem[key], val)
        self.seen[e][key] = val
        self.nwait += 1

    def _deps(self, e, reads, writes):
        deps = {}

        def add(k, v):
            if deps.get(k, 0) < v:
                deps[k] = v
        for b in reads:
            if b.w is not None:
                add(*b.w)
        for b in writes:
            if b.w is not None:
                if not (b.w[0] == e):
                    add(*b.w)
            for k, v in b.r.items():
                if k != e:
                    add(k, v)
        if e == "pe":
            deps.pop("pe", None)
        for k, v in deps.items():
            self._wait(e, k, v)

    def _mark(self, key, val, reads, writes):
        for b in reads:
            if b.r.get(key, 0) < val:
                b.r[key] = val
        for b in writes:
            b.w = (key, val)
            b.r = {}

    def op(self, e, reads, writes, fn, inc=True):
        self._deps(e, reads, writes)
        ins = fn(self.eng[e])
        if inc:
            self.cnt[e] += 1
            ins.then_inc(self.sem[e], 1)
            self._mark(e, self.cnt[e], reads, writes)
        else:
            self._mark(e, self.cnt[e] + 1, reads, writes)
        self.nins += 1
        return ins

    def new_group(self):
        k = self.grp_sems[self.grp_rr]
        self.grp_rr = (self.grp_rr + 1) % len(self.grp_sems)
        return k

    def dma(self, q, out_ap, in_ap, reads, writes, group=None, **kw):
        if group is not None:
            k = group
        else:
            k = self.dma_sems[self.dma_rr]
            self.dma_rr = (self.dma_rr + 1) % len(self.dma_sems)
            self._wait(q, k, self.cnt[k])
        self._deps(q, reads, writes)
        ins = self.eng[q].dma_start(out=out_ap, in_=in_ap, **kw)
        self.cnt[k] += 16
        ins.then_inc(self.sem[k], 16)
        self._mark(k, self.cnt[k], reads, writes)
        self.nins += 1
        return ins

    def barrier(self):
        for e in self.ENG:
            for k in list(self.cnt.keys()):
                if k == e and e == "pe":
                    continue
                self._wait(e, k, self.cnt[k])

    def final_wait(self, bufs):
        for b in bufs:
            if b.w is not None:
                self._wait("sp", *b.w)


D = 1024
T = 4096
CTX = 256
NTL = T // 128
NTC = CTX // 128
NT = NTL + NTC
EPS = 1e-6
NEG = -30000.0
EVEN_IN = 3584
ODD_IN = 5184
XLT = 262 + 4102


class Rot:
    def __init__(self, K, name, shape, dt, n, psum=False):
        self.items = []
        for i in range(n):
            nm = "%s_%d_%d" % (name, K.uid(), i)
            if psum:
                t = K.stack.enter_context(K.nc.psum_tensor(nm, shape, dt))
            else:
                t = K.stack.enter_context(K.nc.sbuf_tensor(nm, shape, dt))
            self.items.append((t, Buf(nm)))
        self.i = 0

    def get(self):
        it = self.items[self.i]
        self.i = (self.i + 1) % len(self.items)
        return it


class Kern:
    def __init__(self, dbg=False, stages=None):
        from contextlib import ExitStack
        self.dbg = dbg
        self.stages = stages
        self.nc = bass.Bass("TRN2", target_bir_lowering=False)
        self.cx = Ctx(self.nc)
        self._uid = 0
        self.gstack = ExitStack()
        self.stack = self.gstack
        self.dram = {}
        self.dbufs = {}
        self.outputs = []

    def uid(self):
        self._uid += 1
        return self._uid

    def din(self, name, shape, dt=F32):
        self.dram[name] = self.nc.dram_tensor(name, list(shape), dt, kind="ExternalInput").ap()
        return self.dram[name]

    def dscr(self, name, shape, dt, out=False):
        kind = "ExternalOutput" if (out or self.dbg) else "Internal"
        self.dram[name] = self.nc.dram_tensor(name, list(shape), dt, kind=kind).ap()
        if kind == "ExternalOutput":
            self.outputs.append(name)
        return self.dram[name]

    def db(self, name, idx=0):
        k = (name, idx)
        if k not in self.dbufs:
            self.dbufs[k] = Buf("%s_%s" % (name, idx))
        return self.dbufs[k]

    def sb(self, name, shape, dt=F32):
        t = self.stack.enter_context(self.nc.sbuf_tensor("%s_%d" % (name, self.uid()), list(shape), dt))
        return t, Buf(name)

    def phase(self):
        from contextlib import ExitStack, contextmanager
        K = self

        @contextmanager
        def cm():
            K.cx.barrier()
            old = K.stack
            with ExitStack() as st:
                K.stack = st
                yield
                K.cx.barrier()
            K.stack = old
        return cm()

    def mm(self, out, lhsT, rhs, start, stop, r, w):
        return self.cx.op("pe", r, w, lambda e: e.matmul(out, lhsT, rhs, start=start, stop=stop), inc=bool(stop))

    def tr(self, out, in_, ident, r, w):
        return self.cx.op("pe", r, w, lambda e: e.transpose(out, in_, ident))

    def act(self, out, in_, func, r, w, bias=None, scale=None, accum_out=None, eng="act"):
        kw = {}
        if bias is not None:
            kw["bias"] = bias
        if scale is not None:
            kw["scale"] = scale
        if accum_out is not None:
            kw["accum_out"] = accum_out
        return self.cx.op("act", r, w, lambda e: e.activation(out, in_, func, **kw))

    def tt(self, eng, out, in0, in1, op, r, w):
        return self.cx.op(eng, r, w, lambda e: e.tensor_tensor(out, in0, in1, op))

    def ts(self, eng, out, in0, s1, s2, op0, op1, r, w):
        if op1 is None:
            return self.cx.op(eng, r, w, lambda e: e.tensor_scalar(out, in0, s1, None, op0))
        return self.cx.op(eng, r, w, lambda e: e.tensor_scalar(out, in0, s1, s2, op0, op1))

    def stt(self, out, in0, scalar, in1, op0, op1, r, w):
        return self.cx.op("dve", r, w, lambda e: e.scalar_tensor_tensor(out, in0, scalar, in1, op0, op1))

    def cp(self, eng, out, in_, r, w):
        if eng == "act":
            return self.cx.op("act", r, w, lambda e: e.copy(out, in_))
        return self.cx.op(eng, r, w, lambda e: e.tensor_copy(out, in_))

    def memset(self, eng, ap, val, w):
        return self.cx.op(eng, [], w, lambda e: e.memset(ap, val))

    def dma(self, q, out, in_, r, w, **kw):
        return self.cx.dma(q, out, in_, r, w, **kw)


def setup(K):
    nc = K.nc
    K.din("x", [T, D]); K.din("ctx", [CTX, D])
    K.din("c", [8, 128]); K.din("c_ctx", [8, 128])
    K.din("w_mod", [2, D, 6 * D]); K.din("b_mod", [2, 48, 128])
    K.din("norm_mix", [2, 8, 128]); K.din("norm_mlp", [2, 8, 128])
    K.din("w_mlp_in", [2, 128, 8, 4 * D]); K.din("w_mlp_out", [2, 128, 32, D])
    K.din("w_in_even", [128, 8, EVEN_IN]); K.din("w_out_even", [128, 8, D])
    K.din("ret_decay_logit", [8]); K.din("na_bias", [128, 8, 9, 128])
    K.din("w_in_odd", [128, 8, ODD_IN]); K.din("conv_w", [7, 24, 128]); K.din("conv_b", [24, 128])
    K.din("dt_bias", [64]); K.din("a_log", [64]); K.din("d_skip", [32])
    K.din("ssm_norm", [16, 128]); K.din("w_out_odd", [128, 16, D]); K.din("norm_final", [D])
    K.din("c_ident", [128, 128]); K.din("c_rope", [NTL, 128, 4, 128])
    K.din("c_tri", [128, 8, 128])
    K.din("c_col", [128, 2])
    K.dscr("y", [T, D], F32, out=True)
    for nm in ("X1", "X2", "X3"):
        K.dscr(nm, [NT * 128, D], F32)
    for nm in ("RQ", "RK", "RV", "RG", "NAO"):
        K.dscr(nm, [NT * 128, 512], BF16)
    K.dscr("ZS", [NT * 128, 2048], BF16)
    K.dscr("DT", [NT * 128, 64], F32)
    K.dscr("XBCT", [24, 128, XLT], BF16)
    K.dscr("XS", [NT * 128, 2048], BF16)
    K.dscr("BS", [NT * 128, 512], BF16)
    K.dscr("BT", [4, 128, NT * 128], BF16)
    K.dscr("CTt", [4, 128, NT * 128], BF16)
    K.dscr("YB", [NT * 128, 2048], BF16)
    K.ident, K.b_ident = K.sb("ident", [128, 128])
    K.identb, K.b_identb = K.sb("identb", [128, 128], BF16)
    K.ones, K.b_ones = K.sb("ones", [128, 128])
    K.dma("sp", K.ident[:], K.dram["c_ident"][:, :], [], [K.b_ident])
    K.cp("dve", K.identb[:], K.ident[:], [K.b_ident], [K.b_identb])
    K.memset("dve", K.ones[:], 1.0, [K.b_ones])
    K.PS = Rot(K, "ps", [128, 512], F32, 8, psum=True)
    K.PSa = Rot.__new__(Rot); K.PSa.items = K.PS.items[0:4]; K.PSa.i = 0
    K.PSb = Rot.__new__(Rot); K.PSb.items = K.PS.items[4:8]; K.PSb.i = 0
    K.modT = []
    K.cols = {}


def xsrc(K, stream, tt):
    if stream == "X0":
        if tt < NTC:
            return K.dram["ctx"][tt * 128:(tt + 1) * 128, :], K.db("ctx", tt)
        return K.dram["x"][(tt - NTC) * 128:(tt - NTC + 1) * 128, :], K.db("x", tt)
    if stream == "Y":
        return K.dram["y"][(tt - NTC) * 128:(tt - NTC + 1) * 128, :], K.db("y", tt)
    return K.dram[stream][tt * 128:(tt + 1) * 128, :], K.db(stream, tt)


def load_cols(K, specs):
    tot = sum(ap.shape[0] for _, ap in specs)
    assert tot <= 128
    stg, bstg = K.sb("stg", [128, 128])
    dst, bdst = K.sb("cols", [128, tot])
    r0 = 0
    offs = {}
    for nm, ap in specs:
        R = ap.shape[0]
        K.dma("sp", stg[r0:r0 + R, :], ap, [], [bstg])
        offs[nm] = (r0, R)
        r0 += R
    ps, bps = K.PS.get()
    K.tr(ps[:, 0:tot], stg[0:tot, :], K.ident[0:tot, 0:tot], [bstg, K.b_ident], [bps])
    K.cp("dve", dst[:], ps[:, 0:tot], [bps], [bdst])
    for nm, (o, R) in offs.items():
        K.cols[nm] = (dst[:, o:o + R], bdst)


def phase_mod(K):
    d = K.dram
    load_cols(K, [("c", d["c"][:, :]), ("c_ctx", d["c_ctx"][:, :]),
                  ("b_mod0", d["b_mod"][0]), ("b_mod1", d["b_mod"][1])])
    load_cols(K, [("norm_mix0", d["norm_mix"][0]), ("norm_mix1", d["norm_mix"][1]),
                  ("norm_mlp0", d["norm_mlp"][0]), ("norm_mlp1", d["norm_mlp"][1]),
                  ("ssm_norm", d["ssm_norm"][:, :]), ("conv_b", d["conv_b"][:, :])])
    load_cols(K, [("conv_w%d" % k, d["conv_w"][k]) for k in range(4)])
    load_cols(K, [("conv_w%d" % k, d["conv_w"][k]) for k in range(4, 7)])
    sc, bsc = K.sb("sc", [128, 8, 2])
    cc, bcc = K.cols["c"]
    K.act(sc[:, :, 0], cc, AF.Silu, [bcc], [bsc])
    K.act(sc[:, :, 1], K.cols["c_ctx"][0], AF.Silu, [bcc], [bsc])
    K.A = {}
    K.G = {}
    for l in range(2):
        K.modT.append(K.gsb("modT%d" % l, [128, 48, 2]))
    A_pre = {(l, s, which): K.gsb("A", [128, 8]) for l in range(2) for s in range(2) for which in range(2)}
    G_pre = {(l, s, which): K.gsb("G", [128, 1024], BF16) for l in range(2) for s in range(2) for which in range(2)
             if not (l == 1 and s == 1)}
    with K.phase():
        wrot = Rot(K, "wmod", [128, 8, 1024], F32, 2)
        for l in range(2):
            modT, bmod = K.modT[l]
            ps, bps = K.PS.get()
            for cg in range(6):
                wp, bwp = wrot.get()
                for k in range(8):
                    K.dma("sp", wp[:, k, :], d["w_mod"][l, k * 128:(k + 1) * 128, cg * 1024:(cg + 1) * 1024],
                          [], [bwp])
                for j8 in range(8):
                    j = cg * 8 + j8
                    for k in range(8):
                        K.mm(ps[:, 2 * j:2 * j + 2], wp[:, k, j8 * 128:(j8 + 1) * 128], sc[:, k, :],
                             k == 0, k == 7, [bwp, bsc], [bps])
            bm, bbm = K.cols["b_mod%d" % l]
            for s in range(2):
                K.tt("dve", modT[:, :, s], ps[:, s:96:2], bm, ALU.add, [bps, bbm], [bmod])
    for l in range(2):
        modT, bmod = K.modT[l]
        for s in range(2):
            for which, (vsc, vsh, nname) in enumerate(((1, 0, "norm_mix%d" % l), (4, 3, "norm_mlp%d" % l))):
                a, ba = A_pre[(l, s, which)]
                nm, bnm = K.cols[nname]
                K.stt(a[:], modT[:, vsc * 8:vsc * 8 + 8, s], 1.0, nm, ALU.add, ALU.mult, [bmod, bnm], [ba])
                K.A[(l, s, which)] = (a[:], modT[:, vsh * 8:vsh * 8 + 8, s], ba, bmod)
    with K.phase():
        dg = Rot(K, "diag", [128, 128], F32, 2)
        for l in range(2):
            modT, bmod = K.modT[l]
            for s in range(2):
                if l == 1 and s == 1:
                    continue
                for which, v in enumerate((2, 5)):
                    g, bg = G_pre[(l, s, which)]
                    for half in range(2):
                        ps, bps = K.PS.get()
                        for kk in range(4):
                            k = half * 4 + kk
                            dd, bdd = dg.get()
                            K.ts("dve", dd[:], K.ident[:], modT[:, v * 8 + k, s:s + 1], None, ALU.mult, None,
                                 [K.b_ident, bmod], [bdd])
                            K.mm(ps[:, kk * 128:(kk + 1) * 128], K.ones[:], dd[:], True, True,
                                 [K.b_ones, bdd], [bps])
                        K.cp("act", g[:, half * 512:(half + 1) * 512], ps[:], [bps], [bg])
                    K.G[(l, s, which)] = (g, bg)


def _gsb(K, name, shape, dt=F32):
    t = K.gstack.enter_context(K.nc.sbuf_tensor("%s_%d" % (name, K.uid()), list(shape), dt))
    return t, Buf(name)


Kern.gsb = _gsb


def rstd_op(K, out, tmp, in_, scale, b):
    K.act(tmp, in_, AF.Ln, [b], [b], bias=EPS, scale=scale)
    K.act(out, tmp, AF.Exp, [b], [b], scale=-0.5)


class NormCtx:
    def __init__(self, K, nx=3, nxn=2):
        self.xin = Rot(K, "xin", [128, D], F32, nx)
        self.junk = Rot(K, "junk", [128, D], BF16, 1)
        self.xn = Rot(K, "xn", [128, D], F32, nxn)
        self.st = Rot(K, "nst", [128, 4], F32, 4)


def norm_pre(K, N, src_ap, bsrc):
    xt, bx = N.xin.get()
    K.dma("sp", xt[:], src_ap, [bsrc], [bx])
    jk, bj = N.junk.get()
    st, bst = N.st.get()
    K.act(jk[:], xt[:], AF.Square, [bx], [bj, bst], accum_out=st[:, 0:1])
    rstd_op(K, st[:, 2:3], st[:, 1:2], st[:, 0:1], 1.0 / D, bst)
    xn, bxn = N.xn.get()
    K.ts("dve", xn[:], xt[:], st[:, 2:3], None, ALU.mult, None, [bx, bst], [bxn])
    return xt, bx, xn, bxn


def norm_tr(K, pre, AB, hT, bhT, col0, psrot=None):
    a, b, ba, bb = AB
    psrot = psrot or K.PS
    xt, bx, xn, bxn = pre
    for half in range(2):
        ps, bps = psrot.get()
        for kk in range(4):
            k = half * 4 + kk
            K.tr(ps[:, kk * 128:(kk + 1) * 128], xn[:, k * 128:(k + 1) * 128], K.ident[:], [bxn, K.b_ident], [bps])
        for kk in range(4):
            k = half * 4 + kk
            if kk % 2 == 0:
                K.act(hT[:, k, col0:col0 + 128], ps[:, kk * 128:(kk + 1) * 128], AF.Identity, [bps, ba, bb], [bhT],
                      bias=b[:, k:k + 1], scale=a[:, k:k + 1])
            else:
                K.ts("dve", hT[:, k, col0:col0 + 128], ps[:, kk * 128:(kk + 1) * 128], a[:, k:k + 1], b[:, k:k + 1],
                     ALU.mult, ALU.add, [bps, ba, bb], [bhT])
    return xt, bx


def norm_tile(K, N, src_ap, bsrc, AB, hT, bhT, col0, psrot=None):
    return norm_tr(K, norm_pre(K, N, src_ap, bsrc), AB, hT, bhT, col0, psrot)


def load_w_bf16(K, w, dram_ap, nk, c0, c1, bw):
    K.dma("pool", w[:, :, 0:c1 - c0], dram_ap[:, :, c0:c1], [], [bw], group=K.cx.new_group())


def load_w_cols(K, w, dram_ap, nk, c0, blocks, split_k=False):
    bufs = []
    for (a, b) in blocks:
        bf = Buf("wblk")
        K.dma("pool", w[:, :, a - c0:b - c0], dram_ap[:, :, a:b], [], [bf], group=K.cx.new_group())
        bufs.append(bf)
    return bufs


def phase_mlp(K, l, src, dst, tiles, final=False):
    d = K.dram
    LAG = 3
    with K.phase():
        w1, bw1 = K.sb("w1", [128, 8, 4 * D], BF16)
        w2, bw2 = K.sb("w2", [128, 32, D], BF16)
        bw1s, bw2s = [], []
        src2 = d["w_mlp_out"][l]
        for b_ in range(8):
            bw1s += load_w_cols(K, w1, d["w_mlp_in"][l], 8, 0, [(b_ * 512, (b_ + 1) * 512)])
            bf = Buf("w2blk")
            K.dma("pool", w2[:, b_ * 4:(b_ + 1) * 4, :], src2[:, b_ * 4:(b_ + 1) * 4, :], [], [bf], group=K.cx.new_group())
            if final:
                g2t, bg2t = K.G[(l, 0, 1)]
                K.tt("dve", w2[:, b_ * 4:(b_ + 1) * 4, :], w2[:, b_ * 4:(b_ + 1) * 4, :],
                     g2t[:].unsqueeze(1).to_broadcast([128, 4, D]), ALU.mult, [bf, bg2t], [bf])
            bw2s.append(bf)
        N = NormCtx(K, nx=4, nxn=2)
        hrot = Rot(K, "hT", [128, 8, 256], BF16, 2)
        urot = Rot(K, "uT", [128, 256], BF16, 5)
        rrot = Rot(K, "relu", [128, 256], F32, 3)
        trot = Rot(K, "tmp", [128, 512], F32, 4)
        orot = Rot(K, "xo", [128, D], F32, 2)
        strot = Rot(K, "fst", [128, 4], F32, 2)
        if final:
            nf, bnf = K.sb("nf", [128, D])
            K.dma("sp", nf[:], d["norm_final"].partition_broadcast(128), [], [bnf])
            jrot = Rot(K, "fjunk", [128, D], BF16, 1)
        blocks = []
        ctxs = [t for t in tiles if t < NTC]
        lats = [t for t in tiles if t >= NTC]
        if ctxs:
            blocks.append(ctxs)
        for i in range(0, len(lats), 2):
            blocks.append(lats[i:i + 2])

        def front_pre(blk):
            return [norm_pre(K, N, *xsrc(K, src, tt)) for tt in blk]

        def front_tr(blk, pres):
            s = 1 if blk[0] < NTC else 0
            hT, bhT = hrot.get()
            xs = [norm_tr(K, pres[i], K.A[(l, s, 1)], hT, bhT, i * 128, psrot=K.PSb) for i in range(len(blk))]
            return hT, bhT, xs

        cur = front_tr(blocks[0], front_pre(blocks[0]))
        nxt_pre = None
        for bi, blk in enumerate(blocks):
            s = 1 if blk[0] < NTC else 0
            nt = len(blk) * 128
            hT, bhT, xs = cur
            acc = [[K.PSa.get() for n in range(2)] for i in range(len(blk))]
            us = {}
            for f in range(32 + LAG):
                if f < 32:
                    ps, bps = K.PSb.get()
                    for k in range(8):
                        K.mm(ps[:, 0:nt], w1[:, k, f * 128:(f + 1) * 128], hT[:, k, 0:nt], k == 0, k == 7,
                             [bw1s[f // 4], bhT], [bps])
                    r, br = rrot.get()
                    K.ts("dve", r[:, 0:nt], ps[:, 0:nt], 0.0, None, ALU.max, None, [bps], [br])
                    u, bu = urot.get()
                    K.act(u[:, 0:nt], r[:, 0:nt], AF.Square, [br], [bu])
                    us[f] = (u, bu)
                if f == 2 and bi + 1 < len(blocks):
                    nxt_pre = front_pre(blocks[bi + 1])
                if f == 22 and bi + 1 < len(blocks):
                    cur = front_tr(blocks[bi + 1], nxt_pre)
                if f >= LAG:
                    f2 = f - LAG
                    u, bu = us.pop(f2)
                    for i in range(len(blk)):
                        for n in range(2):
                            pa, bpa = acc[i][n]
                            K.mm(pa[:], u[:, i * 128:(i + 1) * 128], w2[:, f2, n * 512:(n + 1) * 512], f2 == 0, f2 == 31,
                                 [bu, bw2s[f2 // 4]], [bpa])
            g, bg = K.G[(l, s, 1)]
            tms = {}
            for i, tt in enumerate(blk):
                for n in range(2):
                    pa, bpa = acc[i][n]
                    if final:
                        tms[(i, n)] = (pa, bpa)
                        continue
                    tm, btm = trot.get()
                    K.tt("dve", tm[:], pa[:], g[:, n * 512:(n + 1) * 512], ALU.mult, [bpa, bg], [btm])
                    tms[(i, n)] = (tm, btm)
            for i, tt in enumerate(blk):
                xt, bx = xs[i]
                xo, bxo = orot.get()
                for n in range(2):
                    tm, btm = tms[(i, n)]
                    K.tt("dve", xo[:, n * 512:(n + 1) * 512], tm[:], xt[:, n * 512:(n + 1) * 512], ALU.add,
                         [btm, bx], [bxo])
                dap, dbf = xsrc(K, dst, tt)
                if final:
                    jk, bj = jrot.get()
                    st, bst = strot.get()
                    K.act(jk[:], xo[:], AF.Square, [bxo], [bj, bst], accum_out=st[:, 0:1])
                    rstd_op(K, st[:, 2:3], st[:, 1:2], st[:, 0:1], 1.0 / D, bst)
                    K.stt(xo[:], xo[:], st[:, 2:3], nf[:], ALU.mult, ALU.mult, [bxo, bst, bnf], [bxo])
                K.dma("pool", dap, xo[:], [bxo], [dbf])


def rope(K, ps, bps, rp, brp, ci, si, o, bo, t1rot, t2rot):
    t1, b1 = t1rot.get()
    t2, b2 = t2rot.get()
    K.tt("dve", t1[:].rearrange("p (h d) -> p h d", h=4), ps[:].rearrange("p (h d) -> p h d", h=4),
         rp[:, ci, :].unsqueeze(1).to_broadcast([128, 4, 128]), ALU.mult, [bps, brp], [b1])
    q5 = ps[:].rearrange("p (h a two i) -> p h a two i", h=4, a=2, two=2)
    t5 = t2[:].rearrange("p (h a two i) -> p h a two i", h=4, a=2, two=2)
    s4 = rp[:, si, :].rearrange("p (a two i) -> p a two i", a=2, two=2)
    for two in range(2):
        K.tt("dve", t5[:, :, :, two, :], q5[:, :, :, 1 - two, :],
             s4[:, :, two, :].unsqueeze(1).to_broadcast([128, 4, 2, 32]), ALU.mult, [bps, brp], [b2])
    K.tt("dve", o[:], t1[:], t2[:], ALU.add, [b1, b2], [bo])


def phase_l0_ret_proj(K):
    d = K.dram
    with K.phase():
        w, bw = K.sb("wr", [128, 8, 2048], BF16)
        bws = load_w_cols(K, w, d["w_in_even"], 8, 0, [(n_ * 512, (n_ + 1) * 512) for n_ in range(4)])
        N = NormCtx(K, nx=2, nxn=2)
        hrot = Rot(K, "hT", [128, 8, 128], BF16, 3)
        rrot = Rot(K, "rope", [128, 4, 128], F32, 2)
        t1rot = Rot(K, "t1", [128, 512], F32, 2)
        t2rot = Rot(K, "t2", [128, 512], F32, 2)
        orot = Rot(K, "o", [128, 512], BF16, 6)
        def front_tr(tt, pre):
            s_ = 1 if tt < NTC else 0
            hT_, bhT_ = hrot.get()
            norm_tr(K, pre, K.A[(0, s_, 0)], hT_, bhT_, 0)
            return hT_, bhT_

        cur = front_tr(0, norm_pre(K, N, *xsrc(K, "X0", 0)))
        for tt in range(NT):
            s = 1 if tt < NTC else 0
            hT, bhT = cur
            if tt + 1 < NT:
                npre = norm_pre(K, N, *xsrc(K, "X0", tt + 1))
            if s == 0:
                rp, brp = rrot.get()
                K.dma("sp", rp[:], d["c_rope"][tt - NTC], [], [brp])
            for n, nm in enumerate(("RQ", "RK", "RV", "RG")):
                if n == 3 and tt + 1 < NT:
                    cur = front_tr(tt + 1, npre)
                ps, bps = K.PS.get()
                for k in range(8):
                    K.mm(ps[:], hT[:, k, :], w[:, k, n * 512:(n + 1) * 512], k == 0, k == 7, [bhT, bws[n]], [bps])
                o, bo = orot.get()
                if n < 2 and s == 0:
                    rope(K, ps, bps, rp, brp, 2 * n, 2 * n + 1, o, bo, t1rot, t2rot)
                elif n == 1:
                    K.cx.op("act", [bps], [bo], lambda e: e.mul(o[:], ps[:], float(128 ** -0.5)))
                elif n == 3:
                    K.act(o[:], ps[:], AF.Silu, [bps], [bo])
                else:
                    K.cp("act", o[:], ps[:], [bps], [bo])
                K.dma("pool", d[nm][tt * 128:(tt + 1) * 128, :], o[:], [bo], [K.db(nm, tt)])


def na_slots(tt):
    ctxs = [(0, None), (1, None)]
    if tt < NTC:
        return ctxs
    t = tt - NTC
    if t == 0:
        loc = [(0, 4), (1, 5), (2, 7), (3, 8)]
    elif t == 1:
        loc = [(0, 3), (1, 4), (2, 5), (3, 7)]
    elif t == NTL - 2:
        loc = [(NTL - 4, 1), (NTL - 3, 3), (NTL - 2, 4), (NTL - 1, 5)]
    elif t == NTL - 1:
        loc = [(NTL - 4, 0), (NTL - 3, 1), (NTL - 2, 3), (NTL - 1, 4)]
    else:
        loc = [(t - 2, 2), (t - 1, 3), (t, 4), (t + 1, 5), (t + 2, 6)]
    return [(kt + NTC, v) for kt, v in loc] + ctxs


def phase_l0_na(K):
    d = K.dram
    with K.phase():
        NQT, _ = K.sb("NQT", [128, NT, 512], BF16)
        NKT, _ = K.sb("NKT", [128, NT, 512], BF16)
        NV, _ = K.sb("NV", [128, NT, 8, 65], BF16)
        bNQ = [Buf("nq%d" % i) for i in range(NT)]
        bNK = [Buf("nk%d" % i) for i in range(NT)]
        bNV = [Buf("nv%d" % i) for i in range(NT)]
        with K.phase():
            w, bw = K.sb("wn", [128, 8, 1536], BF16)
            bws = load_w_cols(K, w, d["w_in_even"], 8, 2048, [(2048 + n_ * 512, 2048 + (n_ + 1) * 512) for n_ in range(3)])
            N = NormCtx(K, nx=2, nxn=2)
            hrot = Rot(K, "hT", [128, 8, 128], BF16, 3)
            qrot = Rot(K, "qb", [128, 512], BF16, 3)
            def front_tr(tt, pre):
                s_ = 1 if tt < NTC else 0
                hT_, bhT_ = hrot.get()
                norm_tr(K, pre, K.A[(0, s_, 0)], hT_, bhT_, 0)
                return hT_, bhT_

            cur = front_tr(0, norm_pre(K, N, *xsrc(K, "X0", 0)))
            for tt in range(NT):
                s = 1 if tt < NTC else 0
                hT, bhT = cur
                if tt + 1 < NT:
                    npre = norm_pre(K, N, *xsrc(K, "X0", tt + 1))
                K.memset("pool", NV[:, tt, :, 64:65], 1.0, [bNV[tt]])
                for n in range(3):
                    if n == 2 and tt + 1 < NT:
                        cur = front_tr(tt + 1, npre)
                    ps, bps = K.PS.get()
                    for k in range(8):
                        K.mm(ps[:], hT[:, k, :], w[:, k, n * 512:(n + 1) * 512], k == 0, k == 7, [bhT, bws[n]], [bps])
                    if n == 2:
                        K.cp("act", NV[:, tt, :, 0:64], ps[:].rearrange("p (h e) -> p h e", h=8), [bps], [bNV[tt]])
                        continue
                    qb, bqb = qrot.get()
                    K.cp("act", qb[:], ps[:], [bps], [bqb])
                    pt, bpt = K.PS.get()
                    ptb = pt[:, 0:256].bitcast(BF16)
                    for g in range(4):
                        K.tr(ptb[:, g * 128:(g + 1) * 128], qb[:, g * 128:(g + 1) * 128], K.identb[:],
                             [bqb, K.b_identb], [bpt])
                    dst, bd = (NQT, bNQ[tt]) if n == 0 else (NKT, bNK[tt])
                    K.cp("dve", dst[:, tt, :], ptb, [bpt], [bd])
        with K.phase():
            bias, bbias = K.sb("nabias", [128, 8 * 9 * 128], BF16)
            K.dma("pool", bias[:], d["na_bias"].rearrange("p h v q -> p (h v q)"), [], [bbias])
            sbrot = Rot(K, "sbias", [128, 512], F32, 3)
            ptrot = Rot(K, "PT", [128, 7 * 128], BF16, 3)
            orot = Rot(K, "nao", [128, 512], BF16, 2)
            rcrot = Rot(K, "rec", [128, 8], F32, 2)
            for tt in range(NT):
                slots = na_slots(tt)
                nsl = len(slots)
                psO = [K.PSb.get(), K.PSb.get()]
                pend = None

                def issue_pv(p):
                    h_, PT_, bPT_ = p
                    po, bpo = psO[h_ // 4]
                    hc = (h_ % 4) * 65
                    for si_, (kt_, var_) in enumerate(slots):
                        K.mm(po[:, hc:hc + 65], PT_[:, si_ * 128:(si_ + 1) * 128], NV[:, kt_, h_, :], si_ == 0,
                             si_ == nsl - 1, [bPT_, bNV[kt_]], [bpo])

                for h in range(8):
                    g, hh = divmod(h, 2)
                    p0 = hh * 64
                    banks = [K.PSa.get()]
                    if nsl > 4:
                        banks.append(K.PSa.get())
                    for si, (kt, var) in enumerate(slots):
                        bk, bbk = banks[si // 4]
                        pos = si % 4
                        K.mm(bk[:, pos * 128:(pos + 1) * 128], NKT[p0:p0 + 64, kt, g * 128:(g + 1) * 128],
                             NQT[p0:p0 + 64, tt, g * 128:(g + 1) * 128], True, True, [bNK[kt], bNQ[tt]], [bbk])
                    PT, bPT = ptrot.get()
                    si = 0
                    while si < nsl:
                        kt, var = slots[si]
                        sj = si + 1
                        while sj < nsl and sj // 4 == si // 4 and (
                                (var is None and slots[sj][1] is None) or
                                (var is not None and slots[sj][1] is not None and slots[sj][1] == slots[sj - 1][1] + 1)):
                            sj += 1
                        bk, bbk = banks[si // 4]
                        c0 = (si % 4) * 128
                        w_ = (sj - si) * 128
                        if var is None:
                            K.act(PT[:, si * 128:sj * 128], bk[:, c0:c0 + w_], AF.Exp, [bbk], [bPT], scale=0.125)
                        else:
                            sbt, bsb = sbrot.get()
                            b0 = (h * 9 + var) * 128
                            K.stt(sbt[:, 0:w_], bk[:, c0:c0 + w_], 0.125, bias[:, b0:b0 + w_], ALU.mult, ALU.add,
                                  [bbk, bbias], [bsb])
                            K.act(PT[:, si * 128:sj * 128], sbt[:, 0:w_], AF.Exp, [bsb], [bPT])
                        si = sj
                    if pend is not None:
                        issue_pv(pend)
                    pend = (h, PT, bPT)
                issue_pv(pend)
                o, bo = orot.get()
                rc, brc = rcrot.get()
                for hb in range(2):
                    po, bpo = psO[hb]
                    pv = po[:, 0:260].rearrange("p (h e) -> p h e", h=4)
                    K.cx.op("dve", [bpo], [brc], lambda e: e.reciprocal(rc[:, hb * 4:(hb + 1) * 4], pv[:, :, 64]))
                    K.tt("dve", o[:, hb * 256:(hb + 1) * 256].rearrange("p (h e) -> p h e", h=4), pv[:, :, 0:64],
                         rc[:, hb * 4:(hb + 1) * 4].unsqueeze(2).to_broadcast([128, 4, 64]), ALU.mult, [bpo, brc], [bo])
                K.dma("pool", d["NAO"][tt * 128:(tt + 1) * 128, :], o[:], [bo], [K.db("NAO", tt)])


def phase_l0_ret(K):
    d = K.dram
    with K.phase():
        wo, bwo = K.sb("wo", [128, 8, 1024], BF16)
        load_w_bf16(K, wo, d["w_out_even"], 8, 0, 1024, bwo)
        tri, btri = K.sb("tri", [128, 8, 128])
        K.dma("sp", tri[:], d["c_tri"][:, :, :], [], [btri])
        col, bcol = K.sb("col", [128, 2])
        K.dma("sp", col[:], d["c_col"][:, :], [], [bcol])
        lg, blg = K.sb("lg", [128, 8])
        K.dma("sp", lg[:], d["ret_decay_logit"].partition_broadcast(128), [], [blg])
        K.act(lg[:], lg[:], AF.Exp, [blg], [blg], scale=-1.0)
        K.act(lg[:], lg[:], AF.Ln, [blg], [blg], bias=1.0)
        K.ts("dve", lg[:], lg[:], -1.0, None, ALU.mult, None, [blg], [blg])
        MT, bMT = K.sb("MT", [128, 512])
        QDf, bQDf = K.sb("QDf", [128, 512], BF16)
        QDb, bQDb = K.sb("QDb", [128, 512], BF16)
        kdec, bkd = K.sb("kdec", [128, 8])
        cdec, bcd = K.sb("cdec", [128, 8])
        e1, be1 = K.sb("e1", [128, 128])
        e2, be2 = K.sb("e2", [128, 128])
        for h in range(4):
            hs = slice(h * 128, (h + 1) * 128)
            K.act(e1[:], tri[:, 0, :], AF.Exp, [btri, blg], [be1], scale=lg[:, h:h + 1])
            K.tt("dve", e1[:], e1[:], tri[:, 1, :], ALU.mult, [be1, btri], [be1])
            K.act(e2[:], tri[:, 2, :], AF.Exp, [btri, blg], [be2], scale=lg[:, 4 + h:5 + h])
            K.tt("dve", e2[:], e2[:], tri[:, 3, :], ALU.mult, [be2, btri], [be2])
            K.tt("dve", MT[:, hs], e1[:], e2[:], ALU.add, [be1, be2], [bMT])
            K.act(QDf[:, hs], tri[:, 4, :], AF.Exp, [btri, blg], [bQDf], scale=lg[:, h:h + 1])
            K.act(QDb[:, hs], tri[:, 5, :], AF.Exp, [btri, blg], [bQDb], scale=lg[:, 4 + h:5 + h])
            K.act(kdec[:, h:h + 1], col[:, 0:1], AF.Exp, [bcol, blg], [bkd], scale=lg[:, h:h + 1])
            K.act(kdec[:, 4 + h:5 + h], col[:, 1:2], AF.Exp, [bcol, blg], [bkd], scale=lg[:, 4 + h:5 + h])
        K.act(cdec[:], lg[:], AF.Exp, [blg], [bcd], scale=128.0)
        SBL, _ = K.sb("SBL", [128, NT, 512], BF16)
        bSB = [Buf("sb%d" % i) for i in range(NT)]
        S, bS = K.sb("S", [128, 512])
        lrot = Rot(K, "ld", [128, 512], BF16, 10)
        kdrot = Rot(K, "kd", [128, 512], BF16, 2)

        def bc4(ap4):
            return ap4.unsqueeze(2).to_broadcast([128, 4, 128])

        def v3(ap):
            return ap.rearrange("p (h d) -> p h d", h=4)

        def state_update(rk, brk, rv, brv, c0):
            kd, bkdd = kdrot.get()
            K.tt("dve", v3(kd[:]), v3(rk[:]), bc4(kdec[:, c0:c0 + 4]), ALU.mult, [brk, bkd], [bkdd])
            ps, bps = K.PS.get()
            for h in range(4):
                hs = slice(h * 128, (h + 1) * 128)
                K.mm(ps[:, hs], kd[:, hs], rv[:, hs], True, True, [bkdd, brv], [bps])
            K.tt("dve", v3(S[:]), v3(S[:]), bc4(cdec[:, c0:c0 + 4]), ALU.mult, [bS, bcd], [bS])
            K.tt("dve", S[:], S[:], ps[:], ALU.add, [bS, bps], [bS])

        def load(nm, tt):
            t, b = lrot.get()
            K.dma("sp", t[:], d[nm][tt * 128:(tt + 1) * 128, :], [K.db(nm, tt)], [b])
            return t, b

        K.memset("dve", S[:], 0.0, [bS])
        for tt in [1, 0] + list(range(NT - 1, NTC - 1, -1)):
            K.cp("act", SBL[:, tt, :], S[:], [bS], [bSB[tt]])
            rk, brk = load("RK", tt)
            rv, brv = load("RV", tt)
            state_update(rk, brk, rv, brv, 4)
        K.memset("dve", S[:], 0.0, [bS])
        sfrot = Rot(K, "sfb", [128, 512], BF16, 2)
        qkrot = Rot(K, "QK", [128, 1024], BF16, 2)
        ptrot = Rot(K, "PTm", [128, 512], BF16, 2)
        qfrot = Rot(K, "Qf", [128, 512], BF16, 4)
        strot = Rot(K, "st6", [128, 4, 6], F32, 2)
        mvrot = Rot(K, "mv", [128, 4, 2], F32, 2)
        rsrot = Rot(K, "rs", [128, 8], F32, 2)
        ynrot = Rot(K, "yn", [128, 512], F32, 2)
        mixrot = Rot(K, "mix", [128, 1024], BF16, 3)
        mtrot = Rot(K, "mixT", [128, 1024], BF16, 2)
        xrot = Rot(K, "xt", [128, D], F32, 3)
        xorot = Rot(K, "xo", [128, D], F32, 2)
        tmrot = Rot(K, "tm", [128, 512], F32, 2)

        def A1(tt):
            c = {"tt": tt}
            c["rq"] = load("RQ", tt)
            c["rk"] = load("RK", tt)
            c["rv"] = load("RV", tt)
            c["rg"] = load("RG", tt)
            mix, bmix = mixrot.get()
            K.dma("sp", mix[:, 512:1024], d["NAO"][tt * 128:(tt + 1) * 128, :], [K.db("NAO", tt)], [bmix])
            c["mix"] = (mix, bmix)
            xt, bx = xrot.get()
            sap, sbf = xsrc(K, "X0", tt)
            K.dma("sp", xt[:], sap, [sbf], [bx])
            c["xt"] = (xt, bx)
            rq, brq = c["rq"]
            rk, brk = c["rk"]
            pt, bpt = K.PS.get()
            ptb = pt[:].bitcast(BF16)
            for h in range(4):
                hs = slice(h * 128, (h + 1) * 128)
                K.tr(ptb[:, hs], rq[:, hs], K.identb[:], [brq, K.b_identb], [bpt])
                K.tr(ptb[:, 512 + h * 128:512 + (h + 1) * 128], rk[:, hs], K.identb[:], [brk, K.b_identb], [bpt])
            QK, bQK = qkrot.get()
            K.cp("act", QK[:], ptb, [bpt], [bQK])
            c["QK"] = (QK, bQK)
            return c

        def A2(c):
            QK, bQK = c["QK"]
            pa, bpa = K.PS.get()
            for h in range(4):
                hs = slice(h * 128, (h + 1) * 128)
                K.mm(pa[:, hs], QK[:, 512 + h * 128:512 + (h + 1) * 128], QK[:, hs], True, True, [bQK], [bpa])
            PTm, bPTm = ptrot.get()
            K.tt("dve", PTm[:], pa[:], MT[:], ALU.mult, [bpa, bMT], [bPTm])
            Qf, bQf = qfrot.get()
            K.tt("dve", Qf[:], QK[:, 0:512], QDf[:], ALU.mult, [bQK, bQDf], [bQf])
            Qb, bQb = qfrot.get()
            K.tt("dve", Qb[:], QK[:, 0:512], QDb[:], ALU.mult, [bQK, bQDb], [bQb])
            c["PTm"] = (PTm, bPTm)
            c["Qf"] = (Qf, bQf)
            c["Qb"] = (Qb, bQb)

        def B1(c, sfb, bsfb):
            tt = c["tt"]
            PTm, bPTm = c["PTm"]
            Qf, bQf = c["Qf"]
            Qb, bQb = c["Qb"]
            rv, brv = c["rv"]
            rg, brg = c["rg"]
            mix, bmix = c["mix"]
            py, bpy = K.PS.get()
            for h in range(4):
                hs = slice(h * 128, (h + 1) * 128)
                K.mm(py[:, hs], PTm[:, hs], rv[:, hs], True, False, [bPTm, brv], [bpy])
                K.mm(py[:, hs], Qf[:, hs], sfb[:, hs], False, False, [bQf, bsfb], [bpy])
                K.mm(py[:, hs], Qb[:, hs], SBL[:, tt, hs], False, True, [bQb, bSB[tt]], [bpy])
            st6, bst6 = strot.get()
            mv, bmv = mvrot.get()
            rs, brs = rsrot.get()
            for h in range(4):
                hs = slice(h * 128, (h + 1) * 128)
                K.cx.op("dve", [bpy], [bst6], lambda e: e.bn_stats(st6[:, h, :], py[:, hs]))
            for h in range(4):
                K.cx.op("dve", [bst6], [bmv], lambda e: e.bn_aggr(mv[:, h, :], st6[:, h, :]))
            K.act(rs[:, 4:8], mv[:, :, 1], AF.Ln, [bmv], [brs], bias=EPS, scale=1.0)
            K.act(rs[:, 0:4], rs[:, 4:8], AF.Exp, [brs], [brs], scale=-0.5)
            yn, byn = ynrot.get()
            for h in range(4):
                hs = slice(h * 128, (h + 1) * 128)
                K.ts("dve", yn[:, hs], py[:, hs], mv[:, h, 0:1], rs[:, h:h + 1], ALU.subtract, ALU.mult,
                     [bpy, bmv, brs], [byn])
            K.tt("dve", mix[:, 0:512], yn[:], rg[:], ALU.mult, [byn, brg], [bmix])

        def B2(c):
            tt = c["tt"]
            s_ = 1 if tt < NTC else 0
            mix, bmix = c["mix"]
            xt, bx = c["xt"]
            pm, bpm = K.PS.get()
            pmb = pm[:].bitcast(BF16)
            for k in range(8):
                K.tr(pmb[:, k * 128:(k + 1) * 128], mix[:, k * 128:(k + 1) * 128], K.identb[:],
                     [bmix, K.b_identb], [bpm])
            mixT, bmT = mtrot.get()
            K.cp("act", mixT[:], pmb, [bpm], [bmT])
            g, bg = K.G[(0, s_, 0)]
            xo, bxo = xorot.get()
            for n in range(2):
                ps, bps = K.PS.get()
                for k in range(8):
                    K.mm(ps[:], mixT[:, k * 128:(k + 1) * 128], wo[:, k, n * 512:(n + 1) * 512], k == 0, k == 7,
                         [bmT, bwo], [bps])
                tm, btm = tmrot.get()
                K.tt("dve", tm[:], ps[:], g[:, n * 512:(n + 1) * 512], ALU.mult, [bps, bg], [btm])
                K.tt("dve", xo[:, n * 512:(n + 1) * 512], tm[:], xt[:, n * 512:(n + 1) * 512], ALU.add,
                     [btm, bx], [bxo])
            dap, dbf = xsrc(K, "X1", tt)
            K.dma("pool", dap, xo[:], [bxo], [dbf])

        cur = A1(0)
        A2(cur)
        sfb, bsfb = sfrot.get()
        K.cp("act", sfb[:], S[:], [bS], [bsfb])
        prevB = None
        for i in range(NT + 1):
            sfb_cur = (sfb, bsfb)
            if cur is not None:
                state_update(cur["rk"][0], cur["rk"][1], cur["rv"][0], cur["rv"][1], 0)
                sfb, bsfb = sfrot.get()
                K.cp("act", sfb[:], S[:], [bS], [bsfb])
            nxt = A1(i + 1) if i + 1 < NT else None
            if cur is not None:
                B1(cur, sfb_cur[0], sfb_cur[1])
            if nxt is not None:
                A2(nxt)
            if prevB is not None:
                B2(prevB)
            prevB = cur
            cur = nxt


L1_SRC = ["X2"]


def ssd_blocks():
    blks = [([0, 1], 0, 0)]
    for b in range(NTL // 4):
        blks.append(([NTC + 4 * b + i for i in range(4)], 262, b * 512))
    return blks


def phase_l1_proj(K):
    d = K.dram
    with K.phase():
        w, bw = K.sb("wi", [128, 8, ODD_IN], BF16)
        bw_dt = load_w_cols(K, w, d["w_in_odd"], 8, 0, [(4608, 5184)])[0]
        bw_x = load_w_cols(K, w, d["w_in_odd"], 8, 0, [(2048 + n_ * 512, 2048 + (n_ + 1) * 512) for n_ in range(5)])
        bw_x.append(bw_dt)
        bw_z = load_w_cols(K, w, d["w_in_odd"], 8, 0, [(n_ * 512, (n_ + 1) * 512) for n_ in range(4)], split_k=True)
        N = NormCtx(K, nx=2, nxn=4)
        hrot = Rot(K, "hT", [128, 8, 512], BF16, 2)
        zrot = Rot(K, "zs", [128, 2048], BF16, 2)
        drot = Rot(K, "dtt", [128, 64], F32, 3)
        xrot = Rot(K, "xbc", [128, 512], BF16, 4)
        dtb, bdtb = K.sb("dtb", [128, 64])
        K.dma("sp", dtb[:], d["dt_bias"].partition_broadcast(128), [], [bdtb])
        zz, bzz = K.sb("zz", [128, 24, 3], BF16)
        K.memset("dve", zz[:], 0.0, [bzz])
        for c0 in (0, 259, 262, 4361):
            K.dma("pool", d["XBCT"][:, :, c0:c0 + 3].rearrange("c p t -> p c t"), zz[:], [bzz], [Buf()])
        blks = ssd_blocks()

        def f_pre(bi):
            return [norm_pre(K, N, *xsrc(K, L1_SRC[0], tt)) for tt in blks[bi][0]]

        def f_tr(bi, pres):
            s_ = 1 if blks[bi][0][0] < NTC else 0
            hT_, bhT_ = hrot.get()
            for i_, p_ in enumerate(pres):
                norm_tr(K, p_, K.A[(1, s_, 0)], hT_, bhT_, i_ * 128)
            return hT_, bhT_

        cur = f_tr(0, f_pre(0))
        for bi, (tiles, off, t0) in enumerate(blks):
            s = 1 if tiles[0] < NTC else 0
            nt = len(tiles) * 128
            hT, bhT = cur
            if bi + 1 < len(blks):
                npre = f_pre(bi + 1)
            for i, tt in enumerate(tiles):
                ts_ = slice(i * 128, (i + 1) * 128)
                if s == 0:
                    zs, bzs = zrot.get()
                    for n in range(4):
                        ps, bps = K.PS.get()
                        for k in range(8):
                            K.mm(ps[:], hT[:, k, ts_], w[:, k, n * 512:(n + 1) * 512], k == 0, k == 7, [bhT, bw_z[n]], [bps])
                        K.act(zs[:, n * 512:(n + 1) * 512], ps[:], AF.Silu, [bps], [bzs])
                    K.dma("pool", d["ZS"][tt * 128:(tt + 1) * 128, :], zs[:], [bzs], [Buf()])
                ps, bps = K.PS.get()
                for k in range(8):
                    K.mm(ps[:, 0:64], hT[:, k, ts_], w[:, k, 5120:5184], k == 0, k == 7, [bhT, bw_dt], [bps])
                dt, bdt = drot.get()
                K.tt("dve", dt[:], ps[:, 0:64], dtb[:], ALU.add, [bps, bdtb], [bdt])
                K.act(dt[:], dt[:], AF.Exp, [bdt], [bdt])
                K.act(dt[:], dt[:], AF.Ln, [bdt], [bdt], bias=1.0)
                K.dma("pool", d["DT"][tt * 128:(tt + 1) * 128, :], dt[:], [bdt], [Buf()])
            for ch in range(24):
                if ch == 16 and bi + 1 < len(blks):
                    cur = f_tr(bi + 1, npre)
                ps, bps = K.PS.get()
                for k in range(8):
                    K.mm(ps[:, 0:nt], w[:, k, 2048 + ch * 128:2048 + (ch + 1) * 128], hT[:, k, 0:nt], k == 0, k == 7,
                         [bw_x[ch // 4], bhT], [bps])
                xb, bxb = xrot.get()
                K.cp("act" if ch % 2 == 0 else "dve", xb[:, 0:nt], ps[:, 0:nt], [bps], [bxb])
                c0 = off + 3 + t0
                K.dma("pool", d["XBCT"][ch, :, c0:c0 + nt], xb[:, 0:nt], [bxb], [Buf()])


def phase_l1_conv(K):
    d = K.dram
    with K.phase():
        dW, _ = K.sb("diagW", [128, 24 * 7, 128], BF16)
        bdWs = [Buf("dW%d" % ch) for ch in range(24)]
        built = set()

        def build_dw(ch):
            if ch in built:
                return
            built.add(ch)
            for k in range(7):
                cw, bcw = K.cols["conv_w%d" % k]
                K.ts("dve", dW[:, ch * 7 + k, :], K.ident[:], cw[:, ch:ch + 1], None, ALU.mult, None,
                     [K.b_ident, bcw], [bdWs[ch]])
        cb, bcb = K.cols["conv_b"]
        xirot = Rot(K, "xin", [128, 24, 518], BF16, 2)
        xarot = Rot(K, "xact", [128, 24, 512], BF16, 2)
        xsrot = Rot(K, "xs", [128, 2048], BF16, 2)
        bsrot = Rot(K, "bs", [128, 512], BF16, 2)
        for tiles, off, t0 in ssd_blocks():
            nt = len(tiles) * 128
            xin, bxin = xirot.get()
            K.dma("sp", xin[:, :, 0:nt + 6], d["XBCT"][:, :, off + t0:off + t0 + nt + 6].rearrange("c p t -> p c t"),
                  [], [bxin])
            xa, bxa = xarot.get()
            for ch in range(24):
                build_dw(ch)
                if ch + 1 < 24:
                    build_dw(ch + 1)
                ps, bps = K.PS.get()
                for k in range(7):
                    K.mm(ps[:, 0:nt], dW[:, ch * 7 + k, :], xin[:, ch, k:k + nt], k == 0, k == 6, [bdWs[ch], bxin], [bps])
                K.act(xa[:, ch, 0:nt], ps[:, 0:nt], AF.Silu, [bps, bcb], [bxa], bias=cb[:, ch:ch + 1])
            tok0 = tiles[0] * 128
            K.dma("pool", d["BT"][:, :, tok0:tok0 + nt].rearrange("g p t -> p g t"), xa[:, 16:20, 0:nt], [bxa], [Buf()])
            K.dma("pool", d["CTt"][:, :, tok0:tok0 + nt].rearrange("g p t -> p g t"), xa[:, 20:24, 0:nt], [bxa], [Buf()])
            for i, tt in enumerate(tiles):
                ts_ = slice(i * 128, (i + 1) * 128)
                xs, bxs = xsrot.get()
                for half in range(2):
                    pt, bpt = K.PS.get()
                    ptb = pt[:].bitcast(BF16)
                    for c in range(8):
                        K.tr(ptb[:, c * 128:(c + 1) * 128], xa[:, half * 8 + c, ts_], K.identb[:], [bxa, K.b_identb], [bpt])
                    K.cp("act" if half == 0 else "dve", xs[:, half * 1024:(half + 1) * 1024], ptb, [bpt], [bxs])
                K.dma("pool", d["XS"][tt * 128:(tt + 1) * 128, :], xs[:], [bxs], [Buf()])
                bs, bbs = bsrot.get()
                pt, bpt = K.PS.get()
                ptb = pt[:, 0:256].bitcast(BF16)
                for g in range(4):
                    K.tr(ptb[:, g * 128:(g + 1) * 128], xa[:, 16 + g, ts_], K.identb[:], [bxa, K.b_identb], [bpt])
                K.cp("dve", bs[:], ptb, [bpt], [bbs])
                K.dma("pool", d["BS"][tt * 128:(tt + 1) * 128, :], bs[:], [bbs], [Buf()])


def phase_l1_scan(K):
    d = K.dram
    with K.phase():
        wo, bwo = K.sb("wo1", [128, 16, D], BF16)
        load_w_bf16(K, wo, d["w_out_odd"], 16, 0, D, bwo)
        tri, btri = K.sb("tri", [128, 8, 128])
        K.dma("sp", tri[:], d["c_tri"][:, :, :], [], [btri])
        abc, babc = K.sb("abc", [128, 64])
        K.dma("sp", abc[:], d["a_log"].partition_broadcast(128), [], [babc])
        K.act(abc[:], abc[:], AF.Exp, [babc], [babc])
        K.ts("dve", abc[:], abc[:], -1.0, None, ALU.mult, None, [babc], [babc])
        dsk, bdsk = K.sb("dsk", [128, 32])
        K.dma("sp", dsk[:], d["d_skip"].partition_broadcast(128), [], [bdsk])
        nw, bnw = K.cols["ssm_norm"]

        def subrot(lo, hi):
            r = Rot.__new__(Rot)
            r.items = K.PS.items[lo:hi]
            r.i = 0
            return r
        pdrot = subrot(0, 2)
        pyirot = subrot(2, 5)
        genps = subrot(5, 8)
        Tb, bTb = K.sb("Tb", [128, 2, 128], BF16)
        K.cp("dve", Tb[:, 0, :], tri[:, 1, :], [btri], [bTb])
        K.cp("dve", Tb[:, 1, :], tri[:, 3, :], [btri], [bTb])
        NEGr, bNEGr = K.sb("NEGr", [128, 2, 4, 128], BF16)
        for dd in range(2):
            K.cp("dve", NEGr[:, dd, :, :], tri[:, 6 + dd, :].unsqueeze(1).to_broadcast([128, 4, 128]), [btri], [bNEGr])
        HT, bHT = K.sb("HT", [128, 2048])
        HTb, bHTb = K.sb("HTb", [128, 2048], BF16)
        xsrot = Rot(K, "xs", [128, 2048], BF16, 3)
        bsrot = Rot(K, "bs", [128, 512], BF16, 3)
        btrot = Rot(K, "bt", [128, 4, 128], BF16, 3)
        ctrot = Rot(K, "ct", [128, 4, 128], BF16, 3)
        dtrot = Rot(K, "dt", [128, 64], F32, 3)
        ybrot = Rot(K, "yb", [128, 2048], BF16, 3)
        zsrot = Rot(K, "zs", [128, 2048], BF16, 3)
        xtrot = Rot(K, "xt", [128, D], F32, 3)
        smrot = Rot(K, "sm", [128, 8, 32], F32, 3)
        dhlrot = Rot(K, "dhl", [128, 2, 32], BF16, 3)
        wbrot = Rot(K, "wendb", [128, 32], BF16, 3)
        LTrot = Rot(K, "LT", [128, 512], BF16, 3)
        Wrot = Rot(K, "Wt", [128, 512], BF16, 5)
        xqrot = Rot(K, "xq", [128, 2048], BF16, 1)
        cburot = Rot(K, "cbu", [128, 512], BF16, 3)
        xprot2 = None
        bigrot = Rot(K, "big", [128, 2048], BF16, 4)
        t1rot = Rot(K, "t1big", [128, 2048], BF16, 2)
        tmrot = Rot(K, "tm", [128, 512], F32, 2)
        tmbrot = Rot(K, "tmb", [128, 512], BF16, 3)
        ynrot = Rot(K, "yn", [128, 2048], BF16, 2)
        yTrot = Rot(K, "ynT", [128, 2048], BF16, 1)
        xorot = Rot(K, "xo", [128, D], F32, 1)
        strot = Rot(K, "fst", [128, 4], F32, 2)

        def bch(ap, n, m):
            return ap.unsqueeze(2).to_broadcast([128, n, m])

        phrot = Rot(K, "phs", [128, 2048], BF16, 3)

        def pre1(dirn, tt, defer=False):
            Tm = tri[:, 1, :] if dirn == 0 else tri[:, 3, :]
            NEGm = tri[:, 6, :] if dirn == 0 else tri[:, 7, :]
            dc = dirn * 32
            lat = tt >= NTC
            c = {"tt": tt, "lat": lat, "dirn": dirn}
            xs, bxs = xsrot.get()
            K.dma("sp", xs[:], d["XS"][tt * 128:(tt + 1) * 128, :], [], [bxs])
            bs, bbs = bsrot.get()
            K.dma("sp", bs[:], d["BS"][tt * 128:(tt + 1) * 128, :], [], [bbs])
            dt, bdt = dtrot.get()
            K.dma("sp", dt[:], d["DT"][tt * 128:(tt + 1) * 128, :], [], [bdt])
            if lat:
                bt, bbt = btrot.get()
                K.dma("sp", bt[:], d["BT"][:, :, tt * 128:(tt + 1) * 128].rearrange("g p t -> p g t"), [], [bbt])
                ct, bct = ctrot.get()
                K.dma("sp", ct[:], d["CTt"][:, :, tt * 128:(tt + 1) * 128].rearrange("g p t -> p g t"), [], [bct])
                c["ct"] = (ct, bct)
                if dirn == 0:
                    yb, byb = ybrot.get()
                    K.dma("sp", yb[:], d["YB"][tt * 128:(tt + 1) * 128, :], [], [byb])
                    zs, bzs = zsrot.get()
                    K.dma("sp", zs[:], d["ZS"][tt * 128:(tt + 1) * 128, :], [], [bzs])
                    c["fin_in"] = (yb, byb, zs, bzs)
            c["xs"] = (xs, bxs)
            sm, bsm = smrot.get()
            dta, acol, eac, ela, wend, stmp, lndt, ebias = (sm[:, i, :] for i in range(8))
            c["sm"] = (sm, bsm)
            K.tt("dve", dta, dt[:, dc:dc + 32], abc[:, dc:dc + 32], ALU.mult, [bdt, babc], [bsm])
            dhl, bdhl = dhlrot.get()
            c["dhl"] = (dhl, bdhl)
            K.cp("dve", dhl[:, 0, :], dta, [bsm], [bdhl])
            K.tt("dve", dhl[:, 1, :], dta, dhl[:, 0, :], ALU.subtract, [bsm, bdhl], [bdhl])
            K.act(lndt, dt[:, dc:dc + 32], AF.Ln, [bdt], [bsm])

            return _pre1_rest(c, dirn, tt, Tm, dc, lat, xs, bxs, bs, bbs, dt, bdt, sm, bsm, defer,
                              (bt, bbt, ct, bct) if lat else None)

        def _pre1_rest(c, dirn, tt, Tm, dc, lat, xs, bxs, bs, bbs, dt, bdt, sm, bsm, defer, btct):
            dta, acol, eac, ela, wend, stmp, lndt, ebias = (sm[:, i, :] for i in range(8))
            if lat:
                bt, bbt, ct, bct = btct
            st_ = {}

            def part_b():
                pc, bpc = genps.get()
                K.mm(pc[:, 0:32], Tm, dta, True, True, [btri, bsm], [bpc])
                K.mm(pc[:, 32:64], K.ones[:], dta, True, True, [K.b_ones, bsm], [bpc])
                K.cp("dve", acol, pc[:, 0:32], [bpc], [bsm])
                K.tt("dve", ebias, lndt, acol, ALU.subtract, [bsm], [bsm])
                K.act(eac, pc[:, 0:32], AF.Exp, [bpc], [bsm])
                K.act(ela, pc[:, 32:64], AF.Exp, [bpc], [bsm])
                K.tt("dve", stmp, pc[:, 32:64], acol, ALU.subtract, [bpc, bsm], [bsm])
                K.act(stmp, stmp, AF.Exp, [bsm], [bsm])
                wendb, bwendb = wbrot.get()
                K.tt("dve", wendb[:], stmp, dt[:, dc:dc + 32], ALU.mult, [bsm, bdt], [bwendb])
                xq, bxq = xqrot.get()
                K.tt("dve", xq[:].rearrange("p (h e) -> p h e", h=32), xs[:].rearrange("p (h e) -> p h e", h=32),
                     bch(wendb[:], 32, 64), ALU.mult, [bxs, bwendb], [bxq])
                st_["xq"] = (xq, bxq)

            def part_c():
                xq, bxq = st_["xq"]
                phs, bphs = phrot.get()
                c["phs"] = (phs, bphs)
                for g in range(4):
                    ph, bph = genps.get()
                    K.mm(ph[:], bs[:, g * 128:(g + 1) * 128], xq[:, g * 512:(g + 1) * 512], True, True, [bbs, bxq], [bph])
                    K.cp("act", phs[:, g * 512:(g + 1) * 512], ph[:], [bph], [bphs])
                if lat:
                    pcb, bpcb = genps.get()
                    for g in range(4):
                        K.mm(pcb[:, g * 128:(g + 1) * 128], bt[:, g, :], ct[:, g, :], True, True, [bbt, bct], [bpcb])
                    cbu, bcbu = cburot.get()
                    K.tt("dve", cbu[:].rearrange("p (g i) -> p g i", g=4), pcb[:].rearrange("p (g i) -> p g i", g=4),
                         Tm.unsqueeze(1).to_broadcast([128, 4, 128]), ALU.mult, [bpcb, btri], [bcbu])
                    c["cbu"] = (cbu, bcbu)
                    c["yd"] = bigrot.get()

            if defer:
                c["later"] = [part_b, part_c]
            else:
                part_b()
                part_c()
            return c

        def pre2(c, hooks=None):
            hooks = hooks or {}
            if not c["lat"]:
                for k_ in sorted(hooks):
                    hooks[k_]()
                return
            dirn = c["dirn"]
            sm, bsm = c["sm"]
            ebias = sm[:, 7, :]
            dhl, bdhl = c["dhl"]
            xs, bxs = c["xs"]
            cbu, bcbu = c["cbu"]
            yd, byd = c["yd"]
            pend = []

            def issue_yi(p):
                g_, hb_, Wt_, bWt_, pyi_, bpyi_ = p
                for h4 in range(4):
                    h8 = hb_ * 4 + h4
                    hh = g_ * 8 + h8
                    K.mm(pyi_[:, h8 * 64:(h8 + 1) * 64], Wt_[:, h4 * 128:(h4 + 1) * 128],
                         xs[:, hh * 64:(hh + 1) * 64], True, True, [bWt_, bxs], [bpyi_])
                if hb_ == 1:
                    K.cp("act", yd[:, g_ * 512:(g_ + 1) * 512], pyi_[:], [bpyi_], [byd])

            for g in range(4):
                pyi, bpyi = pyirot.get()
                for hb in range(2):
                    if (g * 2 + hb) in hooks:
                        hooks[g * 2 + hb]()
                    pd, bpd = pdrot.get()
                    K.mm(pd[:], K.identb[:], NEGr[:, dirn, :, :].rearrange("p h i -> p (h i)"), True, False,
                         [K.b_identb, bNEGr], [bpd])
                    for h4 in range(4):
                        hh = g * 8 + hb * 4 + h4
                        K.mm(pd[:, h4 * 128:(h4 + 1) * 128], dhl[:, 0, hh:hh + 1].to_broadcast([128, 128]), Tb[:, dirn, :],
                             False, False, [bdhl, bTb], [bpd])
                        K.mm(pd[:, h4 * 128:(h4 + 1) * 128], dhl[:, 1, hh:hh + 1].to_broadcast([128, 128]), Tb[:, dirn, :],
                             False, h4 == 3, [bdhl, bTb], [bpd])
                    LTt, bLT = LTrot.get()
                    for h4 in range(4):
                        hh = g * 8 + hb * 4 + h4
                        K.act(LTt[:, h4 * 128:(h4 + 1) * 128], pd[:, h4 * 128:(h4 + 1) * 128], AF.Exp, [bpd, bsm], [bLT],
                              bias=ebias[:, hh:hh + 1])
                    Wt, bWt = Wrot.get()
                    K.tt("dve", Wt[:].rearrange("p (h i) -> p h i", h=4), LTt[:].rearrange("p (h i) -> p h i", h=4),
                         cbu[:, g * 128:(g + 1) * 128].unsqueeze(1).to_broadcast([128, 4, 128]), ALU.mult,
                         [bLT, bcbu], [bWt])
                    pend.append((g, hb, Wt, bWt, pyi, bpyi))
                    if len(pend) > 2:
                        issue_yi(pend.pop(0))
            while pend:
                issue_yi(pend.pop(0))

        def post_hooks(c):
            dirn = c["dirn"]
            tt = c["tt"]
            sm, bsm = c["sm"]
            eac = sm[:, 2, :]
            ela = sm[:, 3, :]
            phs, bphs = c["phs"]
            H = {}

            def h_mult():
                K.tt("dve", HT[:].rearrange("p (h e) -> p h e", h=32), HT[:].rearrange("p (h e) -> p h e", h=32),
                     bch(ela, 32, 64), ALU.mult, [bHT, bsm], [bHT])

            def h_add():
                K.tt("dve", HT[:], HT[:], phs[:], ALU.add, [bHT, bphs], [bHT])

            def h_cast():
                K.cp("dve", HTb[:], HT[:], [bHT], [bHTb])

            def mk_pye(g):
                def f():
                    ct, bct = c["ct"]
                    yd, byd = c["yd"]
                    pye, bpye = genps.get()
                    K.mm(pye[:], ct[:, g, :], HTb[:, g * 512:(g + 1) * 512], True, True, [bct, bHTb], [bpye])
                    tm, btm = tmbrot.get()
                    K.tt("dve", tm[:].rearrange("p (h e) -> p h e", h=8), pye[:].rearrange("p (h e) -> p h e", h=8),
                         bch(eac[:, g * 8:(g + 1) * 8], 8, 64), ALU.mult, [bpye, bsm], [btm])
                    K.tt("dve", yd[:, g * 512:(g + 1) * 512], yd[:, g * 512:(g + 1) * 512], tm[:], ALU.add,
                         [byd, btm], [byd])
                return f

            def seq(*fs):
                def f():
                    for x in fs:
                        x()
                return f

            if not c["lat"]:
                H[0] = seq(h_mult, h_add, h_cast)
                return H, None
            fin = None
            if dirn == 1:
                def y_dsk():
                    xs, bxs = c["xs"]
                    t1, bt1 = t1rot.get()
                    c["t1"] = (t1, bt1)
                    K.tt("dve", t1[:].rearrange("p (h e) -> p h e", h=32), xs[:].rearrange("p (h e) -> p h e", h=32),
                         bch(dsk[:], 32, 64), ALU.mult, [bxs, bdsk], [bt1])

                def y_store():
                    yd, byd = c["yd"]
                    t1, bt1 = c["t1"]
                    yb, byb = ybrot.get()
                    K.tt("dve", yb[:], t1[:], yd[:], ALU.add, [bt1, byd], [byb])
                    K.dma("pool", d["YB"][tt * 128:(tt + 1) * 128, :], yb[:], [byb], [Buf()])
                H[0] = seq(h_mult, mk_pye(0))
                H[1] = seq(mk_pye(1), h_add)
                H[2] = seq(mk_pye(2), y_dsk)
                H[3] = mk_pye(3)
                H[4] = h_cast
                H[6] = y_store
            else:
                fin = c
                H[0] = seq(h_mult, mk_pye(0))
                H[1] = seq(mk_pye(1), h_add)
                H[2] = mk_pye(2)
                H[3] = mk_pye(3)
                H[4] = h_cast
            return H, fin

        def fin_chain(c):
            xs, bxs = c["xs"]
            yd, byd = c["yd"]
            yb, byb, zs, bzs = c["fin_in"]
            xt, bx = xtrot.get()
            sap, sbf = xsrc(K, L1_SRC[0], c["tt"])
            K.dma("sp", xt[:], sap, [sbf], [bx])
            t1, bt1 = t1rot.get()
            K.tt("dve", t1[:], yd[:], yb[:], ALU.add, [byd, byb], [bt1])
            K.tt("dve", t1[:], t1[:], zs[:], ALU.mult, [bt1, bzs], [bt1])
            yn, byn = ynrot.get()
            st, bst = strot.get()
            K.cx.op("dve", [bt1], [byn, bst], lambda e: e.scalar_tensor_tensor(yn[:], t1[:], 1.0, t1[:], ALU.mult, ALU.mult,
                                                                             accum_out=st[:, 0:1]))
            rstd_op(K, st[:, 2:3], st[:, 1:2], st[:, 0:1], 1.0 / 2048, bst)
            K.ts("dve", yn[:], t1[:], st[:, 2:3], None, ALU.mult, None, [bt1, bst], [byn])
            return {"tt": c["tt"], "yn": (yn, byn), "xt": (xt, bx)}

        def fin_tr(f):
            yn, byn = f["yn"]
            ynT, byT = yTrot.get()
            f["ynT"] = (ynT, byT)
            for half in range(2):
                pt, bpt = genps.get()
                ptb = pt[:].bitcast(BF16)
                for c8 in range(8):
                    k = half * 8 + c8
                    K.tr(ptb[:, c8 * 128:(c8 + 1) * 128], yn[:, k * 128:(k + 1) * 128], K.identb[:], [byn, K.b_identb], [bpt])
                K.cp("act", ynT[:, half * 1024:(half + 1) * 1024], ptb, [bpt], [byT])

        def fin_out(f):
            tt = f["tt"]
            ynT, byT = f["ynT"]
            xt, bx = f["xt"]
            g, bg = K.G[(1, 0, 0)]
            xo, bxo = xorot.get()
            for n in range(2):
                ps, bps = genps.get()
                for k in range(16):
                    K.mm(ps[:], ynT[:, k * 128:(k + 1) * 128], wo[:, k, n * 512:(n + 1) * 512], k == 0, k == 15,
                         [byT, bwo], [bps])
                K.tt("dve", xo[:, n * 512:(n + 1) * 512], ps[:], xt[:, n * 512:(n + 1) * 512], ALU.add, [bps, bx], [bxo])
            dap, dbf = xsrc(K, "X3", tt)
            K.dma("pool", dap, xo[:], [bxo], [dbf])

        def sweep(dirn):
            K.memset("dve", HT[:], 0.0, [bHT])
            K.memset("pool", HTb[:], 0.0, [bHTb])
            order = list(range(NT)) if dirn == 0 else [1, 0] + list(range(NT - 1, NTC - 1, -1))
            p1 = [pre1(dirn, order[0]), pre1(dirn, order[1])]
            cur = p1.pop(0)
            pre2(cur)
            f_tr = None
            f_out = None
            for i in range(len(order) + 2):
                if f_out is not None:
                    fin_out(f_out)
                    f_out = None
                nxt = p1.pop(0) if p1 else None
                hooks, fc = post_hooks(cur) if cur is not None else ({}, None)
                if i + 2 < len(order):
                    holder = {}

                    def chain(prev, fn):
                        def f():
                            if prev is not None:
                                prev()
                            fn()
                        return f

                    def ha(tt2=order[i + 2]):
                        holder["c"] = pre1(dirn, tt2, defer=True)
                        p1.append(holder["c"])
                    hooks[2] = chain(hooks.get(2), ha)
                    hooks[4] = chain(hooks.get(4), lambda: holder["c"]["later"][0]())
                    hooks[6] = chain(hooks.get(6), lambda: holder["c"]["later"][1]())
                if nxt is not None:
                    pre2(nxt, hooks)
                else:
                    for k_ in sorted(hooks):
                        hooks[k_]()
                if f_tr is not None:
                    fin_tr(f_tr)
                    f_out = f_tr
                    f_tr = None
                if fc is not None:
                    f_tr = fin_chain(fc)
                cur = nxt

        sweep(1)
        K.cx.barrier()
        g1t, bg1t = K.G[(1, 0, 0)]
        for k in range(16):
            K.stt(wo[:, k, :], wo[:, k, :], nw[:, k:k + 1], g1t[:], ALU.mult, ALU.mult, [bwo, bnw, bg1t], [bwo])
        sweep(0)


def host_constants():
    C = {}
    C["c_ident"] = np.eye(128, dtype=np.float32)
    nf = 32
    inv = (10000.0 ** (-np.arange(nf, dtype=np.float32) / nf)).astype(np.float32)
    pos = np.arange(T)
    rpos = (pos // 64).astype(np.float32)
    cpos = (pos % 64).astype(np.float32)
    ang_r = (rpos[:, None] * inv[None, :]).astype(np.float32)
    ang_c = (cpos[:, None] * inv[None, :]).astype(np.float32)
    cr, sr, cc, sc = np.cos(ang_r), np.sin(ang_r), np.cos(ang_c), np.sin(ang_c)
    Cq = np.concatenate([cr, cr, cc, cc], axis=1).astype(np.float32)
    Sq = np.concatenate([-sr, sr, -sc, sc], axis=1).astype(np.float32)
    ks = np.float32(128 ** -0.5)
    rope = np.stack([Cq, Sq, Cq * ks, Sq * ks], axis=1)
    C["c_rope"] = np.ascontiguousarray(rope.reshape(NTL, 128, 4, 128)).astype(np.float32)
    j = np.arange(128)[:, None].astype(np.float32)
    i = np.arange(128)[None, :].astype(np.float32)
    tri = np.zeros((128, 8, 128), np.float32)
    tri[:, 0] = np.maximum(i - j, 0)
    tri[:, 1] = (i >= j)
    tri[:, 2] = np.maximum(j - i, 0)
    tri[:, 3] = (j >= i)
    tri[:, 4] = i + 1 + 0 * j
    tri[:, 5] = 128 - i + 0 * j
    tri[:, 6] = np.where(i >= j, 0.0, NEG)
    tri[:, 7] = np.where(j >= i, 0.0, NEG)
    C["c_tri"] = tri
    col = np.zeros((128, 2), np.float32)
    col[:, 0] = 127 - np.arange(128)
    col[:, 1] = np.arange(128)
    C["c_col"] = col
    return C


def na_bias_tiles(rpb):
    kl = (np.arange(128) // 64)[:, None]
    kc = (np.arange(128) % 64)[:, None]
    ql = (np.arange(128) // 64)[None, :]
    qc = (np.arange(128) % 64)[None, :]
    cstart = np.clip(qc - 8, 0, 48)
    colok = (kc >= cstart) & (kc < cstart + 16)
    variants = [(-3, False), (-2, False), (-2, True), (-1, False), (0, False), (1, False), (2, True),
                (2, False), (3, False)]
    out = np.full((128, 8, 9, 128), NEG, np.float32)
    for vi, (dt, interior) in enumerate(variants):
        dr = 2 * dt + kl - ql
        ok = colok & (dr >= -7) & (dr <= 7)
        if interior:
            ok = ok & (dr >= -4) & (dr <= 3)
        dri = np.clip(dr + 7, 0, 14)
        dci = np.clip(kc - qc + 15, 0, 30)
        dri, dci = np.broadcast_arrays(dri, dci)
        for h in range(8):
            g = rpb[h][dri, dci]
            out[:, h, vi, :] = np.where(ok, g, np.float32(NEG))
    return out


def make_in_maps(inputs):
    I = {k: np.asarray(v) for k, v in inputs.items()}
    C = host_constants()
    shared = dict(C)
    shared["c_ctx"] = I["c_ctx"].reshape(8, 128)
    shared["w_mod"] = I["w_mod"]
    shared["b_mod"] = I["b_mod"].reshape(2, 48, 128)
    shared["norm_mix"] = I["norm_mix"].reshape(2, 8, 128)
    shared["norm_mlp"] = I["norm_mlp"].reshape(2, 8, 128)
    def pkn(a, nk):
        return np.ascontiguousarray(a.reshape(nk, 128, a.shape[-1]).transpose(1, 0, 2))
    shared["w_mlp_in"] = np.stack([pkn(I["w_mlp_in"][l], 8) for l in range(2)])
    shared["w_mlp_out"] = np.stack([pkn(I["w_mlp_out"][l], 32) for l in range(2)])
    shared["w_in_even"] = pkn(I["w_in_even"][0], 8)
    shared["w_out_even"] = pkn(I["w_out_even"][0], 8)
    shared["ret_decay_logit"] = I["ret_decay_logit"].reshape(8)
    shared["na_bias"] = na_bias_tiles(I["na_rpb"][0])
    shared["w_in_odd"] = pkn(I["w_in_odd"][0], 8)
    shared["conv_w"] = I["conv_w"][0].reshape(7, 24, 128)
    shared["conv_b"] = I["conv_b"][0].reshape(24, 128)
    shared["dt_bias"] = I["dt_bias"].reshape(64)
    shared["a_log"] = I["a_log"].reshape(64)
    shared["d_skip"] = I["d_skip"].reshape(32)
    shared["ssm_norm"] = I["ssm_norm"].reshape(16, 128)
    shared["w_out_odd"] = pkn(I["w_out_odd"][0], 16)
    shared["norm_final"] = I["norm_final"].reshape(D)
    shared = {k: np.ascontiguousarray(v, dtype=np.float32) for k, v in shared.items()}
    maps = []
    for b in range(8):
        m = dict(shared)
        m["x"] = np.ascontiguousarray(I["x"][b], dtype=np.float32)
        m["ctx"] = np.ascontiguousarray(I["ctx"][b], dtype=np.float32)
        m["c"] = np.ascontiguousarray(I["c"][b].reshape(8, 128), dtype=np.float32)
        maps.append(m)
    return maps


def build(dbg=False):
    K = Kern(dbg)
    setup(K)
    phase_mod(K)
    phase_l0_ret_proj(K)
    phase_l0_na(K)
    phase_l0_ret(K)
    phase_mlp(K, 0, "X1", "X2", list(range(NT)))
    phase_l1_proj(K)
    phase_l1_conv(K)
    phase_l1_scan(K)
    phase_mlp(K, 1, "X3", "Y", list(range(NTC, NT)), final=True)
    K.cx.barrier()
    return K


_CACHE = {}


def kernel(**inputs):
    if "K" not in _CACHE:
        _CACHE["K"] = build()
    K = _CACHE["K"]
    maps = make_in_maps(inputs)
    used = set(K.dram.keys())
    maps = [{k: v for k, v in m.items() if k in used} for m in maps]
    res = run_bass_kernel_spmd(K.nc, maps, core_ids=list(range(8)))
    return np.stack([np.asarray(r["y"], dtype=np.float32) for r in res.results], axis=0)
```

```python
import numpy as np
import ml_dtypes
import concourse.bass as bass
import concourse.mybir as mybir
from concourse.bass_utils import run_bass_kernel_spmd

F32 = mybir.dt.float32
BF16 = mybir.dt.bfloat16
AF = mybir.ActivationFunctionType
ALU = mybir.AluOpType
AX = mybir.AxisListType


class Buf:
    __slots__ = ("name", "w", "r")

    def __init__(self, name=""):
        self.name = name
        self.w = None
        self.r = {}


class Ctx:
    ENG = ("pe", "act", "dve", "pool", "sp")

    def __init__(self, nc, n_dma_sems=36):
        self.nc = nc
        self.eng = {"pe": nc.tensor, "act": nc.scalar, "dve": nc.vector,
                    "pool": nc.gpsimd, "sp": nc.sync}
        self.sem = {}
        self.cnt = {}
        for e in self.ENG:
            self.sem[e] = nc.alloc_semaphore("sem_" + e)
            self.cnt[e] = 0
        self.dma_sems = []
        for i in range(n_dma_sems):
            k = "dma%d" % i
            self.sem[k] = nc.alloc_semaphore(k)
            self.cnt[k] = 0
            self.dma_sems.append(k)
        self.dma_rr = 0
        self.grp_sems = []
        for i in range(20):
            k = "grp%d" % i
            self.sem[k] = nc.alloc_semaphore(k)
            self.cnt[k] = 0
            self.grp_sems.append(k)
        self.grp_rr = 0
        self.seen = {e: {} for e in self.ENG}
        self.nwait = 0
        self.nins = 0

    def _wait(self, e, key, val):
        if val <= 0:
            return
        if self.seen[e].get(key, 0) >= val:
            return
        self.eng[e].wait_ge(self.sem[key], val)
        self.seen[e][key] = val
        self.nwait += 1

    def _deps(self, e, reads, writes):
        deps = {}

        def add(k, v):
            if deps.get(k, 0) < v:
                deps[k] = v
        for b in reads:
            if b.w is not None:
                add(*b.w)
        for b in writes:
            if b.w is not None:
                if not (b.w[0] == e):
                    add(*b.w)
            for k, v in b.r.items():
                if k != e:
                    add(k, v)
        if e == "pe":
            deps.pop("pe", None)
        for k, v in deps.items():
            self._wait(e, k, v)

    def _mark(self, key, val, reads, writes):
        for b in reads:
            if b.r.get(key, 0) < val:
                b.r[key] = val
        for b in writes:
            b.w = (key, val)
            b.r = {}

    def op(self, e, reads, writes, fn, inc=True):
        self._deps(e, reads, writes)
        ins = fn(self.eng[e])
        if inc:
            self.cnt[e] += 1
            ins.then_inc(self.sem[e], 1)
            self._mark(e, self.cnt[e], reads, writes)
        else:
            self._mark(e, self.cnt[e] + 1, reads, writes)
        self.nins += 1
        return ins

    def new_group(self):
        k = self.grp_sems[self.grp_rr]
        self.grp_rr = (self.grp_rr + 1) % len(self.grp_sems)
        return k

    def dma(self, q, out_ap, in_ap, reads, writes, group=None, **kw):
        if group is not None:
            k = group
        else:
            k = self.dma_sems[self.dma_rr]
            self.dma_rr = (self.dma_rr + 1) % len(self.dma_sems)
            self._wait(q, k, self.cnt[k])
        self._deps(q, reads, writes)
        ins = self.eng[q].dma_start(out=out_ap, in_=in_ap, **kw)
        self.cnt[k] += 16
        ins.then_inc(self.sem[k], 16)
        self._mark(k, self.cnt[k], reads, writes)
        self.nins += 1
        return ins

    def barrier(self):
        for e in self.ENG:
            for k in list(self.cnt.keys()):
                if k == e and e == "pe":
                    continue
                self._wait(e, k, self.cnt[k])

    def final_wait(self, bufs):
        for b in bufs:
            if b.w is not None:
                self._wait("sp", *b.w)


D = 1024
T = 4096
CTX = 256
NTL = T // 128
NTC = CTX // 128
NT = NTL + NTC
EPS = 1e-6
NEG = -30000.0
EVEN_IN = 3584
ODD_IN = 5184
XLT = 262 + 4102


class Rot:
    def __init__(self, K, name, shape, dt, n, psum=False):
        self.items = []
        for i in range(n):
            nm = "%s_%d_%d" % (name, K.uid(), i)
            if psum:
                t = K.stack.enter_context(K.nc.psum_tensor(nm, shape, dt))
            else:
                t = K.stack.enter_context(K.nc.sbuf_tensor(nm, shape, dt))
            self.items.append((t, Buf(nm)))
        self.i = 0

    def get(self):
        it = self.items[self.i]
        self.i = (self.i + 1) % len(self.items)
        return it


class Kern:
    def __init__(self, dbg=False, stages=None):
        from contextlib import ExitStack
        self.dbg = dbg
        self.stages = stages
        self.nc = bass.Bass("TRN2", target_bir_lowering=False)
        self.cx = Ctx(self.nc)
        self._uid = 0
        self.gstack = ExitStack()
        self.stack = self.gstack
        self.dram = {}
        self.dbufs = {}
        self.outputs = []

    def uid(self):
        self._uid += 1
        return self._uid

    def din(self, name, shape, dt=F32):
        self.dram[name] = self.nc.dram_tensor(name, list(shape), dt, kind="ExternalInput").ap()
        return self.dram[name]

    def dscr(self, name, shape, dt, out=False):
        kind = "ExternalOutput" if (out or self.dbg) else "Internal"
        self.dram[name] = self.nc.dram_tensor(name, list(shape), dt, kind=kind).ap()
        if kind == "ExternalOutput":
            self.outputs.append(name)
        return self.dram[name]

    def db(self, name, idx=0):
        k = (name, idx)
        if k not in self.dbufs:
            self.dbufs[k] = Buf("%s_%s" % (name, idx))
        return self.dbufs[k]

    def sb(self, name, shape, dt=F32):
        t = self.stack.enter_context(self.nc.sbuf_tensor("%s_%d" % (name, self.uid()), list(shape), dt))
        return t, Buf(name)

    def phase(self):
        from contextlib import ExitStack, contextmanager
        K = self

        @contextmanager
        def cm():
            K.cx.barrier()
            old = K.stack
            with ExitStack() as st:
                K.stack = st
                yield
                K.cx.barrier()
            K.stack = old
        return cm()

    def mm(self, out, lhsT, rhs, start, stop, r, w):
        return self.cx.op("pe", r, w, lambda e: e.matmul(out, lhsT, rhs, start=start, stop=stop), inc=bool(stop))

    def tr(self, out, in_, ident, r, w):
        return self.cx.op("pe", r, w, lambda e: e.transpose(out, in_, ident))

    def act(self, out, in_, func, r, w, bias=None, scale=None, accum_out=None, eng="act"):
        kw = {}
        if bias is not None:
            kw["bias"] = bias
        if scale is not None:
            kw["scale"] = scale
        if accum_out is not None:
            kw["accum_out"] = accum_out
        return self.cx.op("act", r, w, lambda e: e.activation(out, in_, func, **kw))

    def tt(self, eng, out, in0, in1, op, r, w):
        return self.cx.op(eng, r, w, lambda e: e.tensor_tensor(out, in0, in1, op))

    def ts(self, eng, out, in0, s1, s2, op0, op1, r, w):
        if op1 is None:
            return self.cx.op(eng, r, w, lambda e: e.tensor_scalar(out, in0, s1, None, op0))
        return self.cx.op(eng, r, w, lambda e: e.tensor_scalar(out, in0, s1, s2, op0, op1))

    def stt(self, out, in0, scalar, in1, op0, op1, r, w):
        return self.cx.op("dve", r, w, lambda e: e.scalar_tensor_tensor(out, in0, scalar, in1, op0, op1))

    def cp(self, eng, out, in_, r, w):
        if eng == "act":
            return self.cx.op("act", r, w, lambda e: e.copy(out, in_))
        return self.cx.op(eng, r, w, lambda e: e.tensor_copy(out, in_))

    def memset(self, eng, ap, val, w):
        return self.cx.op(eng, [], w, lambda e: e.memset(ap, val))

    def dma(self, q, out, in_, r, w, **kw):
        return self.cx.dma(q, out, in_, r, w, **kw)


def setup(K):
    nc = K.nc
    K.din("x", [T, D]); K.din("ctx", [CTX, D])
    K.din("c", [8, 128]); K.din("c_ctx", [8, 128])
    K.din("w_mod", [2, D, 6 * D]); K.din("b_mod", [2, 48, 128])
    K.din("norm_mix", [2, 8, 128]); K.din("norm_mlp", [2, 8, 128])
    K.din("w_mlp_in", [2, 128, 8, 4 * D]); K.din("w_mlp_out", [2, 128, 32, D])
    K.din("w_in_even", [128, 8, EVEN_IN]); K.din("w_out_even", [128, 8, D])
    K.din("ret_decay_logit", [8]); K.din("na_bias", [128, 8, 9, 128])
    K.din("w_in_odd", [128, 8, ODD_IN]); K.din("conv_w", [7, 24, 128]); K.din("conv_b", [24, 128])
    K.din("dt_bias", [64]); K.din("a_log", [64]); K.din("d_skip", [32])
    K.din("ssm_norm", [16, 128]); K.din("w_out_odd", [128, 16, D]); K.din("norm_final", [D])
    K.din("c_ident", [128, 128]); K.din("c_rope", [NTL, 128, 4, 128])
    K.din("c_tri", [128, 8, 128])
    K.din("c_col", [128, 2])
    K.dscr("y", [T, D], F32, out=True)
    for nm in ("X1", "X2", "X3"):
        K.dscr(nm, [NT * 128, D], F32)
    for nm in ("RQ", "RK", "RV", "RG", "NAO"):
        K.dscr(nm, [NT * 128, 512], BF16)
    K.dscr("ZS", [NT * 128, 2048], BF16)
    K.dscr("DT", [NT * 128, 64], F32)
    K.dscr("XBCT", [24, 128, XLT], BF16)
    K.dscr("XS", [NT * 128, 2048], BF16)
    K.dscr("BS", [NT * 128, 512], BF16)
    K.dscr("BT", [4, 128, NT * 128], BF16)
    K.dscr("CTt", [4, 128, NT * 128], BF16)
    K.dscr("YB", [NT * 128, 2048], BF16)
    K.ident, K.b_ident = K.sb("ident", [128, 128])
    K.identb, K.b_identb = K.sb("identb", [128, 128], BF16)
    K.ones, K.b_ones = K.sb("ones", [128, 128])
    K.dma("sp", K.ident[:], K.dram["c_ident"][:, :], [], [K.b_ident])
    K.cp("dve", K.identb[:], K.ident[:], [K.b_ident], [K.b_identb])
    K.memset("dve", K.ones[:], 1.0, [K.b_ones])
    K.PS = Rot(K, "ps", [128, 512], F32, 8, psum=True)
    K.PSa = Rot.__new__(Rot); K.PSa.items = K.PS.items[0:4]; K.PSa.i = 0
    K.PSb = Rot.__new__(Rot); K.PSb.items = K.PS.items[4:8]; K.PSb.i = 0
    K.modT = []
    K.cols = {}


def xsrc(K, stream, tt):
    if stream == "X0":
        if tt < NTC:
            return K.dram["ctx"][tt * 128:(tt + 1) * 128, :], K.db("ctx", tt)
        return K.dram["x"][(tt - NTC) * 128:(tt - NTC + 1) * 128, :], K.db("x", tt)
    if stream == "Y":
        return K.dram["y"][(tt - NTC) * 128:(tt - NTC + 1) * 128, :], K.db("y", tt)
    return K.dram[stream][tt * 128:(tt + 1) * 128, :], K.db(stream, tt)


def load_cols(K, specs):
    tot = sum(ap.shape[0] for _, ap in specs)
    assert tot <= 128
    stg, bstg = K.sb("stg", [128, 128])
    dst, bdst = K.sb("cols", [128, tot])
    r0 = 0
    offs = {}
    for nm, ap in specs:
        R = ap.shape[0]
        K.dma("sp", stg[r0:r0 + R, :], ap, [], [bstg])
        offs[nm] = (r0, R)
        r0 += R
    ps, bps = K.PS.get()
    K.tr(ps[:, 0:tot], stg[0:tot, :], K.ident[0:tot, 0:tot], [bstg, K.b_ident], [bps])
    K.cp("dve", dst[:], ps[:, 0:tot], [bps], [bdst])
    for nm, (o, R) in offs.items():
        K.cols[nm] = (dst[:, o:o + R], bdst)


def phase_mod(K):
    d = K.dram
    load_cols(K, [("c", d["c"][:, :]), ("c_ctx", d["c_ctx"][:, :]),
                  ("b_mod0", d["b_mod"][0]), ("b_mod1", d["b_mod"][1])])
    load_cols(K, [("norm_mix0", d["norm_mix"][0]), ("norm_mix1", d["norm_mix"][1]),
                  ("norm_mlp0", d["norm_mlp"][0]), ("norm_mlp1", d["norm_mlp"][1]),
                  ("ssm_norm", d["ssm_norm"][:, :]), ("conv_b", d["conv_b"][:, :])])
    load_cols(K, [("conv_w%d" % k, d["conv_w"][k]) for k in range(4)])
    load_cols(K, [("conv_w%d" % k, d["conv_w"][k]) for k in range(4, 7)])
    sc, bsc = K.sb("sc", [128, 8, 2])
    cc, bcc = K.cols["c"]
    K.act(sc[:, :, 0], cc, AF.Silu, [bcc], [bsc])
    K.act(sc[:, :, 1], K.cols["c_ctx"][0], AF.Silu, [bcc], [bsc])
    K.A = {}
    K.G = {}
    for l in range(2):
        K.modT.append(K.gsb("modT%d" % l, [128, 48, 2]))
    A_pre = {(l, s, which): K.gsb("A", [128, 8]) for l in range(2) for s in range(2) for which in range(2)}
    G_pre = {(l, s, which): K.gsb("G", [128, 1024], BF16) for l in range(2) for s in range(2) for which in range(2)
             if not (l == 1 and s == 1)}
    with K.phase():
        wrot = Rot(K, "wmod", [128, 8, 1024], F32, 2)
        for l in range(2):
            modT, bmod = K.modT[l]
            ps, bps = K.PS.get()
            for cg in range(6):
                wp, bwp = wrot.get()
                for k in range(8):
                    K.dma("sp", wp[:, k, :], d["w_mod"][l, k * 128:(k + 1) * 128, cg * 1024:(cg + 1) * 1024],
                          [], [bwp])
                for j8 in range(8):
                    j = cg * 8 + j8
                    for k in range(8):
                        K.mm(ps[:, 2 * j:2 * j + 2], wp[:, k, j8 * 128:(j8 + 1) * 128], sc[:, k, :],
                             k == 0, k == 7, [bwp, bsc], [bps])
            bm, bbm = K.cols["b_mod%d" % l]
            for s in range(2):
                K.tt("dve", modT[:, :, s], ps[:, s:96:2], bm, ALU.add, [bps, bbm], [bmod])
    for l in range(2):
        modT, bmod = K.modT[l]
        for s in range(2):
            for which, (vsc, vsh, nname) in enumerate(((1, 0, "norm_mix%d" % l), (4, 3, "norm_mlp%d" % l))):
                a, ba = A_pre[(l, s, which)]
                nm, bnm = K.cols[nname]
                K.stt(a[:], modT[:, vsc * 8:vsc * 8 + 8, s], 1.0, nm, ALU.add, ALU.mult, [bmod, bnm], [ba])
                K.A[(l, s, which)] = (a[:], modT[:, vsh * 8:vsh * 8 + 8, s], ba, bmod)
    with K.phase():
        dg = Rot(K, "diag", [128, 128], F32, 2)
        for l in range(2):
            modT, bmod = K.modT[l]
            for s in range(2):
                if l == 1 and s == 1:
                    continue
                for which, v in enumerate((2, 5)):
                    g, bg = G_pre[(l, s, which)]
                    for half in range(2):
                        ps, bps = K.PS.get()
                        for kk in range(4):
                            k = half * 4 + kk
                            dd, bdd = dg.get()
                            K.ts("dve", dd[:], K.ident[:], modT[:, v * 8 + k, s:s + 1], None, ALU.mult, None,
                                 [K.b_ident, bmod], [bdd])
                            K.mm(ps[:, kk * 128:(kk + 1) * 128], K.ones[:], dd[:], True, True,
                                 [K.b_ones, bdd], [bps])
                        K.cp("act", g[:, half * 512:(half + 1) * 512], ps[:], [bps], [bg])
                    K.G[(l, s, which)] = (g, bg)


def _gsb(K, name, shape, dt=F32):
    t = K.gstack.enter_context(K.nc.sbuf_tensor("%s_%d" % (name, K.uid()), list(shape), dt))
    return t, Buf(name)


Kern.gsb = _gsb


def rstd_op(K, out, tmp, in_, scale, b):
    K.act(tmp, in_, AF.Ln, [b], [b], bias=EPS, scale=scale)
    K.act(out, tmp, AF.Exp, [b], [b], scale=-0.5)


class NormCtx:
    def __init__(self, K, nx=3, nxn=2):
        self.xin = Rot(K, "xin", [128, D], F32, nx)
        self.junk = Rot(K, "junk", [128, D], BF16, 1)
        self.xn = Rot(K, "xn", [128, D], F32, nxn)
        self.st = Rot(K, "nst", [128, 4], F32, 4)


def norm_pre(K, N, src_ap, bsrc):
    xt, bx = N.xin.get()
    K.dma("sp", xt[:], src_ap, [bsrc], [bx])
    jk, bj = N.junk.get()
    st, bst = N.st.get()
    K.act(jk[:], xt[:], AF.Square, [bx], [bj, bst], accum_out=st[:, 0:1])
    rstd_op(K, st[:, 2:3], st[:, 1:2], st[:, 0:1], 1.0 / D, bst)
    xn, bxn = N.xn.get()
    K.ts("dve", xn[:], xt[:], st[:, 2:3], None, ALU.mult, None, [bx, bst], [bxn])
    return xt, bx, xn, bxn


def norm_tr(K, pre, AB, hT, bhT, col0, psrot=None):
    a, b, ba, bb = AB
    psrot = psrot or K.PS
    xt, bx, xn, bxn = pre
    for half in range(2):
        ps, bps = psrot.get()
        for kk in range(4):
            k = half * 4 + kk
            K.tr(ps[:, kk * 128:(kk + 1) * 128], xn[:, k * 128:(k + 1) * 128], K.ident[:], [bxn, K.b_ident], [bps])
        for kk in range(4):
            k = half * 4 + kk
            if kk % 2 == 0:
                K.act(hT[:, k, col0:col0 + 128], ps[:, kk * 128:(kk + 1) * 128], AF.Identity, [bps, ba, bb], [bhT],
                      bias=b[:, k:k + 1], scale=a[:, k:k + 1])
            else:
                K.ts("dve", hT[:, k, col0:col0 + 128], ps[:, kk * 128:(kk + 1) * 128], a[:, k:k + 1], b[:, k:k + 1],
                     ALU.mult, ALU.add, [bps, ba, bb], [bhT])
    return xt, bx


def norm_tile(K, N, src_ap, bsrc, AB, hT, bhT, col0, psrot=None):
    return norm_tr(K, norm_pre(K, N, src_ap, bsrc), AB, hT, bhT, col0, psrot)


def load_w_bf16(K, w, dram_ap, nk, c0, c1, bw):
    K.dma("pool", w[:, :, 0:c1 - c0], dram_ap[:, :, c0:c1], [], [bw], group=K.cx.new_group())


def load_w_cols(K, w, dram_ap, nk, c0, blocks, split_k=False):
    bufs = []
    for (a, b) in blocks:
        bf = Buf("wblk")
        K.dma("pool", w[:, :, a - c0:b - c0], dram_ap[:, :, a:b], [], [bf], group=K.cx.new_group())
        bufs.append(bf)
    return bufs


def phase_mlp(K, l, src, dst, tiles, final=False):
    d = K.dram
    LAG = 5
    with K.phase():
        w1, bw1 = K.sb("w1", [128, 8, 4 * D], BF16)
        w2, bw2 = K.sb("w2", [128, 32, D], BF16)
        bw1s, bw2s = [], []
        src2 = d["w_mlp_out"][l]
        for b_ in range(8):
            bw1s += load_w_cols(K, w1, d["w_mlp_in"][l], 8, 0, [(b_ * 512, (b_ + 1) * 512)])
            bf = Buf("w2blk")
            K.dma("pool", w2[:, b_ * 4:(b_ + 1) * 4, :], src2[:, b_ * 4:(b_ + 1) * 4, :], [], [bf], group=K.cx.new_group())
            bw2s.append(bf)
        N = NormCtx(K, nx=4, nxn=2)
        hrot = Rot(K, "hT", [128, 8, 256], BF16, 2)
        urot = Rot(K, "uT", [128, 256], BF16, 7)
        rrot = Rot(K, "relu", [128, 256], F32, 2)
        trot = Rot(K, "tmp", [128, 512], F32, 4)
        orot = Rot(K, "xo", [128, D], F32, 2)
        strot = Rot(K, "fst", [128, 4], F32, 2)
        if final:
            nf, bnf = K.sb("nf", [128, D])
            K.dma("sp", nf[:], d["norm_final"].partition_broadcast(128), [], [bnf])
            jrot = Rot(K, "fjunk", [128, D], BF16, 1)
        blocks = []
        ctxs = [t for t in tiles if t < NTC]
        lats = [t for t in tiles if t >= NTC]
        if ctxs:
            blocks.append(ctxs)
        for i in range(0, len(lats), 2):
            blocks.append(lats[i:i + 2])

        def front_pre(blk):
            return [norm_pre(K, N, *xsrc(K, src, tt)) for tt in blk]

        def front_tr(blk, pres):
            s = 1 if blk[0] < NTC else 0
            hT, bhT = hrot.get()
            xs = [norm_tr(K, pres[i], K.A[(l, s, 1)], hT, bhT, i * 128, psrot=K.PSb) for i in range(len(blk))]
            return hT, bhT, xs

        cur = front_tr(blocks[0], front_pre(blocks[0]))
        nxt_pre = None
        for bi, blk in enumerate(blocks):
            s = 1 if blk[0] < NTC else 0
            nt = len(blk) * 128
            hT, bhT, xs = cur
            acc = [[K.PSa.get() for n in range(2)] for i in range(len(blk))]
            us = {}
            for f in range(32 + LAG):
                if f < 32:
                    ps, bps = K.PSb.get()
                    for k in range(8):
                        K.mm(ps[:, 0:nt], w1[:, k, f * 128:(f + 1) * 128], hT[:, k, 0:nt], k == 0, k == 7,
                             [bw1s[f // 4], bhT], [bps])
                    r, br = rrot.get()
                    K.ts("dve", r[:, 0:nt], ps[:, 0:nt], 0.0, None, ALU.max, None, [bps], [br])
                    u, bu = urot.get()
                    K.act(u[:, 0:nt], r[:, 0:nt], AF.Square, [br], [bu])
                    us[f] = (u, bu)
                if f == 2 and bi + 1 < len(blocks):
                    nxt_pre = front_pre(blocks[bi + 1])
                if f == 22 and bi + 1 < len(blocks):
                    cur = front_tr(blocks[bi + 1], nxt_pre)
                if f >= LAG:
                    f2 = f - LAG
                    u, bu = us.pop(f2)
                    for i in range(len(blk)):
                        for n in range(2):
                            pa, bpa = acc[i][n]
                            K.mm(pa[:], u[:, i * 128:(i + 1) * 128], w2[:, f2, n * 512:(n + 1) * 512], f2 == 0, f2 == 31,
                                 [bu, bw2s[f2 // 4]], [bpa])
            g, bg = K.G[(l, s, 1)]
            tms = {}
            for i, tt in enumerate(blk):
                for n in range(2):
                    pa, bpa = acc[i][n]
                    tm, btm = trot.get()
                    K.tt("dve", tm[:], pa[:], g[:, n * 512:(n + 1) * 512], ALU.mult, [bpa, bg], [btm])
                    tms[(i, n)] = (tm, btm)
            for i, tt in enumerate(blk):
                xt, bx = xs[i]
                xo, bxo = orot.get()
                for n in range(2):
                    tm, btm = tms[(i, n)]
                    K.tt("dve", xo[:, n * 512:(n + 1) * 512], tm[:], xt[:, n * 512:(n + 1) * 512], ALU.add,
                         [btm, bx], [bxo])
                dap, dbf = xsrc(K, dst, tt)
                if final:
                    jk, bj = jrot.get()
                    st, bst = strot.get()
                    K.act(jk[:], xo[:], AF.Square, [bxo], [bj, bst], accum_out=st[:, 0:1])
                    rstd_op(K, st[:, 2:3], st[:, 1:2], st[:, 0:1], 1.0 / D, bst)
                    K.stt(xo[:], xo[:], st[:, 2:3], nf[:], ALU.mult, ALU.mult, [bxo, bst, bnf], [bxo])
                K.dma("pool", dap, xo[:], [bxo], [dbf])


def rope(K, ps, bps, rp, brp, ci, si, o, bo, t1rot, t2rot):
    t1, b1 = t1rot.get()
    t2, b2 = t2rot.get()
    K.tt("dve", t1[:].rearrange("p (h d) -> p h d", h=4), ps[:].rearrange("p (h d) -> p h d", h=4),
         rp[:, ci, :].unsqueeze(1).to_broadcast([128, 4, 128]), ALU.mult, [bps, brp], [b1])
    q5 = ps[:].rearrange("p (h a two i) -> p h a two i", h=4, a=2, two=2)
    t5 = t2[:].rearrange("p (h a two i) -> p h a two i", h=4, a=2, two=2)
    s4 = rp[:, si, :].rearrange("p (a two i) -> p a two i", a=2, two=2)
    for two in range(2):
        K.tt("dve", t5[:, :, :, two, :], q5[:, :, :, 1 - two, :],
             s4[:, :, two, :].unsqueeze(1).to_broadcast([128, 4, 2, 32]), ALU.mult, [bps, brp], [b2])
    K.tt("dve", o[:], t1[:], t2[:], ALU.add, [b1, b2], [bo])


def phase_l0_ret_proj(K):
    d = K.dram
    with K.phase():
        w, bw = K.sb("wr", [128, 8, 2048], BF16)
        bws = load_w_cols(K, w, d["w_in_even"], 8, 0, [(n_ * 512, (n_ + 1) * 512) for n_ in range(4)])
        N = NormCtx(K, nx=2, nxn=2)
        hrot = Rot(K, "hT", [128, 8, 128], BF16, 3)
        rrot = Rot(K, "rope", [128, 4, 128], F32, 2)
        t1rot = Rot(K, "t1", [128, 512], F32, 2)
        t2rot = Rot(K, "t2", [128, 512], F32, 2)
        orot = Rot(K, "o", [128, 512], BF16, 6)
        def front_tr(tt, pre):
            s_ = 1 if tt < NTC else 0
            hT_, bhT_ = hrot.get()
            norm_tr(K, pre, K.A[(0, s_, 0)], hT_, bhT_, 0)
            return hT_, bhT_

        cur = front_tr(0, norm_pre(K, N, *xsrc(K, "X0", 0)))
        for tt in range(NT):
            s = 1 if tt < NTC else 0
            hT, bhT = cur
            if tt + 1 < NT:
                npre = norm_pre(K, N, *xsrc(K, "X0", tt + 1))
            if s == 0:
                rp, brp = rrot.get()
                K.dma("sp", rp[:], d["c_rope"][tt - NTC], [], [brp])
            for n, nm in enumerate(("RQ", "RK", "RV", "RG")):
                if n == 3 and tt + 1 < NT:
                    cur = front_tr(tt + 1, npre)
                ps, bps = K.PS.get()
                for k in range(8):
                    K.mm(ps[:], hT[:, k, :], w[:, k, n * 512:(n + 1) * 512], k == 0, k == 7, [bhT, bws[n]], [bps])
                o, bo = orot.get()
                if n < 2 and s == 0:
                    rope(K, ps, bps, rp, brp, 2 * n, 2 * n + 1, o, bo, t1rot, t2rot)
                elif n == 1:
                    K.cx.op("act", [bps], [bo], lambda e: e.mul(o[:], ps[:], float(128 ** -0.5)))
                elif n == 3:
                    K.act(o[:], ps[:], AF.Silu, [bps], [bo])
                else:
                    K.cp("act", o[:], ps[:], [bps], [bo])
                K.dma("pool", d[nm][tt * 128:(tt + 1) * 128, :], o[:], [bo], [K.db(nm, tt)])


def na_slots(tt):
    ctxs = [(0, None), (1, None)]
    if tt < NTC:
        return ctxs
    t = tt - NTC
    if t == 0:
        loc = [(0, 4), (1, 5), (2, 7), (3, 8)]
    elif t == 1:
        loc = [(0, 3), (1, 4), (2, 5), (3, 7)]
    elif t == NTL - 2:
        loc = [(NTL - 4, 1), (NTL - 3, 3), (NTL - 2, 4), (NTL - 1, 5)]
    elif t == NTL - 1:
        loc = [(NTL - 4, 0), (NTL - 3, 1), (NTL - 2, 3), (NTL - 1, 4)]
    else:
        loc = [(t - 2, 2), (t - 1, 3), (t, 4), (t + 1, 5), (t + 2, 6)]
    return [(kt + NTC, v) for kt, v in loc] + ctxs


def phase_l0_na(K):
    d = K.dram
    with K.phase():
        NQT, _ = K.sb("NQT", [128, NT, 512], BF16)
        NKT, _ = K.sb("NKT", [128, NT, 512], BF16)
        NV, _ = K.sb("NV", [128, NT, 8, 65], BF16)
        bNQ = [Buf("nq%d" % i) for i in range(NT)]
        bNK = [Buf("nk%d" % i) for i in range(NT)]
        bNV = [Buf("nv%d" % i) for i in range(NT)]
        with K.phase():
            w, bw = K.sb("wn", [128, 8, 1536], BF16)
            bws = load_w_cols(K, w, d["w_in_even"], 8, 2048, [(2048 + n_ * 512, 2048 + (n_ + 1) * 512) for n_ in range(3)])
            N = NormCtx(K, nx=2, nxn=2)
            hrot = Rot(K, "hT", [128, 8, 128], BF16, 3)
            qrot = Rot(K, "qb", [128, 512], BF16, 3)
            def front_tr(tt, pre):
                s_ = 1 if tt < NTC else 0
                hT_, bhT_ = hrot.get()
                norm_tr(K, pre, K.A[(0, s_, 0)], hT_, bhT_, 0)
                return hT_, bhT_

            cur = front_tr(0, norm_pre(K, N, *xsrc(K, "X0", 0)))
            for tt in range(NT):
                s = 1 if tt < NTC else 0
                hT, bhT = cur
                if tt + 1 < NT:
                    npre = norm_pre(K, N, *xsrc(K, "X0", tt + 1))
                K.memset("pool", NV[:, tt, :, 64:65], 1.0, [bNV[tt]])
                for n in range(3):
                    if n == 2 and tt + 1 < NT:
                        cur = front_tr(tt + 1, npre)
                    ps, bps = K.PS.get()
                    for k in range(8):
                        K.mm(ps[:], hT[:, k, :], w[:, k, n * 512:(n + 1) * 512], k == 0, k == 7, [bhT, bws[n]], [bps])
                    if n == 2:
                        K.cp("act", NV[:, tt, :, 0:64], ps[:].rearrange("p (h e) -> p h e", h=8), [bps], [bNV[tt]])
                        continue
                    qb, bqb = qrot.get()
                    K.cp("act", qb[:], ps[:], [bps], [bqb])
                    pt, bpt = K.PS.get()
                    ptb = pt[:, 0:256].bitcast(BF16)
                    for g in range(4):
                        K.tr(ptb[:, g * 128:(g + 1) * 128], qb[:, g * 128:(g + 1) * 128], K.identb[:],
                             [bqb, K.b_identb], [bpt])
                    dst, bd = (NQT, bNQ[tt]) if n == 0 else (NKT, bNK[tt])
                    K.cp("dve", dst[:, tt, :], ptb, [bpt], [bd])
        with K.phase():
            bias, bbias = K.sb("nabias", [128, 8 * 9 * 128], BF16)
            K.dma("pool", bias[:], d["na_bias"].rearrange("p h v q -> p (h v q)"), [], [bbias])
            sbrot = Rot(K, "sbias", [128, 512], F32, 3)
            ptrot = Rot(K, "PT", [128, 7 * 128], BF16, 3)
            orot = Rot(K, "nao", [128, 512], BF16, 2)
            rcrot = Rot(K, "rec", [128, 8], F32, 2)
            for tt in range(NT):
                slots = na_slots(tt)
                nsl = len(slots)
                psO = [K.PSb.get(), K.PSb.get()]
                pend = None

                def issue_pv(p):
                    h_, PT_, bPT_ = p
                    po, bpo = psO[h_ // 4]
                    hc = (h_ % 4) * 65
                    for si_, (kt_, var_) in enumerate(slots):
                        K.mm(po[:, hc:hc + 65], PT_[:, si_ * 128:(si_ + 1) * 128], NV[:, kt_, h_, :], si_ == 0,
                             si_ == nsl - 1, [bPT_, bNV[kt_]], [bpo])

                for h in range(8):
                    g, hh = divmod(h, 2)
                    p0 = hh * 64
                    banks = [K.PSa.get()]
                    if nsl > 4:
                        banks.append(K.PSa.get())
                    for si, (kt, var) in enumerate(slots):
                        bk, bbk = banks[si // 4]
                        pos = si % 4
                        K.mm(bk[:, pos * 128:(pos + 1) * 128], NKT[p0:p0 + 64, kt, g * 128:(g + 1) * 128],
                             NQT[p0:p0 + 64, tt, g * 128:(g + 1) * 128], True, True, [bNK[kt], bNQ[tt]], [bbk])
                    PT, bPT = ptrot.get()
                    si = 0
                    while si < nsl:
                        kt, var = slots[si]
                        sj = si + 1
                        while sj < nsl and sj // 4 == si // 4 and (
                                (var is None and slots[sj][1] is None) or
                                (var is not None and slots[sj][1] is not None and slots[sj][1] == slots[sj - 1][1] + 1)):
                            sj += 1
                        bk, bbk = banks[si // 4]
                        c0 = (si % 4) * 128
                        w_ = (sj - si) * 128
                        if var is None:
                            K.act(PT[:, si * 128:sj * 128], bk[:, c0:c0 + w_], AF.Exp, [bbk], [bPT], scale=0.125)
                        else:
                            sbt, bsb = sbrot.get()
                            b0 = (h * 9 + var) * 128
                            K.stt(sbt[:, 0:w_], bk[:, c0:c0 + w_], 0.125, bias[:, b0:b0 + w_], ALU.mult, ALU.add,
                                  [bbk, bbias], [bsb])
                            K.act(PT[:, si * 128:sj * 128], sbt[:, 0:w_], AF.Exp, [bsb], [bPT])
                        si = sj
                    if pend is not None:
                        issue_pv(pend)
                    pend = (h, PT, bPT)
                issue_pv(pend)
                o, bo = orot.get()
                rc, brc = rcrot.get()
                for hb in range(2):
                    po, bpo = psO[hb]
                    pv = po[:, 0:260].rearrange("p (h e) -> p h e", h=4)
                    K.cx.op("dve", [bpo], [brc], lambda e: e.reciprocal(rc[:, hb * 4:(hb + 1) * 4], pv[:, :, 64]))
                    K.tt("dve", o[:, hb * 256:(hb + 1) * 256].rearrange("p (h e) -> p h e", h=4), pv[:, :, 0:64],
                         rc[:, hb * 4:(hb + 1) * 4].unsqueeze(2).to_broadcast([128, 4, 64]), ALU.mult, [bpo, brc], [bo])
                K.dma("pool", d["NAO"][tt * 128:(tt + 1) * 128, :], o[:], [bo], [K.db("NAO", tt)])


def phase_l0_ret(K):
    d = K.dram
    with K.phase():
        wo, bwo = K.sb("wo", [128, 8, 1024], BF16)
        load_w_bf16(K, wo, d["w_out_even"], 8, 0, 1024, bwo)
        tri, btri = K.sb("tri", [128, 8, 128])
        K.dma("sp", tri[:], d["c_tri"][:, :, :], [], [btri])
        col, bcol = K.sb("col", [128, 2])
        K.dma("sp", col[:], d["c_col"][:, :], [], [bcol])
        lg, blg = K.sb("lg", [128, 8])
        K.dma("sp", lg[:], d["ret_decay_logit"].partition_broadcast(128), [], [blg])
        K.act(lg[:], lg[:], AF.Exp, [blg], [blg], scale=-1.0)
        K.act(lg[:], lg[:], AF.Ln, [blg], [blg], bias=1.0)
        K.ts("dve", lg[:], lg[:], -1.0, None, ALU.mult, None, [blg], [blg])
        MT, bMT = K.sb("MT", [128, 512])
        QDf, bQDf = K.sb("QDf", [128, 512], BF16)
        QDb, bQDb = K.sb("QDb", [128, 512], BF16)
        kdec, bkd = K.sb("kdec", [128, 8])
        cdec, bcd = K.sb("cdec", [128, 8])
        e1, be1 = K.sb("e1", [128, 128])
        e2, be2 = K.sb("e2", [128, 128])
        for h in range(4):
            hs = slice(h * 128, (h + 1) * 128)
            K.act(e1[:], tri[:, 0, :], AF.Exp, [btri, blg], [be1], scale=lg[:, h:h + 1])
            K.tt("dve", e1[:], e1[:], tri[:, 1, :], ALU.mult, [be1, btri], [be1])
            K.act(e2[:], tri[:, 2, :], AF.Exp, [btri, blg], [be2], scale=lg[:, 4 + h:5 + h])
            K.tt("dve", e2[:], e2[:], tri[:, 3, :], ALU.mult, [be2, btri], [be2])
            K.tt("dve", MT[:, hs], e1[:], e2[:], ALU.add, [be1, be2], [bMT])
            K.act(QDf[:, hs], tri[:, 4, :], AF.Exp, [btri, blg], [bQDf], scale=lg[:, h:h + 1])
            K.act(QDb[:, hs], tri[:, 5, :], AF.Exp, [btri, blg], [bQDb], scale=lg[:, 4 + h:5 + h])
            K.act(kdec[:, h:h + 1], col[:, 0:1], AF.Exp, [bcol, blg], [bkd], scale=lg[:, h:h + 1])
            K.act(kdec[:, 4 + h:5 + h], col[:, 1:2], AF.Exp, [bcol, blg], [bkd], scale=lg[:, 4 + h:5 + h])
        K.act(cdec[:], lg[:], AF.Exp, [blg], [bcd], scale=128.0)
        SBL, _ = K.sb("SBL", [128, NT, 512], BF16)
        bSB = [Buf("sb%d" % i) for i in range(NT)]
        S, bS = K.sb("S", [128, 512])
        lrot = Rot(K, "ld", [128, 512], BF16, 10)
        kdrot = Rot(K, "kd", [128, 512], BF16, 2)

        def bc4(ap4):
            return ap4.unsqueeze(2).to_broadcast([128, 4, 128])

        def v3(ap):
            return ap.rearrange("p (h d) -> p h d", h=4)

        def state_update(rk, brk, rv, brv, c0):
            kd, bkdd = kdrot.get()
            K.tt("dve", v3(kd[:]), v3(rk[:]), bc4(kdec[:, c0:c0 + 4]), ALU.mult, [brk, bkd], [bkdd])
            ps, bps = K.PS.get()
            for h in range(4):
                hs = slice(h * 128, (h + 1) * 128)
                K.mm(ps[:, hs], kd[:, hs], rv[:, hs], True, True, [bkdd, brv], [bps])
            K.tt("dve", v3(S[:]), v3(S[:]), bc4(cdec[:, c0:c0 + 4]), ALU.mult, [bS, bcd], [bS])
            K.tt("dve", S[:], S[:], ps[:], ALU.add, [bS, bps], [bS])

        def load(nm, tt):
            t, b = lrot.get()
            K.dma("sp", t[:], d[nm][tt * 128:(tt + 1) * 128, :], [K.db(nm, tt)], [b])
            return t, b

        K.memset("dve", S[:], 0.0, [bS])
        for tt in [1, 0] + list(range(NT - 1, NTC - 1, -1)):
            K.cp("act", SBL[:, tt, :], S[:], [bS], [bSB[tt]])
            rk, brk = load("RK", tt)
            rv, brv = load("RV", tt)
            state_update(rk, brk, rv, brv, 4)
        K.memset("dve", S[:], 0.0, [bS])
        sfrot = Rot(K, "sfb", [128, 512], BF16, 2)
        qkrot = Rot(K, "QK", [128, 1024], BF16, 2)
        ptrot = Rot(K, "PTm", [128, 512], BF16, 2)
        qfrot = Rot(K, "Qf", [128, 512], BF16, 4)
        strot = Rot(K, "st6", [128, 4, 6], F32, 2)
        mvrot = Rot(K, "mv", [128, 4, 2], F32, 2)
        rsrot = Rot(K, "rs", [128, 8], F32, 2)
        ynrot = Rot(K, "yn", [128, 512], F32, 2)
        mixrot = Rot(K, "mix", [128, 1024], BF16, 3)
        mtrot = Rot(K, "mixT", [128, 1024], BF16, 2)
        xrot = Rot(K, "xt", [128, D], F32, 3)
        xorot = Rot(K, "xo", [128, D], F32, 2)
        tmrot = Rot(K, "tm", [128, 512], F32, 2)

        def A1(tt):
            c = {"tt": tt}
            c["rq"] = load("RQ", tt)
            c["rk"] = load("RK", tt)
            c["rv"] = load("RV", tt)
            c["rg"] = load("RG", tt)
            mix, bmix = mixrot.get()
            K.dma("sp", mix[:, 512:1024], d["NAO"][tt * 128:(tt + 1) * 128, :], [K.db("NAO", tt)], [bmix])
            c["mix"] = (mix, bmix)
            xt, bx = xrot.get()
            sap, sbf = xsrc(K, "X0", tt)
            K.dma("sp", xt[:], sap, [sbf], [bx])
            c["xt"] = (xt, bx)
            rq, brq = c["rq"]
            rk, brk = c["rk"]
            pt, bpt = K.PS.get()
            ptb = pt[:].bitcast(BF16)
            for h in range(4):
                hs = slice(h * 128, (h + 1) * 128)
                K.tr(ptb[:, hs], rq[:, hs], K.identb[:], [brq, K.b_identb], [bpt])
                K.tr(ptb[:, 512 + h * 128:512 + (h + 1) * 128], rk[:, hs], K.identb[:], [brk, K.b_identb], [bpt])
            QK, bQK = qkrot.get()
            K.cp("act", QK[:], ptb, [bpt], [bQK])
            c["QK"] = (QK, bQK)
            return c

        def A2(c):
            QK, bQK = c["QK"]
            pa, bpa = K.PS.get()
            for h in range(4):
                hs = slice(h * 128, (h + 1) * 128)
                K.mm(pa[:, hs], QK[:, 512 + h * 128:512 + (h + 1) * 128], QK[:, hs], True, True, [bQK], [bpa])
            PTm, bPTm = ptrot.get()
            K.tt("dve", PTm[:], pa[:], MT[:], ALU.mult, [bpa, bMT], [bPTm])
            Qf, bQf = qfrot.get()
            K.tt("dve", Qf[:], QK[:, 0:512], QDf[:], ALU.mult, [bQK, bQDf], [bQf])
            Qb, bQb = qfrot.get()
            K.tt("dve", Qb[:], QK[:, 0:512], QDb[:], ALU.mult, [bQK, bQDb], [bQb])
            c["PTm"] = (PTm, bPTm)
            c["Qf"] = (Qf, bQf)
            c["Qb"] = (Qb, bQb)

        def B1(c, sfb, bsfb):
            tt = c["tt"]
            PTm, bPTm = c["PTm"]
            Qf, bQf = c["Qf"]
            Qb, bQb = c["Qb"]
            rv, brv = c["rv"]
            rg, brg = c["rg"]
            mix, bmix = c["mix"]
            py, bpy = K.PS.get()
            for h in range(4):
                hs = slice(h * 128, (h + 1) * 128)
                K.mm(py[:, hs], PTm[:, hs], rv[:, hs], True, False, [bPTm, brv], [bpy])
                K.mm(py[:, hs], Qf[:, hs], sfb[:, hs], False, False, [bQf, bsfb], [bpy])
                K.mm(py[:, hs], Qb[:, hs], SBL[:, tt, hs], False, True, [bQb, bSB[tt]], [bpy])
            st6, bst6 = strot.get()
            mv, bmv = mvrot.get()
            rs, brs = rsrot.get()
            for h in range(4):
                hs = slice(h * 128, (h + 1) * 128)
                K.cx.op("dve", [bpy], [bst6], lambda e: e.bn_stats(st6[:, h, :], py[:, hs]))
            for h in range(4):
                K.cx.op("dve", [bst6], [bmv], lambda e: e.bn_aggr(mv[:, h, :], st6[:, h, :]))
            K.act(rs[:, 4:8], mv[:, :, 1], AF.Ln, [bmv], [brs], bias=EPS, scale=1.0)
            K.act(rs[:, 0:4], rs[:, 4:8], AF.Exp, [brs], [brs], scale=-0.5)
            yn, byn = ynrot.get()
            for h in range(4):
                hs = slice(h * 128, (h + 1) * 128)
                K.ts("dve", yn[:, hs], py[:, hs], mv[:, h, 0:1], rs[:, h:h + 1], ALU.subtract, ALU.mult,
                     [bpy, bmv, brs], [byn])
            K.tt("dve", mix[:, 0:512], yn[:], rg[:], ALU.mult, [byn, brg], [bmix])

        def B2(c):
            tt = c["tt"]
            s_ = 1 if tt < NTC else 0
            mix, bmix = c["mix"]
            xt, bx = c["xt"]
            pm, bpm = K.PS.get()
            pmb = pm[:].bitcast(BF16)
            for k in range(8):
                K.tr(pmb[:, k * 128:(k + 1) * 128], mix[:, k * 128:(k + 1) * 128], K.identb[:],
                     [bmix, K.b_identb], [bpm])
            mixT, bmT = mtrot.get()
            K.cp("act", mixT[:], pmb, [bpm], [bmT])
            g, bg = K.G[(0, s_, 0)]
            xo, bxo = xorot.get()
            for n in range(2):
                ps, bps = K.PS.get()
                for k in range(8):
                    K.mm(ps[:], mixT[:, k * 128:(k + 1) * 128], wo[:, k, n * 512:(n + 1) * 512], k == 0, k == 7,
                         [bmT, bwo], [bps])
                tm, btm = tmrot.get()
                K.tt("dve", tm[:], ps[:], g[:, n * 512:(n + 1) * 512], ALU.mult, [bps, bg], [btm])
                K.tt("dve", xo[:, n * 512:(n + 1) * 512], tm[:], xt[:, n * 512:(n + 1) * 512], ALU.add,
                     [btm, bx], [bxo])
            dap, dbf = xsrc(K, "X1", tt)
            K.dma("pool", dap, xo[:], [bxo], [dbf])

        cur = A1(0)
        A2(cur)
        sfb, bsfb = sfrot.get()
        K.cp("act", sfb[:], S[:], [bS], [bsfb])
        prevB = None
        for i in range(NT + 1):
            sfb_cur = (sfb, bsfb)
            if cur is not None:
                state_update(cur["rk"][0], cur["rk"][1], cur["rv"][0], cur["rv"][1], 0)
                sfb, bsfb = sfrot.get()
                K.cp("act", sfb[:], S[:], [bS], [bsfb])
            nxt = A1(i + 1) if i + 1 < NT else None
            if cur is not None:
                B1(cur, sfb_cur[0], sfb_cur[1])
            if nxt is not None:
                A2(nxt)
            if prevB is not None:
                B2(prevB)
            prevB = cur
            cur = nxt


L1_SRC = ["X2"]


def ssd_blocks():
    blks = [([0, 1], 0, 0)]
    for b in range(NTL // 4):
        blks.append(([NTC + 4 * b + i for i in range(4)], 262, b * 512))
    return blks


def phase_l1_proj(K):
    d = K.dram
    with K.phase():
        w, bw = K.sb("wi", [128, 8, ODD_IN], BF16)
        bw_dt = load_w_cols(K, w, d["w_in_odd"], 8, 0, [(4608, 5184)])[0]
        bw_x = load_w_cols(K, w, d["w_in_odd"], 8, 0, [(2048 + n_ * 512, 2048 + (n_ + 1) * 512) for n_ in range(5)])
        bw_x.append(bw_dt)
        bw_z = load_w_cols(K, w, d["w_in_odd"], 8, 0, [(n_ * 512, (n_ + 1) * 512) for n_ in range(4)], split_k=True)
        N = NormCtx(K, nx=2, nxn=4)
        hrot = Rot(K, "hT", [128, 8, 512], BF16, 2)
        zrot = Rot(K, "zs", [128, 2048], BF16, 2)
        drot = Rot(K, "dtt", [128, 64], F32, 3)
        xrot = Rot(K, "xbc", [128, 512], BF16, 4)
        dtb, bdtb = K.sb("dtb", [128, 64])
        K.dma("sp", dtb[:], d["dt_bias"].partition_broadcast(128), [], [bdtb])
        zz, bzz = K.sb("zz", [128, 24, 3], BF16)
        K.memset("dve", zz[:], 0.0, [bzz])
        for c0 in (0, 259, 262, 4361):
            K.dma("pool", d["XBCT"][:, :, c0:c0 + 3].rearrange("c p t -> p c t"), zz[:], [bzz], [Buf()])
        blks = ssd_blocks()

        def f_pre(bi):
            return [norm_pre(K, N, *xsrc(K, L1_SRC[0], tt)) for tt in blks[bi][0]]

        def f_tr(bi, pres):
            s_ = 1 if blks[bi][0][0] < NTC else 0
            hT_, bhT_ = hrot.get()
            for i_, p_ in enumerate(pres):
                norm_tr(K, p_, K.A[(1, s_, 0)], hT_, bhT_, i_ * 128)
            return hT_, bhT_

        cur = f_tr(0, f_pre(0))
        for bi, (tiles, off, t0) in enumerate(blks):
            s = 1 if tiles[0] < NTC else 0
            nt = len(tiles) * 128
            hT, bhT = cur
            if bi + 1 < len(blks):
                npre = f_pre(bi + 1)
            for i, tt in enumerate(tiles):
                ts_ = slice(i * 128, (i + 1) * 128)
                if s == 0:
                    zs, bzs = zrot.get()
                    for n in range(4):
                        ps, bps = K.PS.get()
                        for k in range(8):
                            K.mm(ps[:], hT[:, k, ts_], w[:, k, n * 512:(n + 1) * 512], k == 0, k == 7, [bhT, bw_z[n]], [bps])
                        K.act(zs[:, n * 512:(n + 1) * 512], ps[:], AF.Silu, [bps], [bzs])
                    K.dma("pool", d["ZS"][tt * 128:(tt + 1) * 128, :], zs[:], [bzs], [Buf()])
                ps, bps = K.PS.get()
                for k in range(8):
                    K.mm(ps[:, 0:64], hT[:, k, ts_], w[:, k, 5120:5184], k == 0, k == 7, [bhT, bw_dt], [bps])
                dt, bdt = drot.get()
                K.tt("dve", dt[:], ps[:, 0:64], dtb[:], ALU.add, [bps, bdtb], [bdt])
                K.act(dt[:], dt[:], AF.Exp, [bdt], [bdt])
                K.act(dt[:], dt[:], AF.Ln, [bdt], [bdt], bias=1.0)
                K.dma("pool", d["DT"][tt * 128:(tt + 1) * 128, :], dt[:], [bdt], [Buf()])
            for ch in range(24):
                if ch == 16 and bi + 1 < len(blks):
                    cur = f_tr(bi + 1, npre)
                ps, bps = K.PS.get()
                for k in range(8):
                    K.mm(ps[:, 0:nt], w[:, k, 2048 + ch * 128:2048 + (ch + 1) * 128], hT[:, k, 0:nt], k == 0, k == 7,
                         [bw_x[ch // 4], bhT], [bps])
                xb, bxb = xrot.get()
                K.cp("act" if ch % 2 == 0 else "dve", xb[:, 0:nt], ps[:, 0:nt], [bps], [bxb])
                c0 = off + 3 + t0
                K.dma("pool", d["XBCT"][ch, :, c0:c0 + nt], xb[:, 0:nt], [bxb], [Buf()])


def phase_l1_conv(K):
    d = K.dram
    with K.phase():
        dW, _ = K.sb("diagW", [128, 24 * 7, 128], BF16)
        bdWs = [Buf("dW%d" % ch) for ch in range(24)]
        built = set()

        def build_dw(ch):
            if ch in built:
                return
            built.add(ch)
            for k in range(7):
                cw, bcw = K.cols["conv_w%d" % k]
                K.ts("dve", dW[:, ch * 7 + k, :], K.ident[:], cw[:, ch:ch + 1], None, ALU.mult, None,
                     [K.b_ident, bcw], [bdWs[ch]])
        cb, bcb = K.cols["conv_b"]
        xirot = Rot(K, "xin", [128, 24, 518], BF16, 2)
        xarot = Rot(K, "xact", [128, 24, 512], BF16, 2)
        xsrot = Rot(K, "xs", [128, 2048], BF16, 2)
        bsrot = Rot(K, "bs", [128, 512], BF16, 2)
        for tiles, off, t0 in ssd_blocks():
            nt = len(tiles) * 128
            xin, bxin = xirot.get()
            K.dma("sp", xin[:, :, 0:nt + 6], d["XBCT"][:, :, off + t0:off + t0 + nt + 6].rearrange("c p t -> p c t"),
                  [], [bxin])
            xa, bxa = xarot.get()
            for ch in range(24):
                build_dw(ch)
                if ch + 1 < 24:
                    build_dw(ch + 1)
                ps, bps = K.PS.get()
                for k in range(7):
                    K.mm(ps[:, 0:nt], dW[:, ch * 7 + k, :], xin[:, ch, k:k + nt], k == 0, k == 6, [bdWs[ch], bxin], [bps])
                K.act(xa[:, ch, 0:nt], ps[:, 0:nt], AF.Silu, [bps, bcb], [bxa], bias=cb[:, ch:ch + 1])
            tok0 = tiles[0] * 128
            K.dma("pool", d["BT"][:, :, tok0:tok0 + nt].rearrange("g p t -> p g t"), xa[:, 16:20, 0:nt], [bxa], [Buf()])
            K.dma("pool", d["CTt"][:, :, tok0:tok0 + nt].rearrange("g p t -> p g t"), xa[:, 20:24, 0:nt], [bxa], [Buf()])
            for i, tt in enumerate(tiles):
                ts_ = slice(i * 128, (i + 1) * 128)
                xs, bxs = xsrot.get()
                for half in range(2):
                    pt, bpt = K.PS.get()
                    ptb = pt[:].bitcast(BF16)
                    for c in range(8):
                        K.tr(ptb[:, c * 128:(c + 1) * 128], xa[:, half * 8 + c, ts_], K.identb[:], [bxa, K.b_identb], [bpt])
                    K.cp("act" if half == 0 else "dve", xs[:, half * 1024:(half + 1) * 1024], ptb, [bpt], [bxs])
                K.dma("pool", d["XS"][tt * 128:(tt + 1) * 128, :], xs[:], [bxs], [Buf()])
                bs, bbs = bsrot.get()
                pt, bpt = K.PS.get()
                ptb = pt[:, 0:256].bitcast(BF16)
                for g in range(4):
                    K.tr(ptb[:, g * 128:(g + 1) * 128], xa[:, 16 + g, ts_], K.identb[:], [bxa, K.b_identb], [bpt])
                K.cp("dve", bs[:], ptb, [bpt], [bbs])
                K.dma("pool", d["BS"][tt * 128:(tt + 1) * 128, :], bs[:], [bbs], [Buf()])


def phase_l1_scan(K):
    d = K.dram
    with K.phase():
        wo, bwo = K.sb("wo1", [128, 16, D], BF16)
        load_w_bf16(K, wo, d["w_out_odd"], 16, 0, D, bwo)
        tri, btri = K.sb("tri", [128, 8, 128])
        K.dma("sp", tri[:], d["c_tri"][:, :, :], [], [btri])
        abc, babc = K.sb("abc", [128, 64])
        K.dma("sp", abc[:], d["a_log"].partition_broadcast(128), [], [babc])
        K.act(abc[:], abc[:], AF.Exp, [babc], [babc])
        K.ts("dve", abc[:], abc[:], -1.0, None, ALU.mult, None, [babc], [babc])
        dsk, bdsk = K.sb("dsk", [128, 32])
        K.dma("sp", dsk[:], d["d_skip"].partition_broadcast(128), [], [bdsk])
        nw, bnw = K.cols["ssm_norm"]

        def subrot(lo, hi):
            r = Rot.__new__(Rot)
            r.items = K.PS.items[lo:hi]
            r.i = 0
            return r
        pdrot = subrot(0, 2)
        pyirot = subrot(2, 5)
        genps = subrot(5, 8)
        Tb, bTb = K.sb("Tb", [128, 2, 128], BF16)
        K.cp("dve", Tb[:, 0, :], tri[:, 1, :], [btri], [bTb])
        K.cp("dve", Tb[:, 1, :], tri[:, 3, :], [btri], [bTb])
        NEGr, bNEGr = K.sb("NEGr", [128, 2, 4, 128], BF16)
        for dd in range(2):
            K.cp("dve", NEGr[:, dd, :, :], tri[:, 6 + dd, :].unsqueeze(1).to_broadcast([128, 4, 128]), [btri], [bNEGr])
        HT, bHT = K.sb("HT", [128, 2048])
        HTb, bHTb = K.sb("HTb", [128, 2048], BF16)
        xsrot = Rot(K, "xs", [128, 2048], BF16, 3)
        bsrot = Rot(K, "bs", [128, 512], BF16, 3)
        btrot = Rot(K, "bt", [128, 4, 128], BF16, 3)
        ctrot = Rot(K, "ct", [128, 4, 128], BF16, 3)
        dtrot = Rot(K, "dt", [128, 64], F32, 3)
        ybrot = Rot(K, "yb", [128, 2048], BF16, 3)
        zsrot = Rot(K, "zs", [128, 2048], BF16, 3)
        xtrot = Rot(K, "xt", [128, D], F32, 3)
        smrot = Rot(K, "sm", [128, 8, 32], F32, 3)
        dhlrot = Rot(K, "dhl", [128, 2, 32], BF16, 3)
        wbrot = Rot(K, "wendb", [128, 32], BF16, 3)
        LTrot = Rot(K, "LT", [128, 512], BF16, 3)
        Wrot = Rot(K, "Wt", [128, 512], BF16, 5)
        xqrot = Rot(K, "xq", [128, 2048], BF16, 1)
        cburot = Rot(K, "cbu", [128, 512], BF16, 3)
        xprot2 = None
        bigrot = Rot(K, "big", [128, 2048], BF16, 4)
        t1rot = Rot(K, "t1big", [128, 2048], BF16, 2)
        tmrot = Rot(K, "tm", [128, 512], F32, 2)
        tmbrot = Rot(K, "tmb", [128, 512], BF16, 3)
        ynrot = Rot(K, "yn", [128, 2048], BF16, 2)
        yTrot = Rot(K, "ynT", [128, 2048], BF16, 1)
        xorot = Rot(K, "xo", [128, D], F32, 1)
        strot = Rot(K, "fst", [128, 4], F32, 2)

        def bch(ap, n, m):
            return ap.unsqueeze(2).to_broadcast([128, n, m])

        phrot = Rot(K, "phs", [128, 2048], BF16, 3)

        def pre1(dirn, tt, defer=False):
            Tm = tri[:, 1, :] if dirn == 0 else tri[:, 3, :]
            NEGm = tri[:, 6, :] if dirn == 0 else tri[:, 7, :]
            dc = dirn * 32
            lat = tt >= NTC
            c = {"tt": tt, "lat": lat, "dirn": dirn}
            xs, bxs = xsrot.get()
            K.dma("sp", xs[:], d["XS"][tt * 128:(tt + 1) * 128, :], [], [bxs])
            bs, bbs = bsrot.get()
            K.dma("sp", bs[:], d["BS"][tt * 128:(tt + 1) * 128, :], [], [bbs])
            dt, bdt = dtrot.get()
            K.dma("sp", dt[:], d["DT"][tt * 128:(tt + 1) * 128, :], [], [bdt])
            if lat:
                bt, bbt = btrot.get()
                K.dma("sp", bt[:], d["BT"][:, :, tt * 128:(tt + 1) * 128].rearrange("g p t -> p g t"), [], [bbt])
                ct, bct = ctrot.get()
                K.dma("sp", ct[:], d["CTt"][:, :, tt * 128:(tt + 1) * 128].rearrange("g p t -> p g t"), [], [bct])
                c["ct"] = (ct, bct)
                if dirn == 0:
                    yb, byb = ybrot.get()
                    K.dma("sp", yb[:], d["YB"][tt * 128:(tt + 1) * 128, :], [], [byb])
                    zs, bzs = zsrot.get()
                    K.dma("sp", zs[:], d["ZS"][tt * 128:(tt + 1) * 128, :], [], [bzs])
                    c["fin_in"] = (yb, byb, zs, bzs)
            c["xs"] = (xs, bxs)
            sm, bsm = smrot.get()
            dta, acol, eac, ela, wend, stmp, lndt, ebias = (sm[:, i, :] for i in range(8))
            c["sm"] = (sm, bsm)
            K.tt("dve", dta, dt[:, dc:dc + 32], abc[:, dc:dc + 32], ALU.mult, [bdt, babc], [bsm])
            dhl, bdhl = dhlrot.get()
            c["dhl"] = (dhl, bdhl)
            K.cp("dve", dhl[:, 0, :], dta, [bsm], [bdhl])
            K.tt("dve", dhl[:, 1, :], dta, dhl[:, 0, :], ALU.subtract, [bsm, bdhl], [bdhl])
            K.act(lndt, dt[:, dc:dc + 32], AF.Ln, [bdt], [bsm])

            return _pre1_rest(c, dirn, tt, Tm, dc, lat, xs, bxs, bs, bbs, dt, bdt, sm, bsm, defer,
                              (bt, bbt, ct, bct) if lat else None)

        def _pre1_rest(c, dirn, tt, Tm, dc, lat, xs, bxs, bs, bbs, dt, bdt, sm, bsm, defer, btct):
            dta, acol, eac, ela, wend, stmp, lndt, ebias = (sm[:, i, :] for i in range(8))
            if lat:
                bt, bbt, ct, bct = btct
            st_ = {}

            def part_b():
                pc, bpc = genps.get()
                K.mm(pc[:, 0:32], Tm, dta, True, True, [btri, bsm], [bpc])
                K.mm(pc[:, 32:64], K.ones[:], dta, True, True, [K.b_ones, bsm], [bpc])
                K.cp("dve", acol, pc[:, 0:32], [bpc], [bsm])
                K.tt("dve", ebias, lndt, acol, ALU.subtract, [bsm], [bsm])
                K.act(eac, pc[:, 0:32], AF.Exp, [bpc], [bsm])
                K.act(ela, pc[:, 32:64], AF.Exp, [bpc], [bsm])
                K.tt("dve", stmp, pc[:, 32:64], acol, ALU.subtract, [bpc, bsm], [bsm])
                K.act(stmp, stmp, AF.Exp, [bsm], [bsm])
                wendb, bwendb = wbrot.get()
                K.tt("dve", wendb[:], stmp, dt[:, dc:dc + 32], ALU.mult, [bsm, bdt], [bwendb])
                xq, bxq = xqrot.get()
                K.tt("dve", xq[:].rearrange("p (h e) -> p h e", h=32), xs[:].rearrange("p (h e) -> p h e", h=32),
                     bch(wendb[:], 32, 64), ALU.mult, [bxs, bwendb], [bxq])
                st_["xq"] = (xq, bxq)

            def part_c():
                xq, bxq = st_["xq"]
                phs, bphs = phrot.get()
                c["phs"] = (phs, bphs)
                for g in range(4):
                    ph, bph = genps.get()
                    K.mm(ph[:], bs[:, g * 128:(g + 1) * 128], xq[:, g * 512:(g + 1) * 512], True, True, [bbs, bxq], [bph])
                    K.cp("act", phs[:, g * 512:(g + 1) * 512], ph[:], [bph], [bphs])
                if lat:
                    pcb, bpcb = genps.get()
                    for g in range(4):
                        K.mm(pcb[:, g * 128:(g + 1) * 128], bt[:, g, :], ct[:, g, :], True, True, [bbt, bct], [bpcb])
                    cbu, bcbu = cburot.get()
                    K.tt("dve", cbu[:].rearrange("p (g i) -> p g i", g=4), pcb[:].rearrange("p (g i) -> p g i", g=4),
                         Tm.unsqueeze(1).to_broadcast([128, 4, 128]), ALU.mult, [bpcb, btri], [bcbu])
                    c["cbu"] = (cbu, bcbu)
                    c["yd"] = bigrot.get()

            if defer:
                c["later"] = [part_b, part_c]
            else:
                part_b()
                part_c()
            return c

        def pre2(c, hooks=None):
            hooks = hooks or {}
            if not c["lat"]:
                for k_ in sorted(hooks):
                    hooks[k_]()
                return
            dirn = c["dirn"]
            sm, bsm = c["sm"]
            ebias = sm[:, 7, :]
            dhl, bdhl = c["dhl"]
            xs, bxs = c["xs"]
            cbu, bcbu = c["cbu"]
            yd, byd = c["yd"]
            pend = []

            def issue_yi(p):
                g_, hb_, Wt_, bWt_, pyi_, bpyi_ = p
                for h4 in range(4):
                    h8 = hb_ * 4 + h4
                    hh = g_ * 8 + h8
                    K.mm(pyi_[:, h8 * 64:(h8 + 1) * 64], Wt_[:, h4 * 128:(h4 + 1) * 128],
                         xs[:, hh * 64:(hh + 1) * 64], True, True, [bWt_, bxs], [bpyi_])
                if hb_ == 1:
                    K.cp("act", yd[:, g_ * 512:(g_ + 1) * 512], pyi_[:], [bpyi_], [byd])

            for g in range(4):
                pyi, bpyi = pyirot.get()
                for hb in range(2):
                    if (g * 2 + hb) in hooks:
                        hooks[g * 2 + hb]()
                    pd, bpd = pdrot.get()
                    K.mm(pd[:], K.identb[:], NEGr[:, dirn, :, :].rearrange("p h i -> p (h i)"), True, False,
                         [K.b_identb, bNEGr], [bpd])
                    for h4 in range(4):
                        hh = g * 8 + hb * 4 + h4
                        K.mm(pd[:, h4 * 128:(h4 + 1) * 128], dhl[:, 0, hh:hh + 1].to_broadcast([128, 128]), Tb[:, dirn, :],
                             False, False, [bdhl, bTb], [bpd])
                        K.mm(pd[:, h4 * 128:(h4 + 1) * 128], dhl[:, 1, hh:hh + 1].to_broadcast([128, 128]), Tb[:, dirn, :],
                             False, h4 == 3, [bdhl, bTb], [bpd])
                    LTt, bLT = LTrot.get()
                    for h4 in range(4):
                        hh = g * 8 + hb * 4 + h4
                        K.act(LTt[:, h4 * 128:(h4 + 1) * 128], pd[:, h4 * 128:(h4 + 1) * 128], AF.Exp, [bpd, bsm], [bLT],
                              bias=ebias[:, hh:hh + 1])
                    Wt, bWt = Wrot.get()
                    K.tt("dve", Wt[:].rearrange("p (h i) -> p h i", h=4), LTt[:].rearrange("p (h i) -> p h i", h=4),
                         cbu[:, g * 128:(g + 1) * 128].unsqueeze(1).to_broadcast([128, 4, 128]), ALU.mult,
                         [bLT, bcbu], [bWt])
                    pend.append((g, hb, Wt, bWt, pyi, bpyi))
                    if len(pend) > 2:
                        issue_yi(pend.pop(0))
            while pend:
                issue_yi(pend.pop(0))

        def post_hooks(c):
            dirn = c["dirn"]
            tt = c["tt"]
            sm, bsm = c["sm"]
            eac = sm[:, 2, :]
            ela = sm[:, 3, :]
            phs, bphs = c["phs"]
            H = {}

            def h_mult():
                K.tt("dve", HT[:].rearrange("p (h e) -> p h e", h=32), HT[:].rearrange("p (h e) -> p h e", h=32),
                     bch(ela, 32, 64), ALU.mult, [bHT, bsm], [bHT])

            def h_add():
                K.tt("dve", HT[:], HT[:], phs[:], ALU.add, [bHT, bphs], [bHT])

            def h_cast():
                K.cp("dve", HTb[:], HT[:], [bHT], [bHTb])

            def mk_pye(g):
                def f():
                    ct, bct = c["ct"]
                    yd, byd = c["yd"]
                    pye, bpye = genps.get()
                    K.mm(pye[:], ct[:, g, :], HTb[:, g * 512:(g + 1) * 512], True, True, [bct, bHTb], [bpye])
                    tm, btm = tmbrot.get()
                    K.tt("dve", tm[:].rearrange("p (h e) -> p h e", h=8), pye[:].rearrange("p (h e) -> p h e", h=8),
                         bch(eac[:, g * 8:(g + 1) * 8], 8, 64), ALU.mult, [bpye, bsm], [btm])
                    K.tt("dve", yd[:, g * 512:(g + 1) * 512], yd[:, g * 512:(g + 1) * 512], tm[:], ALU.add,
                         [byd, btm], [byd])
                return f

            def seq(*fs):
                def f():
                    for x in fs:
                        x()
                return f

            if not c["lat"]:
                H[0] = seq(h_mult, h_add, h_cast)
                return H, None
            fin = None
            if dirn == 1:
                def y_dsk():
                    xs, bxs = c["xs"]
                    t1, bt1 = t1rot.get()
                    c["t1"] = (t1, bt1)
                    K.tt("dve", t1[:].rearrange("p (h e) -> p h e", h=32), xs[:].rearrange("p (h e) -> p h e", h=32),
                         bch(dsk[:], 32, 64), ALU.mult, [bxs, bdsk], [bt1])

                def y_store():
                    yd, byd = c["yd"]
                    t1, bt1 = c["t1"]
                    yb, byb = ybrot.get()
                    K.tt("dve", yb[:], t1[:], yd[:], ALU.add, [bt1, byd], [byb])
                    K.dma("pool", d["YB"][tt * 128:(tt + 1) * 128, :], yb[:], [byb], [Buf()])
                H[0] = seq(h_mult, mk_pye(0))
                H[1] = seq(mk_pye(1), h_add)
                H[2] = seq(mk_pye(2), y_dsk)
                H[3] = mk_pye(3)
                H[4] = h_cast
                H[6] = y_store
            else:
                fin = c
                H[0] = seq(h_mult, mk_pye(0))
                H[1] = seq(mk_pye(1), h_add)
                H[2] = mk_pye(2)
                H[3] = mk_pye(3)
                H[4] = h_cast
            return H, fin

        def fin_chain(c):
            xs, bxs = c["xs"]
            yd, byd = c["yd"]
            yb, byb, zs, bzs = c["fin_in"]
            xt, bx = xtrot.get()
            sap, sbf = xsrc(K, L1_SRC[0], c["tt"])
            K.dma("sp", xt[:], sap, [sbf], [bx])
            t1, bt1 = t1rot.get()
            K.tt("dve", t1[:], yd[:], yb[:], ALU.add, [byd, byb], [bt1])
            K.tt("dve", t1[:], t1[:], zs[:], ALU.mult, [bt1, bzs], [bt1])
            yn, byn = ynrot.get()
            st, bst = strot.get()
            K.cx.op("dve", [bt1], [byn, bst], lambda e: e.scalar_tensor_tensor(yn[:], t1[:], 1.0, t1[:], ALU.mult, ALU.mult,
                                                                             accum_out=st[:, 0:1]))
            rstd_op(K, st[:, 2:3], st[:, 1:2], st[:, 0:1], 1.0 / 2048, bst)
            K.ts("dve", yn[:], t1[:], st[:, 2:3], None, ALU.mult, None, [bt1, bst], [byn])
            return {"tt": c["tt"], "yn": (yn, byn), "xt": (xt, bx)}

        def fin_tr(f):
            yn, byn = f["yn"]
            ynT, byT = yTrot.get()
            f["ynT"] = (ynT, byT)
            for half in range(2):
                pt, bpt = genps.get()
                ptb = pt[:].bitcast(BF16)
                for c8 in range(8):
                    k = half * 8 + c8
                    K.tr(ptb[:, c8 * 128:(c8 + 1) * 128], yn[:, k * 128:(k + 1) * 128], K.identb[:], [byn, K.b_identb], [bpt])
                K.cp("act", ynT[:, half * 1024:(half + 1) * 1024], ptb, [bpt], [byT])

        def fin_out(f):
            tt = f["tt"]
            ynT, byT = f["ynT"]
            xt, bx = f["xt"]
            g, bg = K.G[(1, 0, 0)]
            xo, bxo = xorot.get()
            for n in range(2):
                ps, bps = genps.get()
                for k in range(16):
                    K.mm(ps[:], ynT[:, k * 128:(k + 1) * 128], wo[:, k, n * 512:(n + 1) * 512], k == 0, k == 15,
                         [byT, bwo], [bps])
                tm, btm = tmrot.get()
                K.tt("dve", tm[:], ps[:], g[:, n * 512:(n + 1) * 512], ALU.mult, [bps, bg], [btm])
                K.tt("dve", xo[:, n * 512:(n + 1) * 512], tm[:], xt[:, n * 512:(n + 1) * 512], ALU.add, [btm, bx], [bxo])
            dap, dbf = xsrc(K, "X3", tt)
            K.dma("pool", dap, xo[:], [bxo], [dbf])

        def sweep(dirn):
            K.memset("dve", HT[:], 0.0, [bHT])
            K.memset("pool", HTb[:], 0.0, [bHTb])
            order = list(range(NT)) if dirn == 0 else [1, 0] + list(range(NT - 1, NTC - 1, -1))
            p1 = [pre1(dirn, order[0]), pre1(dirn, order[1])]
            cur = p1.pop(0)
            pre2(cur)
            f_tr = None
            f_out = None
            for i in range(len(order) + 2):
                if f_out is not None:
                    fin_out(f_out)
                    f_out = None
                nxt = p1.pop(0) if p1 else None
                hooks, fc = post_hooks(cur) if cur is not None else ({}, None)
                if i + 2 < len(order):
                    holder = {}

                    def chain(prev, fn):
                        def f():
                            if prev is not None:
                                prev()
                            fn()
                        return f

                    def ha(tt2=order[i + 2]):
                        holder["c"] = pre1(dirn, tt2, defer=True)
                        p1.append(holder["c"])
                    hooks[2] = chain(hooks.get(2), ha)
                    hooks[4] = chain(hooks.get(4), lambda: holder["c"]["later"][0]())
                    hooks[6] = chain(hooks.get(6), lambda: holder["c"]["later"][1]())
                if nxt is not None:
                    pre2(nxt, hooks)
                else:
                    for k_ in sorted(hooks):
                        hooks[k_]()
                if f_tr is not None:
                    fin_tr(f_tr)
                    f_out = f_tr
                    f_tr = None
                if fc is not None:
                    f_tr = fin_chain(fc)
                cur = nxt

        sweep(1)
        K.cx.barrier()
        for k in range(16):
            K.ts("dve", wo[:, k, :], wo[:, k, :], nw[:, k:k + 1], None, ALU.mult, None, [bwo, bnw], [bwo])
        sweep(0)


def host_constants():
    C = {}
    C["c_ident"] = np.eye(128, dtype=np.float32)
    nf = 32
    inv = (10000.0 ** (-np.arange(nf, dtype=np.float32) / nf)).astype(np.float32)
    pos = np.arange(T)
    rpos = (pos // 64).astype(np.float32)
    cpos = (pos % 64).astype(np.float32)
    ang_r = (rpos[:, None] * inv[None, :]).astype(np.float32)
    ang_c = (cpos[:, None] * inv[None, :]).astype(np.float32)
    cr, sr, cc, sc = np.cos(ang_r), np.sin(ang_r), np.cos(ang_c), np.sin(ang_c)
    Cq = np.concatenate([cr, cr, cc, cc], axis=1).astype(np.float32)
    Sq = np.concatenate([-sr, sr, -sc, sc], axis=1).astype(np.float32)
    ks = np.float32(128 ** -0.5)
    rope = np.stack([Cq, Sq, Cq * ks, Sq * ks], axis=1)
    C["c_rope"] = np.ascontiguousarray(rope.reshape(NTL, 128, 4, 128)).astype(np.float32)
    j = np.arange(128)[:, None].astype(np.float32)
    i = np.arange(128)[None, :].astype(np.float32)
    tri = np.zeros((128, 8, 128), np.float32)
    tri[:, 0] = np.maximum(i - j, 0)
    tri[:, 1] = (i >= j)
    tri[:, 2] = np.maximum(j - i, 0)
    tri[:, 3] = (j >= i)
    tri[:, 4] = i + 1 + 0 * j
    tri[:, 5] = 128 - i + 0 * j
    tri[:, 6] = np.where(i >= j, 0.0, NEG)
    tri[:, 7] = np.where(j >= i, 0.0, NEG)
    C["c_tri"] = tri
    col = np.zeros((128, 2), np.float32)
    col[:, 0] = 127 - np.arange(128)
    col[:, 1] = np.arange(128)
    C["c_col"] = col
    return C


def na_bias_tiles(rpb):
    kl = (np.arange(128) // 64)[:, None]
    kc = (np.arange(128) % 64)[:, None]
    ql = (np.arange(128) // 64)[None, :]
    qc = (np.arange(128) % 64)[None, :]
    cstart = np.clip(qc - 8, 0, 48)
    colok = (kc >= cstart) & (kc < cstart + 16)
    variants = [(-3, False), (-2, False), (-2, True), (-1, False), (0, False), (1, False), (2, True),
                (2, False), (3, False)]
    out = np.full((128, 8, 9, 128), NEG, np.float32)
    for vi, (dt, interior) in enumerate(variants):
        dr = 2 * dt + kl - ql
        ok = colok & (dr >= -7) & (dr <= 7)
        if interior:
            ok = ok & (dr >= -4) & (dr <= 3)
        dri = np.clip(dr + 7, 0, 14)
        dci = np.clip(kc - qc + 15, 0, 30)
        dri, dci = np.broadcast_arrays(dri, dci)
        for h in range(8):
            g = rpb[h][dri, dci]
            out[:, h, vi, :] = np.where(ok, g, np.float32(NEG))
    return out


def make_in_maps(inputs):
    I = {k: np.asarray(v) for k, v in inputs.items()}
    C = host_constants()
    shared = dict(C)
    shared["c_ctx"] = I["c_ctx"].reshape(8, 128)
    shared["w_mod"] = I["w_mod"]
    shared["b_mod"] = I["b_mod"].reshape(2, 48, 128)
    shared["norm_mix"] = I["norm_mix"].reshape(2, 8, 128)
    shared["norm_mlp"] = I["norm_mlp"].reshape(2, 8, 128)
    def pkn(a, nk):
        return np.ascontiguousarray(a.reshape(nk, 128, a.shape[-1]).transpose(1, 0, 2))
    shared["w_mlp_in"] = np.stack([pkn(I["w_mlp_in"][l], 8) for l in range(2)])
    shared["w_mlp_out"] = np.stack([pkn(I["w_mlp_out"][l], 32) for l in range(2)])
    shared["w_in_even"] = pkn(I["w_in_even"][0], 8)
    shared["w_out_even"] = pkn(I["w_out_even"][0], 8)
    shared["ret_decay_logit"] = I["ret_decay_logit"].reshape(8)
    shared["na_bias"] = na_bias_tiles(I["na_rpb"][0])
    shared["w_in_odd"] = pkn(I["w_in_odd"][0], 8)
    shared["conv_w"] = I["conv_w"][0].reshape(7, 24, 128)
    shared["conv_b"] = I["conv_b"][0].reshape(24, 128)
    shared["dt_bias"] = I["dt_bias"].reshape(64)
    shared["a_log"] = I["a_log"].reshape(64)
    shared["d_skip"] = I["d_skip"].reshape(32)
    shared["ssm_norm"] = I["ssm_norm"].reshape(16, 128)
    shared["w_out_odd"] = pkn(I["w_out_odd"][0], 16)
    shared["norm_final"] = I["norm_final"].reshape(D)
    shared = {k: np.ascontiguousarray(v, dtype=np.float32) for k, v in shared.items()}
    maps = []
    for b in range(8):
        m = dict(shared)
        m["x"] = np.ascontiguousarray(I["x"][b], dtype=np.float32)
        m["ctx"] = np.ascontiguousarray(I["ctx"][b], dtype=np.float32)
        m["c"] = np.ascontiguousarray(I["c"][b].reshape(8, 128), dtype=np.float32)
        maps.append(m)
    return maps


def build(dbg=False):
    K = Kern(dbg)
    setup(K)
    phase_mod(K)
    phase_l0_ret_proj(K)
    phase_l0_na(K)
    phase_l0_ret(K)
    phase_mlp(K, 0, "X1", "X2", list(range(NT)))
    phase_l1_proj(K)
    phase_l1_conv(K)
    phase_l1_scan(K)
    phase_mlp(K, 1, "X3", "Y", list(range(NTC, NT)), final=True)
    K.cx.barrier()
    return K


_CACHE = {}


def kernel(**inputs):
    if "K" not in _CACHE:
        _CACHE["K"] = build()
    K = _CACHE["K"]
    maps = make_in_maps(inputs)
    used = set(K.dram.keys())
    maps = [{k: v for k, v in m.items() if k in used} for m in maps]
    res = run_bass_kernel_spmd(K.nc, maps, core_ids=list(range(8)))
    return np.stack([np.asarray(r["y"], dtype=np.float32) for r in res.results], axis=0)
```

```python
import numpy as np
import ml_dtypes
import concourse.bass as bass
import concourse.mybir as mybir
from concourse.bass_utils import run_bass_kernel_spmd

F32 = mybir.dt.float32
BF16 = mybir.dt.bfloat16
AF = mybir.ActivationFunctionType
ALU = mybir.AluOpType
AX = mybir.AxisListType


class Buf:
    __slots__ = ("name", "w", "r")

    def __init__(self, name=""):
        self.name = name
        self.w = None
        self.r = {}


class Ctx:
    ENG = ("pe", "act", "dve", "pool", "sp")

    def __init__(self, nc, n_dma_sems=36):
        self.nc = nc
        self.eng = {"pe": nc.tensor, "act": nc.scalar, "dve": nc.vector,
                    "pool": nc.gpsimd, "sp": nc.sync}
        self.sem = {}
        self.cnt = {}
        for e in self.ENG:
            self.sem[e] = nc.alloc_semaphore("sem_" + e)
            self.cnt[e] = 0
        self.dma_sems = []
        for i in range(n_dma_sems):
            k = "dma%d" % i
            self.sem[k] = nc.alloc_semaphore(k)
            self.cnt[k] = 0
            self.dma_sems.append(k)
        self.dma_rr = 0
        self.grp_sems = []
        for i in range(20):
            k = "grp%d" % i
            self.sem[k] = nc.alloc_semaphore(k)
            self.cnt[k] = 0
            self.grp_sems.append(k)
        self.grp_rr = 0
        self.seen = {e: {} for e in self.ENG}
        self.nwait = 0
        self.nins = 0

    def _wait(self, e, key, val):
        if val <= 0:
            return
        if self.seen[e].get(key, 0) >= val:
            return
        self.eng[e].wait_ge(self.sem[key], val)
        self.seen[e][key] = val
        self.nwait += 1

    def _deps(self, e, reads, writes):
        deps = {}

        def add(k, v):
            if deps.get(k, 0) < v:
                deps[k] = v
        for b in reads:
            if b.w is not None:
                add(*b.w)
        for b in writes:
            if b.w is not None:
                if not (b.w[0] == e):
                    add(*b.w)
            for k, v in b.r.items():
                if k != e:
                    add(k, v)
        if e == "pe":
            deps.pop("pe", None)
        for k, v in deps.items():
            self._wait(e, k, v)

    def _mark(self, key, val, reads, writes):
        for b in reads:
            if b.r.get(key, 0) < val:
                b.r[key] = val
        for b in writes:
            b.w = (key, val)
            b.r = {}

    def op(self, e, reads, writes, fn, inc=True):
        self._deps(e, reads, writes)
        ins = fn(self.eng[e])
        if inc:
            self.cnt[e] += 1
            ins.then_inc(self.sem[e], 1)
            self._mark(e, self.cnt[e], reads, writes)
        else:
            self._mark(e, self.cnt[e] + 1, reads, writes)
        self.nins += 1
        return ins

    def new_group(self):
        k = self.grp_sems[self.grp_rr]
        self.grp_rr = (self.grp_rr + 1) % len(self.grp_sems)
        return k

    def dma(self, q, out_ap, in_ap, reads, writes, group=None, **kw):
        if group is not None:
            k = group
        else:
            k = self.dma_sems[self.dma_rr]
            self.dma_rr = (self.dma_rr + 1) % len(self.dma_sems)
            self._wait(q, k, self.cnt[k])
        self._deps(q, reads, writes)
        ins = self.eng[q].dma_start(out=out_ap, in_=in_ap, **kw)
        self.cnt[k] += 16
        ins.then_inc(self.sem[k], 16)
        self._mark(k, self.cnt[k], reads, writes)
        self.nins += 1
        return ins

    def barrier(self):
        for e in self.ENG:
            for k in list(self.cnt.keys()):
                if k == e and e == "pe":
                    continue
                self._wait(e, k, self.cnt[k])

    def final_wait(self, bufs):
        for b in bufs:
            if b.w is not None:
                self._wait("sp", *b.w)


D = 1024
T = 4096
CTX = 256
NTL = T // 128
NTC = CTX // 128
NT = NTL + NTC
EPS = 1e-6
NEG = -30000.0
EVEN_IN = 3584
ODD_IN = 5184
XLT = 262 + 4102


class Rot:
    def __init__(self, K, name, shape, dt, n, psum=False):
        self.items = []
        for i in range(n):
            nm = "%s_%d_%d" % (name, K.uid(), i)
            if psum:
                t = K.stack.enter_context(K.nc.psum_tensor(nm, shape, dt))
            else:
                t = K.stack.enter_context(K.nc.sbuf_tensor(nm, shape, dt))
            self.items.append((t, Buf(nm)))
        self.i = 0

    def get(self):
        it = self.items[self.i]
        self.i = (self.i + 1) % len(self.items)
        return it


class Kern:
    def __init__(self, dbg=False, stages=None):
        from contextlib import ExitStack
        self.dbg = dbg
        self.stages = stages
        self.nc = bass.Bass("TRN2", target_bir_lowering=False)
        self.cx = Ctx(self.nc)
        self._uid = 0
        self.gstack = ExitStack()
        self.stack = self.gstack
        self.dram = {}
        self.dbufs = {}
        self.outputs = []

    def uid(self):
        self._uid += 1
        return self._uid

    def din(self, name, shape, dt=F32):
        self.dram[name] = self.nc.dram_tensor(name, list(shape), dt, kind="ExternalInput").ap()
        return self.dram[name]

    def dscr(self, name, shape, dt, out=False):
        kind = "ExternalOutput" if (out or self.dbg) else "Internal"
        self.dram[name] = self.nc.dram_tensor(name, list(shape), dt, kind=kind).ap()
        if kind == "ExternalOutput":
            self.outputs.append(name)
        return self.dram[name]

    def db(self, name, idx=0):
        k = (name, idx)
        if k not in self.dbufs:
            self.dbufs[k] = Buf("%s_%s" % (name, idx))
        return self.dbufs[k]

    def sb(self, name, shape, dt=F32):
        t = self.stack.enter_context(self.nc.sbuf_tensor("%s_%d" % (name, self.uid()), list(shape), dt))
        return t, Buf(name)

    def phase(self):
        from contextlib import ExitStack, contextmanager
        K = self

        @contextmanager
        def cm():
            K.cx.barrier()
            old = K.stack
            with ExitStack() as st:
                K.stack = st
                yield
                K.cx.barrier()
            K.stack = old
        return cm()

    def mm(self, out, lhsT, rhs, start, stop, r, w):
        return self.cx.op("pe", r, w, lambda e: e.matmul(out, lhsT, rhs, start=start, stop=stop), inc=bool(stop))

    def tr(self, out, in_, ident, r, w):
        return self.cx.op("pe", r, w, lambda e: e.transpose(out, in_, ident))

    def act(self, out, in_, func, r, w, bias=None, scale=None, accum_out=None, eng="act"):
        kw = {}
        if bias is not None:
            kw["bias"] = bias
        if scale is not None:
            kw["scale"] = scale
        if accum_out is not None:
            kw["accum_out"] = accum_out
        return self.cx.op("act", r, w, lambda e: e.activation(out, in_, func, **kw))

    def tt(self, eng, out, in0, in1, op, r, w):
        return self.cx.op(eng, r, w, lambda e: e.tensor_tensor(out, in0, in1, op))

    def ts(self, eng, out, in0, s1, s2, op0, op1, r, w):
        if op1 is None:
            return self.cx.op(eng, r, w, lambda e: e.tensor_scalar(out, in0, s1, None, op0))
        return self.cx.op(eng, r, w, lambda e: e.tensor_scalar(out, in0, s1, s2, op0, op1))

    def stt(self, out, in0, scalar, in1, op0, op1, r, w):
        return self.cx.op("dve", r, w, lambda e: e.scalar_tensor_tensor(out, in0, scalar, in1, op0, op1))

    def cp(self, eng, out, in_, r, w):
        if eng == "act":
            return self.cx.op("act", r, w, lambda e: e.copy(out, in_))
        return self.cx.op(eng, r, w, lambda e: e.tensor_copy(out, in_))

    def memset(self, eng, ap, val, w):
        return self.cx.op(eng, [], w, lambda e: e.memset(ap, val))

    def dma(self, q, out, in_, r, w, **kw):
        return self.cx.dma(q, out, in_, r, w, **kw)


def setup(K):
    nc = K.nc
    K.din("x", [T, D]); K.din("ctx", [CTX, D])
    K.din("c", [8, 128]); K.din("c_ctx", [8, 128])
    K.din("w_mod", [2, D, 6 * D]); K.din("b_mod", [2, 48, 128])
    K.din("norm_mix", [2, 8, 128]); K.din("norm_mlp", [2, 8, 128])
    K.din("w_mlp_in", [2, 128, 8, 4 * D]); K.din("w_mlp_out", [2, 128, 32, D])
    K.din("w_in_even", [128, 8, EVEN_IN]); K.din("w_out_even", [128, 8, D])
    K.din("ret_decay_logit", [8]); K.din("na_bias", [128, 8, 9, 128])
    K.din("w_in_odd", [128, 8, ODD_IN]); K.din("conv_w", [7, 24, 128]); K.din("conv_b", [24, 128])
    K.din("dt_bias", [64]); K.din("a_log", [64]); K.din("d_skip", [32])
    K.din("ssm_norm", [16, 128]); K.din("w_out_odd", [128, 16, D]); K.din("norm_final", [D])
    K.din("c_ident", [128, 128]); K.din("c_rope", [NTL, 128, 4, 128])
    K.din("c_tri", [128, 8, 128])
    K.din("c_col", [128, 2])
    K.dscr("y", [T, D], F32, out=True)
    for nm in ("X1", "X2", "X3"):
        K.dscr(nm, [NT * 128, D], F32)
    for nm in ("RQ", "RK", "RV", "RG", "NAO"):
        K.dscr(nm, [NT * 128, 512], BF16)
    K.dscr("ZS", [NT * 128, 2048], BF16)
    K.dscr("DT", [NT * 128, 64], F32)
    K.dscr("XBCT", [24, 128, XLT], BF16)
    K.dscr("XS", [NT * 128, 2048], BF16)
    K.dscr("BS", [NT * 128, 512], BF16)
    K.dscr("BT", [4, 128, NT * 128], BF16)
    K.dscr("CTt", [4, 128, NT * 128], BF16)
    K.dscr("YB", [NT * 128, 2048], BF16)
    K.ident, K.b_ident = K.sb("ident", [128, 128])
    K.identb, K.b_identb = K.sb("identb", [128, 128], BF16)
    K.ones, K.b_ones = K.sb("ones", [128, 128])
    K.dma("sp", K.ident[:], K.dram["c_ident"][:, :], [], [K.b_ident])
    K.cp("dve", K.identb[:], K.ident[:], [K.b_ident], [K.b_identb])
    K.memset("dve", K.ones[:], 1.0, [K.b_ones])
    K.PS = Rot(K, "ps", [128, 512], F32, 8, psum=True)
    K.PSa = Rot.__new__(Rot); K.PSa.items = K.PS.items[0:4]; K.PSa.i = 0
    K.PSb = Rot.__new__(Rot); K.PSb.items = K.PS.items[4:8]; K.PSb.i = 0
    K.modT = []
    K.cols = {}


def xsrc(K, stream, tt):
    if stream == "X0":
        if tt < NTC:
            return K.dram["ctx"][tt * 128:(tt + 1) * 128, :], K.db("ctx", tt)
        return K.dram["x"][(tt - NTC) * 128:(tt - NTC + 1) * 128, :], K.db("x", tt)
    if stream == "Y":
        return K.dram["y"][(tt - NTC) * 128:(tt - NTC + 1) * 128, :], K.db("y", tt)
    return K.dram[stream][tt * 128:(tt + 1) * 128, :], K.db(stream, tt)


def load_cols(K, specs):
    tot = sum(ap.shape[0] for _, ap in specs)
    assert tot <= 128
    stg, bstg = K.sb("stg", [128, 128])
    dst, bdst = K.sb("cols", [128, tot])
    r0 = 0
    offs = {}
    for nm, ap in specs:
        R = ap.shape[0]
        K.dma("sp", stg[r0:r0 + R, :], ap, [], [bstg])
        offs[nm] = (r0, R)
        r0 += R
    ps, bps = K.PS.get()
    K.tr(ps[:, 0:tot], stg[0:tot, :], K.ident[0:tot, 0:tot], [bstg, K.b_ident], [bps])
    K.cp("dve", dst[:], ps[:, 0:tot], [bps], [bdst])
    for nm, (o, R) in offs.items():
        K.cols[nm] = (dst[:, o:o + R], bdst)


def phase_mod(K):
    d = K.dram
    load_cols(K, [("c", d["c"][:, :]), ("c_ctx", d["c_ctx"][:, :]),
                  ("b_mod0", d["b_mod"][0]), ("b_mod1", d["b_mod"][1])])
    load_cols(K, [("norm_mix0", d["norm_mix"][0]), ("norm_mix1", d["norm_mix"][1]),
                  ("norm_mlp0", d["norm_mlp"][0]), ("norm_mlp1", d["norm_mlp"][1]),
                  ("ssm_norm", d["ssm_norm"][:, :]), ("conv_b", d["conv_b"][:, :])])
    load_cols(K, [("conv_w%d" % k, d["conv_w"][k]) for k in range(4)])
    load_cols(K, [("conv_w%d" % k, d["conv_w"][k]) for k in range(4, 7)])
    sc, bsc = K.sb("sc", [128, 8, 2])
    cc, bcc = K.cols["c"]
    K.act(sc[:, :, 0], cc, AF.Silu, [bcc], [bsc])
    K.act(sc[:, :, 1], K.cols["c_ctx"][0], AF.Silu, [bcc], [bsc])
    K.A = {}
    K.G = {}
    for l in range(2):
        K.modT.append(K.gsb("modT%d" % l, [128, 48, 2]))
    A_pre = {(l, s, which): K.gsb("A", [128, 8]) for l in range(2) for s in range(2) for which in range(2)}
    G_pre = {(l, s, which): K.gsb("G", [128, 1024], BF16) for l in range(2) for s in range(2) for which in range(2)
             if not (l == 1 and s == 1)}
    with K.phase():
        wrot = Rot(K, "wmod", [128, 8, 1024], F32, 2)
        for l in range(2):
            modT, bmod = K.modT[l]
            ps, bps = K.PS.get()
            for cg in range(6):
                wp, bwp = wrot.get()
                for k in range(8):
                    K.dma("sp", wp[:, k, :], d["w_mod"][l, k * 128:(k + 1) * 128, cg * 1024:(cg + 1) * 1024],
                          [], [bwp])
                for j8 in range(8):
                    j = cg * 8 + j8
                    for k in range(8):
                        K.mm(ps[:, 2 * j:2 * j + 2], wp[:, k, j8 * 128:(j8 + 1) * 128], sc[:, k, :],
                             k == 0, k == 7, [bwp, bsc], [bps])
            bm, bbm = K.cols["b_mod%d" % l]
            for s in range(2):
                K.tt("dve", modT[:, :, s], ps[:, s:96:2], bm, ALU.add, [bps, bbm], [bmod])
    for l in range(2):
        modT, bmod = K.modT[l]
        for s in range(2):
            for which, (vsc, vsh, nname) in enumerate(((1, 0, "norm_mix%d" % l), (4, 3, "norm_mlp%d" % l))):
                a, ba = A_pre[(l, s, which)]
                nm, bnm = K.cols[nname]
                K.stt(a[:], modT[:, vsc * 8:vsc * 8 + 8, s], 1.0, nm, ALU.add, ALU.mult, [bmod, bnm], [ba])
                K.A[(l, s, which)] = (a[:], modT[:, vsh * 8:vsh * 8 + 8, s], ba, bmod)
    with K.phase():
        dg = Rot(K, "diag", [128, 128], F32, 2)
        for l in range(2):
            modT, bmod = K.modT[l]
            for s in range(2):
                if l == 1 and s == 1:
                    continue
                for which, v in enumerate((2, 5)):
                    g, bg = G_pre[(l, s, which)]
                    for half in range(2):
                        ps, bps = K.PS.get()
                        for kk in range(4):
                            k = half * 4 + kk
                            dd, bdd = dg.get()
                            K.ts("dve", dd[:], K.ident[:], modT[:, v * 8 + k, s:s + 1], None, ALU.mult, None,
                                 [K.b_ident, bmod], [bdd])
                            K.mm(ps[:, kk * 128:(kk + 1) * 128], K.ones[:], dd[:], True, True,
                                 [K.b_ones, bdd], [bps])
                        K.cp("act", g[:, half * 512:(half + 1) * 512], ps[:], [bps], [bg])
                    K.G[(l, s, which)] = (g, bg)


def _gsb(K, name, shape, dt=F32):
    t = K.gstack.enter_context(K.nc.sbuf_tensor("%s_%d" % (name, K.uid()), list(shape), dt))
    return t, Buf(name)


Kern.gsb = _gsb


def rstd_op(K, out, tmp, in_, scale, b):
    K.act(tmp, in_, AF.Ln, [b], [b], bias=EPS, scale=scale)
    K.act(out, tmp, AF.Exp, [b], [b], scale=-0.5)


class NormCtx:
    def __init__(self, K, nx=3, nxn=2):
        self.xin = Rot(K, "xin", [128, D], F32, nx)
        self.junk = Rot(K, "junk", [128, D], BF16, 1)
        self.xn = Rot(K, "xn", [128, D], F32, nxn)
        self.st = Rot(K, "nst", [128, 4], F32, 4)


def norm_pre(K, N, src_ap, bsrc):
    xt, bx = N.xin.get()
    K.dma("sp", xt[:], src_ap, [bsrc], [bx])
    jk, bj = N.junk.get()
    st, bst = N.st.get()
    K.act(jk[:], xt[:], AF.Square, [bx], [bj, bst], accum_out=st[:, 0:1])
    rstd_op(K, st[:, 2:3], st[:, 1:2], st[:, 0:1], 1.0 / D, bst)
    xn, bxn = N.xn.get()
    K.ts("dve", xn[:], xt[:], st[:, 2:3], None, ALU.mult, None, [bx, bst], [bxn])
    return xt, bx, xn, bxn


def norm_tr(K, pre, AB, hT, bhT, col0, psrot=None):
    a, b, ba, bb = AB
    psrot = psrot or K.PS
    xt, bx, xn, bxn = pre
    for half in range(2):
        ps, bps = psrot.get()
        for kk in range(4):
            k = half * 4 + kk
            K.tr(ps[:, kk * 128:(kk + 1) * 128], xn[:, k * 128:(k + 1) * 128], K.ident[:], [bxn, K.b_ident], [bps])
        for kk in range(4):
            k = half * 4 + kk
            if kk % 2 == 0:
                K.act(hT[:, k, col0:col0 + 128], ps[:, kk * 128:(kk + 1) * 128], AF.Identity, [bps, ba, bb], [bhT],
                      bias=b[:, k:k + 1], scale=a[:, k:k + 1])
            else:
                K.ts("dve", hT[:, k, col0:col0 + 128], ps[:, kk * 128:(kk + 1) * 128], a[:, k:k + 1], b[:, k:k + 1],
                     ALU.mult, ALU.add, [bps, ba, bb], [bhT])
    return xt, bx


def norm_tile(K, N, src_ap, bsrc, AB, hT, bhT, col0, psrot=None):
    return norm_tr(K, norm_pre(K, N, src_ap, bsrc), AB, hT, bhT, col0, psrot)


def load_w_bf16(K, w, dram_ap, nk, c0, c1, bw):
    K.dma("pool", w[:, :, 0:c1 - c0], dram_ap[:, :, c0:c1], [], [bw], group=K.cx.new_group())


def load_w_cols(K, w, dram_ap, nk, c0, blocks, split_k=False):
    bufs = []
    for (a, b) in blocks:
        bf = Buf("wblk")
        K.dma("pool", w[:, :, a - c0:b - c0], dram_ap[:, :, a:b], [], [bf], group=K.cx.new_group())
        bufs.append(bf)
    return bufs


def phase_mlp(K, l, src, dst, tiles, final=False):
    d = K.dram
    LAG = 3
    with K.phase():
        w1, bw1 = K.sb("w1", [128, 8, 4 * D], BF16)
        w2, bw2 = K.sb("w2", [128, 32, D], BF16)
        bw1s, bw2s = [], []
        src2 = d["w_mlp_out"][l]
        for b_ in range(8):
            bw1s += load_w_cols(K, w1, d["w_mlp_in"][l], 8, 0, [(b_ * 512, (b_ + 1) * 512)])
            bf = Buf("w2blk")
            K.dma("pool", w2[:, b_ * 4:(b_ + 1) * 4, :], src2[:, b_ * 4:(b_ + 1) * 4, :], [], [bf], group=K.cx.new_group())
            bw2s.append(bf)
        N = NormCtx(K, nx=4, nxn=2)
        hrot = Rot(K, "hT", [128, 8, 256], BF16, 2)
        urot = Rot(K, "uT", [128, 256], BF16, 5)
        rrot = Rot(K, "relu", [128, 256], F32, 3)
        trot = Rot(K, "tmp", [128, 512], F32, 4)
        orot = Rot(K, "xo", [128, D], F32, 2)
        strot = Rot(K, "fst", [128, 4], F32, 2)
        if final:
            nf, bnf = K.sb("nf", [128, D])
            K.dma("sp", nf[:], d["norm_final"].partition_broadcast(128), [], [bnf])
            jrot = Rot(K, "fjunk", [128, D], BF16, 1)
        blocks = []
        ctxs = [t for t in tiles if t < NTC]
        lats = [t for t in tiles if t >= NTC]
        if ctxs:
            blocks.append(ctxs)
        for i in range(0, len(lats), 2):
            blocks.append(lats[i:i + 2])

        def front_pre(blk):
            return [norm_pre(K, N, *xsrc(K, src, tt)) for tt in blk]

        def front_tr(blk, pres):
            s = 1 if blk[0] < NTC else 0
            hT, bhT = hrot.get()
            xs = [norm_tr(K, pres[i], K.A[(l, s, 1)], hT, bhT, i * 128, psrot=K.PSb) for i in range(len(blk))]
            return hT, bhT, xs

        cur = front_tr(blocks[0], front_pre(blocks[0]))
        nxt_pre = None
        for bi, blk in enumerate(blocks):
            s = 1 if blk[0] < NTC else 0
            nt = len(blk) * 128
            hT, bhT, xs = cur
            acc = [[K.PSa.get() for n in range(2)] for i in range(len(blk))]
            us = {}
            for f in range(32 + LAG):
                if f < 32:
                    ps, bps = K.PSb.get()
                    for k in range(8):
                        K.mm(ps[:, 0:nt], w1[:, k, f * 128:(f + 1) * 128], hT[:, k, 0:nt], k == 0, k == 7,
                             [bw1s[f // 4], bhT], [bps])
                    r, br = rrot.get()
                    K.ts("dve", r[:, 0:nt], ps[:, 0:nt], 0.0, None, ALU.max, None, [bps], [br])
                    u, bu = urot.get()
                    K.act(u[:, 0:nt], r[:, 0:nt], AF.Square, [br], [bu])
                    us[f] = (u, bu)
                if f == 2 and bi + 1 < len(blocks):
                    nxt_pre = front_pre(blocks[bi + 1])
                if f == 22 and bi + 1 < len(blocks):
                    cur = front_tr(blocks[bi + 1], nxt_pre)
                if f >= LAG:
                    f2 = f - LAG
                    u, bu = us.pop(f2)
                    for i in range(len(blk)):
                        for n in range(2):
                            pa, bpa = acc[i][n]
                            K.mm(pa[:], u[:, i * 128:(i + 1) * 128], w2[:, f2, n * 512:(n + 1) * 512], f2 == 0, f2 == 31,
                                 [bu, bw2s[f2 // 4]], [bpa])
            g, bg = K.G[(l, s, 1)]
            tms = {}
            for i, tt in enumerate(blk):
                for n in range(2):
                    pa, bpa = acc[i][n]
                    tm, btm = trot.get()
                    K.tt("dve", tm[:], pa[:], g[:, n * 512:(n + 1) * 512], ALU.mult, [bpa, bg], [btm])
                    tms[(i, n)] = (tm, btm)
            for i, tt in enumerate(blk):
                xt, bx = xs[i]
                xo, bxo = orot.get()
                for n in range(2):
                    tm, btm = tms[(i, n)]
                    K.tt("dve", xo[:, n * 512:(n + 1) * 512], tm[:], xt[:, n * 512:(n + 1) * 512], ALU.add,
                         [btm, bx], [bxo])
                dap, dbf = xsrc(K, dst, tt)
                if final:
                    jk, bj = jrot.get()
                    st, bst = strot.get()
                    K.act(jk[:], xo[:], AF.Square, [bxo], [bj, bst], accum_out=st[:, 0:1])
                    rstd_op(K, st[:, 2:3], st[:, 1:2], st[:, 0:1], 1.0 / D, bst)
                    K.stt(xo[:], xo[:], st[:, 2:3], nf[:], ALU.mult, ALU.mult, [bxo, bst, bnf], [bxo])
                K.dma("pool", dap, xo[:], [bxo], [dbf])


def rope(K, ps, bps, rp, brp, ci, si, o, bo, t1rot, t2rot):
    t1, b1 = t1rot.get()
    t2, b2 = t2rot.get()
    K.tt("dve", t1[:].rearrange("p (h d) -> p h d", h=4), ps[:].rearrange("p (h d) -> p h d", h=4),
         rp[:, ci, :].unsqueeze(1).to_broadcast([128, 4, 128]), ALU.mult, [bps, brp], [b1])
    q5 = ps[:].rearrange("p (h a two i) -> p h a two i", h=4, a=2, two=2)
    t5 = t2[:].rearrange("p (h a two i) -> p h a two i", h=4, a=2, two=2)
    s4 = rp[:, si, :].rearrange("p (a two i) -> p a two i", a=2, two=2)
    for two in range(2):
        K.tt("dve", t5[:, :, :, two, :], q5[:, :, :, 1 - two, :],
             s4[:, :, two, :].unsqueeze(1).to_broadcast([128, 4, 2, 32]), ALU.mult, [bps, brp], [b2])
    K.tt("dve", o[:], t1[:], t2[:], ALU.add, [b1, b2], [bo])


def phase_l0_ret_proj(K):
    d = K.dram
    with K.phase():
        w, bw = K.sb("wr", [128, 8, 2048], BF16)
        bws = load_w_cols(K, w, d["w_in_even"], 8, 0, [(n_ * 512, (n_ + 1) * 512) for n_ in range(4)])
        N = NormCtx(K, nx=2, nxn=2)
        hrot = Rot(K, "hT", [128, 8, 128], BF16, 3)
        rrot = Rot(K, "rope", [128, 4, 128], F32, 2)
        t1rot = Rot(K, "t1", [128, 512], F32, 2)
        t2rot = Rot(K, "t2", [128, 512], F32, 2)
        orot = Rot(K, "o", [128, 512], BF16, 6)
        def front_tr(tt, pre):
            s_ = 1 if tt < NTC else 0
            hT_, bhT_ = hrot.get()
            norm_tr(K, pre, K.A[(0, s_, 0)], hT_, bhT_, 0)
            return hT_, bhT_

        cur = front_tr(0, norm_pre(K, N, *xsrc(K, "X0", 0)))
        for tt in range(NT):
            s = 1 if tt < NTC else 0
            hT, bhT = cur
            if tt + 1 < NT:
                npre = norm_pre(K, N, *xsrc(K, "X0", tt + 1))
            if s == 0:
                rp, brp = rrot.get()
                K.dma("sp", rp[:], d["c_rope"][tt - NTC], [], [brp])
            for n, nm in enumerate(("RQ", "RK", "RV", "RG")):
                if n == 3 and tt + 1 < NT:
                    cur = front_tr(tt + 1, npre)
                ps, bps = K.PS.get()
                for k in range(8):
                    K.mm(ps[:], hT[:, k, :], w[:, k, n * 512:(n + 1) * 512], k == 0, k == 7, [bhT, bws[n]], [bps])
                o, bo = orot.get()
                if n < 2 and s == 0:
                    rope(K, ps, bps, rp, brp, 2 * n, 2 * n + 1, o, bo, t1rot, t2rot)
                elif n == 1:
                    K.cx.op("act", [bps], [bo], lambda e: e.mul(o[:], ps[:], float(128 ** -0.5)))
                elif n == 3:
                    K.act(o[:], ps[:], AF.Silu, [bps], [bo])
                else:
                    K.cp("act", o[:], ps[:], [bps], [bo])
                K.dma("pool", d[nm][tt * 128:(tt + 1) * 128, :], o[:], [bo], [K.db(nm, tt)])


def na_slots(tt):
    ctxs = [(0, None), (1, None)]
    if tt < NTC:
        return ctxs
    t = tt - NTC
    if t == 0:
        loc = [(0, 4), (1, 5), (2, 7), (3, 8)]
    elif t == 1:
        loc = [(0, 3), (1, 4), (2, 5), (3, 7)]
    elif t == NTL - 2:
        loc = [(NTL - 4, 1), (NTL - 3, 3), (NTL - 2, 4), (NTL - 1, 5)]
    elif t == NTL - 1:
        loc = [(NTL - 4, 0), (NTL - 3, 1), (NTL - 2, 3), (NTL - 1, 4)]
    else:
        loc = [(t - 2, 2), (t - 1, 3), (t, 4), (t + 1, 5), (t + 2, 6)]
    return [(kt + NTC, v) for kt, v in loc] + ctxs


def phase_l0_na(K):
    d = K.dram
    with K.phase():
        NQT, _ = K.sb("NQT", [128, NT, 512], BF16)
        NKT, _ = K.sb("NKT", [128, NT, 512], BF16)
        NV, _ = K.sb("NV", [128, NT, 8, 65], BF16)
        bNQ = [Buf("nq%d" % i) for i in range(NT)]
        bNK = [Buf("nk%d" % i) for i in range(NT)]
        bNV = [Buf("nv%d" % i) for i in range(NT)]
        with K.phase():
            w, bw = K.sb("wn", [128, 8, 1536], BF16)
            bws = load_w_cols(K, w, d["w_in_even"], 8, 2048, [(2048 + n_ * 512, 2048 + (n_ + 1) * 512) for n_ in range(3)])
            N = NormCtx(K, nx=2, nxn=2)
            hrot = Rot(K, "hT", [128, 8, 128], BF16, 3)
            qrot = Rot(K, "qb", [128, 512], BF16, 3)
            def front_tr(tt, pre):
                s_ = 1 if tt < NTC else 0
                hT_, bhT_ = hrot.get()
                norm_tr(K, pre, K.A[(0, s_, 0)], hT_, bhT_, 0)
                return hT_, bhT_

            cur = front_tr(0, norm_pre(K, N, *xsrc(K, "X0", 0)))
            for tt in range(NT):
                s = 1 if tt < NTC else 0
                hT, bhT = cur
                if tt + 1 < NT:
                    npre = norm_pre(K, N, *xsrc(K, "X0", tt + 1))
                K.memset("pool", NV[:, tt, :, 64:65], 1.0, [bNV[tt]])
                for n in range(3):
                    if n == 2 and tt + 1 < NT:
                        cur = front_tr(tt + 1, npre)
                    ps, bps = K.PS.get()
                    for k in range(8):
                        K.mm(ps[:], hT[:, k, :], w[:, k, n * 512:(n + 1) * 512], k == 0, k == 7, [bhT, bws[n]], [bps])
                    if n == 2:
                        K.cp("act", NV[:, tt, :, 0:64], ps[:].rearrange("p (h e) -> p h e", h=8), [bps], [bNV[tt]])
                        continue
                    qb, bqb = qrot.get()
                    K.cp("act", qb[:], ps[:], [bps], [bqb])
                    pt, bpt = K.PS.get()
                    ptb = pt[:, 0:256].bitcast(BF16)
                    for g in range(4):
                        K.tr(ptb[:, g * 128:(g + 1) * 128], qb[:, g * 128:(g + 1) * 128], K.identb[:],
                             [bqb, K.b_identb], [bpt])
                    dst, bd = (NQT, bNQ[tt]) if n == 0 else (NKT, bNK[tt])
                    K.cp("dve", dst[:, tt, :], ptb, [bpt], [bd])
        with K.phase():
            bias, bbias = K.sb("nabias", [128, 8 * 9 * 128], BF16)
            K.dma("pool", bias[:], d["na_bias"].rearrange("p h v q -> p (h v q)"), [], [bbias])
            sbrot = Rot(K, "sbias", [128, 512], F32, 3)
            ptrot = Rot(K, "PT", [128, 7 * 128], BF16, 3)
            orot = Rot(K, "nao", [128, 512], BF16, 2)
            rcrot = Rot(K, "rec", [128, 8], F32, 2)
            for tt in range(NT):
                slots = na_slots(tt)
                nsl = len(slots)
                psO = [K.PSb.get(), K.PSb.get()]
                pend = None

                def issue_pv(p):
                    h_, PT_, bPT_ = p
                    po, bpo = psO[h_ // 4]
                    hc = (h_ % 4) * 65
                    for si_, (kt_, var_) in enumerate(slots):
                        K.mm(po[:, hc:hc + 65], PT_[:, si_ * 128:(si_ + 1) * 128], NV[:, kt_, h_, :], si_ == 0,
                             si_ == nsl - 1, [bPT_, bNV[kt_]], [bpo])

                for h in range(8):
                    g, hh = divmod(h, 2)
                    p0 = hh * 64
                    banks = [K.PSa.get()]
                    if nsl > 4:
                        banks.append(K.PSa.get())
                    for si, (kt, var) in enumerate(slots):
                        bk, bbk = banks[si // 4]
                        pos = si % 4
                        K.mm(bk[:, pos * 128:(pos + 1) * 128], NKT[p0:p0 + 64, kt, g * 128:(g + 1) * 128],
                             NQT[p0:p0 + 64, tt, g * 128:(g + 1) * 128], True, True, [bNK[kt], bNQ[tt]], [bbk])
                    PT, bPT = ptrot.get()
                    si = 0
                    while si < nsl:
                        kt, var = slots[si]
                        sj = si + 1
                        while sj < nsl and sj // 4 == si // 4 and (
                                (var is None and slots[sj][1] is None) or
                                (var is not None and slots[sj][1] is not None and slots[sj][1] == slots[sj - 1][1] + 1)):
                            sj += 1
                        bk, bbk = banks[si // 4]
                        c0 = (si % 4) * 128
                        w_ = (sj - si) * 128
                        if var is None:
                            K.act(PT[:, si * 128:sj * 128], bk[:, c0:c0 + w_], AF.Exp, [bbk], [bPT], scale=0.125)
                        else:
                            sbt, bsb = sbrot.get()
                            b0 = (h * 9 + var) * 128
                            K.stt(sbt[:, 0:w_], bk[:, c0:c0 + w_], 0.125, bias[:, b0:b0 + w_], ALU.mult, ALU.add,
                                  [bbk, bbias], [bsb])
                            K.act(PT[:, si * 128:sj * 128], sbt[:, 0:w_], AF.Exp, [bsb], [bPT])
                        si = sj
                    if pend is not None:
                        issue_pv(pend)
                    pend = (h, PT, bPT)
                issue_pv(pend)
                o, bo = orot.get()
                rc, brc = rcrot.get()
                for hb in range(2):
                    po, bpo = psO[hb]
                    pv = po[:, 0:260].rearrange("p (h e) -> p h e", h=4)
                    K.cx.op("dve", [bpo], [brc], lambda e: e.reciprocal(rc[:, hb * 4:(hb + 1) * 4], pv[:, :, 64]))
                    K.tt("dve", o[:, hb * 256:(hb + 1) * 256].rearrange("p (h e) -> p h e", h=4), pv[:, :, 0:64],
                         rc[:, hb * 4:(hb + 1) * 4].unsqueeze(2).to_broadcast([128, 4, 64]), ALU.mult, [bpo, brc], [bo])
                K.dma("pool", d["NAO"][tt * 128:(tt + 1) * 128, :], o[:], [bo], [K.db("NAO", tt)])


def phase_l0_ret(K):
    d = K.dram
    with K.phase():
        wo, bwo = K.sb("wo", [128, 8, 1024], BF16)
        load_w_bf16(K, wo, d["w_out_even"], 8, 0, 1024, bwo)
        tri, btri = K.sb("tri", [128, 8, 128])
        K.dma("sp", tri[:], d["c_tri"][:, :, :], [], [btri])
        col, bcol = K.sb("col", [128, 2])
        K.dma("sp", col[:], d["c_col"][:, :], [], [bcol])
        lg, blg = K.sb("lg", [128, 8])
        K.dma("sp", lg[:], d["ret_decay_logit"].partition_broadcast(128), [], [blg])
        K.act(lg[:], lg[:], AF.Exp, [blg], [blg], scale=-1.0)
        K.act(lg[:], lg[:], AF.Ln, [blg], [blg], bias=1.0)
        K.ts("dve", lg[:], lg[:], -1.0, None, ALU.mult, None, [blg], [blg])
        MT, bMT = K.sb("MT", [128, 512])
        QDf, bQDf = K.sb("QDf", [128, 512], BF16)
        QDb, bQDb = K.sb("QDb", [128, 512], BF16)
        kdec, bkd = K.sb("kdec", [128, 8])
        cdec, bcd = K.sb("cdec", [128, 8])
        e1, be1 = K.sb("e1", [128, 128])
        e2, be2 = K.sb("e2", [128, 128])
        for h in range(4):
            hs = slice(h * 128, (h + 1) * 128)
            K.act(e1[:], tri[:, 0, :], AF.Exp, [btri, blg], [be1], scale=lg[:, h:h + 1])
            K.tt("dve", e1[:], e1[:], tri[:, 1, :], ALU.mult, [be1, btri], [be1])
            K.act(e2[:], tri[:, 2, :], AF.Exp, [btri, blg], [be2], scale=lg[:, 4 + h:5 + h])
            K.tt("dve", e2[:], e2[:], tri[:, 3, :], ALU.mult, [be2, btri], [be2])
            K.tt("dve", MT[:, hs], e1[:], e2[:], ALU.add, [be1, be2], [bMT])
            K.act(QDf[:, hs], tri[:, 4, :], AF.Exp, [btri, blg], [bQDf], scale=lg[:, h:h + 1])
            K.act(QDb[:, hs], tri[:, 5, :], AF.Exp, [btri, blg], [bQDb], scale=lg[:, 4 + h:5 + h])
            K.act(kdec[:, h:h + 1], col[:, 0:1], AF.Exp, [bcol, blg], [bkd], scale=lg[:, h:h + 1])
            K.act(kdec[:, 4 + h:5 + h], col[:, 1:2], AF.Exp, [bcol, blg], [bkd], scale=lg[:, 4 + h:5 + h])
        K.act(cdec[:], lg[:], AF.Exp, [blg], [bcd], scale=128.0)
        SBL, _ = K.sb("SBL", [128, NT, 512], BF16)
        bSB = [Buf("sb%d" % i) for i in range(NT)]
        S, bS = K.sb("S", [128, 512])
        lrot = Rot(K, "ld", [128, 512], BF16, 10)
        kdrot = Rot(K, "kd", [128, 512], BF16, 2)

        def bc4(ap4):
            return ap4.unsqueeze(2).to_broadcast([128, 4, 128])

        def v3(ap):
            return ap.rearrange("p (h d) -> p h d", h=4)

        def state_update(rk, brk, rv, brv, c0):
            kd, bkdd = kdrot.get()
            K.tt("dve", v3(kd[:]), v3(rk[:]), bc4(kdec[:, c0:c0 + 4]), ALU.mult, [brk, bkd], [bkdd])
            ps, bps = K.PS.get()
            for h in range(4):
                hs = slice(h * 128, (h + 1) * 128)
                K.mm(ps[:, hs], kd[:, hs], rv[:, hs], True, True, [bkdd, brv], [bps])
            K.tt("dve", v3(S[:]), v3(S[:]), bc4(cdec[:, c0:c0 + 4]), ALU.mult, [bS, bcd], [bS])
            K.tt("dve", S[:], S[:], ps[:], ALU.add, [bS, bps], [bS])

        def load(nm, tt):
            t, b = lrot.get()
            K.dma("sp", t[:], d[nm][tt * 128:(tt + 1) * 128, :], [K.db(nm, tt)], [b])
            return t, b

        K.memset("dve", S[:], 0.0, [bS])
        for tt in [1, 0] + list(range(NT - 1, NTC - 1, -1)):
            K.cp("act", SBL[:, tt, :], S[:], [bS], [bSB[tt]])
            rk, brk = load("RK", tt)
            rv, brv = load("RV", tt)
            state_update(rk, brk, rv, brv, 4)
        K.memset("dve", S[:], 0.0, [bS])
        sfrot = Rot(K, "sfb", [128, 512], BF16, 2)
        qkrot = Rot(K, "QK", [128, 1024], BF16, 2)
        ptrot = Rot(K, "PTm", [128, 512], BF16, 2)
        qfrot = Rot(K, "Qf", [128, 512], BF16, 4)
        strot = Rot(K, "st6", [128, 4, 6], F32, 2)
        mvrot = Rot(K, "mv", [128, 4, 2], F32, 2)
        rsrot = Rot(K, "rs", [128, 8], F32, 2)
        ynrot = Rot(K, "yn", [128, 512], F32, 2)
        mixrot = Rot(K, "mix", [128, 1024], BF16, 3)
        mtrot = Rot(K, "mixT", [128, 1024], BF16, 2)
        xrot = Rot(K, "xt", [128, D], F32, 3)
        xorot = Rot(K, "xo", [128, D], F32, 2)
        tmrot = Rot(K, "tm", [128, 512], F32, 2)

        def A1(tt):
            c = {"tt": tt}
            c["rq"] = load("RQ", tt)
            c["rk"] = load("RK", tt)
            c["rv"] = load("RV", tt)
            c["rg"] = load("RG", tt)
            mix, bmix = mixrot.get()
            K.dma("sp", mix[:, 512:1024], d["NAO"][tt * 128:(tt + 1) * 128, :], [K.db("NAO", tt)], [bmix])
            c["mix"] = (mix, bmix)
            xt, bx = xrot.get()
            sap, sbf = xsrc(K, "X0", tt)
            K.dma("sp", xt[:], sap, [sbf], [bx])
            c["xt"] = (xt, bx)
            rq, brq = c["rq"]
            rk, brk = c["rk"]
            pt, bpt = K.PS.get()
            ptb = pt[:].bitcast(BF16)
            for h in range(4):
                hs = slice(h * 128, (h + 1) * 128)
                K.tr(ptb[:, hs], rq[:, hs], K.identb[:], [brq, K.b_identb], [bpt])
                K.tr(ptb[:, 512 + h * 128:512 + (h + 1) * 128], rk[:, hs], K.identb[:], [brk, K.b_identb], [bpt])
            QK, bQK = qkrot.get()
            K.cp("act", QK[:], ptb, [bpt], [bQK])
            c["QK"] = (QK, bQK)
            return c

        def A2(c):
            QK, bQK = c["QK"]
            pa, bpa = K.PS.get()
            for h in range(4):
                hs = slice(h * 128, (h + 1) * 128)
                K.mm(pa[:, hs], QK[:, 512 + h * 128:512 + (h + 1) * 128], QK[:, hs], True, True, [bQK], [bpa])
            PTm, bPTm = ptrot.get()
            K.tt("dve", PTm[:], pa[:], MT[:], ALU.mult, [bpa, bMT], [bPTm])
            Qf, bQf = qfrot.get()
            K.tt("dve", Qf[:], QK[:, 0:512], QDf[:], ALU.mult, [bQK, bQDf], [bQf])
            Qb, bQb = qfrot.get()
            K.tt("dve", Qb[:], QK[:, 0:512], QDb[:], ALU.mult, [bQK, bQDb], [bQb])
            c["PTm"] = (PTm, bPTm)
            c["Qf"] = (Qf, bQf)
            c["Qb"] = (Qb, bQb)

        def B1(c, sfb, bsfb):
            tt = c["tt"]
            PTm, bPTm = c["PTm"]
            Qf, bQf = c["Qf"]
            Qb, bQb = c["Qb"]
            rv, brv = c["rv"]
            rg, brg = c["rg"]
            mix, bmix = c["mix"]
            py, bpy = K.PS.get()
            for h in range(4):
                hs = slice(h * 128, (h + 1) * 128)
                K.mm(py[:, hs], PTm[:, hs], rv[:, hs], True, False, [bPTm, brv], [bpy])
                K.mm(py[:, hs], Qf[:, hs], sfb[:, hs], False, False, [bQf, bsfb], [bpy])
                K.mm(py[:, hs], Qb[:, hs], SBL[:, tt, hs], False, True, [bQb, bSB[tt]], [bpy])
            st6, bst6 = strot.get()
            mv, bmv = mvrot.get()
            rs, brs = rsrot.get()
            for h in range(4):
                hs = slice(h * 128, (h + 1) * 128)
                K.cx.op("dve", [bpy], [bst6], lambda e: e.bn_stats(st6[:, h, :], py[:, hs]))
            for h in range(4):
                K.cx.op("dve", [bst6], [bmv], lambda e: e.bn_aggr(mv[:, h, :], st6[:, h, :]))
            K.act(rs[:, 4:8], mv[:, :, 1], AF.Ln, [bmv], [brs], bias=EPS, scale=1.0)
            K.act(rs[:, 0:4], rs[:, 4:8], AF.Exp, [brs], [brs], scale=-0.5)
            yn, byn = ynrot.get()
            for h in range(4):
                hs = slice(h * 128, (h + 1) * 128)
                K.ts("dve", yn[:, hs], py[:, hs], mv[:, h, 0:1], rs[:, h:h + 1], ALU.subtract, ALU.mult,
                     [bpy, bmv, brs], [byn])
            K.tt("dve", mix[:, 0:512], yn[:], rg[:], ALU.mult, [byn, brg], [bmix])

        def B2(c):
            tt = c["tt"]
            s_ = 1 if tt < NTC else 0
            mix, bmix = c["mix"]
            xt, bx = c["xt"]
            pm, bpm = K.PS.get()
            pmb = pm[:].bitcast(BF16)
            for k in range(8):
                K.tr(pmb[:, k * 128:(k + 1) * 128], mix[:, k * 128:(k + 1) * 128], K.identb[:],
                     [bmix, K.b_identb], [bpm])
            mixT, bmT = mtrot.get()
            K.cp("act", mixT[:], pmb, [bpm], [bmT])
            g, bg = K.G[(0, s_, 0)]
            xo, bxo = xorot.get()
            for n in range(2):
                ps, bps = K.PS.get()
                for k in range(8):
                    K.mm(ps[:], mixT[:, k * 128:(k + 1) * 128], wo[:, k, n * 512:(n + 1) * 512], k == 0, k == 7,
                         [bmT, bwo], [bps])
                tm, btm = tmrot.get()
                K.tt("dve", tm[:], ps[:], g[:, n * 512:(n + 1) * 512], ALU.mult, [bps, bg], [btm])
                K.tt("dve", xo[:, n * 512:(n + 1) * 512], tm[:], xt[:, n * 512:(n + 1) * 512], ALU.add,
                     [btm, bx], [bxo])
            dap, dbf = xsrc(K, "X1", tt)
            K.dma("pool", dap, xo[:], [bxo], [dbf])

        cur = A1(0)
        A2(cur)
        sfb, bsfb = sfrot.get()
        K.cp("act", sfb[:], S[:], [bS], [bsfb])
        prevB = None
        for i in range(NT + 1):
            sfb_cur = (sfb, bsfb)
            if cur is not None:
                state_update(cur["rk"][0], cur["rk"][1], cur["rv"][0], cur["rv"][1], 0)
                sfb, bsfb = sfrot.get()
                K.cp("act", sfb[:], S[:], [bS], [bsfb])
            nxt = A1(i + 1) if i + 1 < NT else None
            if cur is not None:
                B1(cur, sfb_cur[0], sfb_cur[1])
            if nxt is not None:
                A2(nxt)
            if prevB is not None:
                B2(prevB)
            prevB = cur
            cur = nxt


L1_SRC = ["X2"]


def ssd_blocks():
    blks = [([0, 1], 0, 0)]
    for b in range(NTL // 4):
        blks.append(([NTC + 4 * b + i for i in range(4)], 262, b * 512))
    return blks


def phase_l1_proj(K):
    d = K.dram
    with K.phase():
        w, bw = K.sb("wi", [128, 8, ODD_IN], BF16)
        bw_dt = load_w_cols(K, w, d["w_in_odd"], 8, 0, [(4608, 5184)])[0]
        bw_x = load_w_cols(K, w, d["w_in_odd"], 8, 0, [(2048 + n_ * 512, 2048 + (n_ + 1) * 512) for n_ in range(5)])
        bw_x.append(bw_dt)
        bw_z = load_w_cols(K, w, d["w_in_odd"], 8, 0, [(n_ * 512, (n_ + 1) * 512) for n_ in range(4)], split_k=True)
        N = NormCtx(K, nx=2, nxn=4)
        hrot = Rot(K, "hT", [128, 8, 512], BF16, 2)
        zrot = Rot(K, "zs", [128, 2048], BF16, 2)
        drot = Rot(K, "dtt", [128, 64], F32, 3)
        xrot = Rot(K, "xbc", [128, 512], BF16, 4)
        dtb, bdtb = K.sb("dtb", [128, 64])
        K.dma("sp", dtb[:], d["dt_bias"].partition_broadcast(128), [], [bdtb])
        zz, bzz = K.sb("zz", [128, 24, 3], BF16)
        K.memset("dve", zz[:], 0.0, [bzz])
        for c0 in (0, 259, 262, 4361):
            K.dma("pool", d["XBCT"][:, :, c0:c0 + 3].rearrange("c p t -> p c t"), zz[:], [bzz], [Buf()])
        blks = ssd_blocks()

        def f_pre(bi):
            return [norm_pre(K, N, *xsrc(K, L1_SRC[0], tt)) for tt in blks[bi][0]]

        def f_tr(bi, pres):
            s_ = 1 if blks[bi][0][0] < NTC else 0
            hT_, bhT_ = hrot.get()
            for i_, p_ in enumerate(pres):
                norm_tr(K, p_, K.A[(1, s_, 0)], hT_, bhT_, i_ * 128)
            return hT_, bhT_

        cur = f_tr(0, f_pre(0))
        for bi, (tiles, off, t0) in enumerate(blks):
            s = 1 if tiles[0] < NTC else 0
            nt = len(tiles) * 128
            hT, bhT = cur
            if bi + 1 < len(blks):
                npre = f_pre(bi + 1)
            for i, tt in enumerate(tiles):
                ts_ = slice(i * 128, (i + 1) * 128)
                if s == 0:
                    zs, bzs = zrot.get()
                    for n in range(4):
                        ps, bps = K.PS.get()
                        for k in range(8):
                            K.mm(ps[:], hT[:, k, ts_], w[:, k, n * 512:(n + 1) * 512], k == 0, k == 7, [bhT, bw_z[n]], [bps])
                        K.act(zs[:, n * 512:(n + 1) * 512], ps[:], AF.Silu, [bps], [bzs])
                    K.dma("pool", d["ZS"][tt * 128:(tt + 1) * 128, :], zs[:], [bzs], [Buf()])
                ps, bps = K.PS.get()
                for k in range(8):
                    K.mm(ps[:, 0:64], hT[:, k, ts_], w[:, k, 5120:5184], k == 0, k == 7, [bhT, bw_dt], [bps])
                dt, bdt = drot.get()
                K.tt("dve", dt[:], ps[:, 0:64], dtb[:], ALU.add, [bps, bdtb], [bdt])
                K.act(dt[:], dt[:], AF.Exp, [bdt], [bdt])
                K.act(dt[:], dt[:], AF.Ln, [bdt], [bdt], bias=1.0)
                K.dma("pool", d["DT"][tt * 128:(tt + 1) * 128, :], dt[:], [bdt], [Buf()])
            for ch in range(24):
                if ch == 16 and bi + 1 < len(blks):
                    cur = f_tr(bi + 1, npre)
                ps, bps = K.PS.get()
                for k in range(8):
                    K.mm(ps[:, 0:nt], w[:, k, 2048 + ch * 128:2048 + (ch + 1) * 128], hT[:, k, 0:nt], k == 0, k == 7,
                         [bw_x[ch // 4], bhT], [bps])
                xb, bxb = xrot.get()
                K.cp("act" if ch % 2 == 0 else "dve", xb[:, 0:nt], ps[:, 0:nt], [bps], [bxb])
                c0 = off + 3 + t0
                K.dma("pool", d["XBCT"][ch, :, c0:c0 + nt], xb[:, 0:nt], [bxb], [Buf()])


def phase_l1_conv(K):
    d = K.dram
    with K.phase():
        dW, _ = K.sb("diagW", [128, 24 * 7, 128], BF16)
        bdWs = [Buf("dW%d" % ch) for ch in range(24)]
        built = set()

        def build_dw(ch):
            if ch in built:
                return
            built.add(ch)
            for k in range(7):
                cw, bcw = K.cols["conv_w%d" % k]
                K.ts("dve", dW[:, ch * 7 + k, :], K.ident[:], cw[:, ch:ch + 1], None, ALU.mult, None,
                     [K.b_ident, bcw], [bdWs[ch]])
        cb, bcb = K.cols["conv_b"]
        xirot = Rot(K, "xin", [128, 24, 518], BF16, 2)
        xarot = Rot(K, "xact", [128, 24, 512], BF16, 2)
        xsrot = Rot(K, "xs", [128, 2048], BF16, 2)
        bsrot = Rot(K, "bs", [128, 512], BF16, 2)
        for tiles, off, t0 in ssd_blocks():
            nt = len(tiles) * 128
            xin, bxin = xirot.get()
            K.dma("sp", xin[:, :, 0:nt + 6], d["XBCT"][:, :, off + t0:off + t0 + nt + 6].rearrange("c p t -> p c t"),
                  [], [bxin])
            xa, bxa = xarot.get()
            for ch in range(24):
                build_dw(ch)
                if ch + 1 < 24:
                    build_dw(ch + 1)
                ps, bps = K.PS.get()
                for k in range(7):
                    K.mm(ps[:, 0:nt], dW[:, ch * 7 + k, :], xin[:, ch, k:k + nt], k == 0, k == 6, [bdWs[ch], bxin], [bps])
                K.act(xa[:, ch, 0:nt], ps[:, 0:nt], AF.Silu, [bps, bcb], [bxa], bias=cb[:, ch:ch + 1])
            tok0 = tiles[0] * 128
            K.dma("pool", d["BT"][:, :, tok0:tok0 + nt].rearrange("g p t -> p g t"), xa[:, 16:20, 0:nt], [bxa], [Buf()])
            K.dma("pool", d["CTt"][:, :, tok0:tok0 + nt].rearrange("g p t -> p g t"), xa[:, 20:24, 0:nt], [bxa], [Buf()])
            for i, tt in enumerate(tiles):
                ts_ = slice(i * 128, (i + 1) * 128)
                xs, bxs = xsrot.get()
                for half in range(2):
                    pt, bpt = K.PS.get()
                    ptb = pt[:].bitcast(BF16)
                    for c in range(8):
                        K.tr(ptb[:, c * 128:(c + 1) * 128], xa[:, half * 8 + c, ts_], K.identb[:], [bxa, K.b_identb], [bpt])
                    K.cp("act" if half == 0 else "dve", xs[:, half * 1024:(half + 1) * 1024], ptb, [bpt], [bxs])
                K.dma("pool", d["XS"][tt * 128:(tt + 1) * 128, :], xs[:], [bxs], [Buf()])
                bs, bbs = bsrot.get()
                pt, bpt = K.PS.get()
                ptb = pt[:, 0:256].bitcast(BF16)
                for g in range(4):
                    K.tr(ptb[:, g * 128:(g + 1) * 128], xa[:, 16 + g, ts_], K.identb[:], [bxa, K.b_identb], [bpt])
                K.cp("dve", bs[:], ptb, [bpt], [bbs])
                K.dma("pool", d["BS"][tt * 128:(tt + 1) * 128, :], bs[:], [bbs], [Buf()])


def phase_l1_scan(K):
    d = K.dram
    with K.phase():
        wo, bwo = K.sb("wo1", [128, 16, D], BF16)
        load_w_bf16(K, wo, d["w_out_odd"], 16, 0, D, bwo)
        tri, btri = K.sb("tri", [128, 8, 128])
        K.dma("sp", tri[:], d["c_tri"][:, :, :], [], [btri])
        abc, babc = K.sb("abc", [128, 64])
        K.dma("sp", abc[:], d["a_log"].partition_broadcast(128), [], [babc])
        K.act(abc[:], abc[:], AF.Exp, [babc], [babc])
        K.ts("dve", abc[:], abc[:], -1.0, None, ALU.mult, None, [babc], [babc])
        dsk, bdsk = K.sb("dsk", [128, 32])
        K.dma("sp", dsk[:], d["d_skip"].partition_broadcast(128), [], [bdsk])
        nw, bnw = K.cols["ssm_norm"]

        def subrot(lo, hi):
            r = Rot.__new__(Rot)
            r.items = K.PS.items[lo:hi]
            r.i = 0
            return r
        pdrot = subrot(0, 2)
        pyirot = subrot(2, 5)
        genps = subrot(5, 8)
        Tb, bTb = K.sb("Tb", [128, 2, 128], BF16)
        K.cp("dve", Tb[:, 0, :], tri[:, 1, :], [btri], [bTb])
        K.cp("dve", Tb[:, 1, :], tri[:, 3, :], [btri], [bTb])
        NEGr, bNEGr = K.sb("NEGr", [128, 2, 4, 128], BF16)
        for dd in range(2):
            K.cp("dve", NEGr[:, dd, :, :], tri[:, 6 + dd, :].unsqueeze(1).to_broadcast([128, 4, 128]), [btri], [bNEGr])
        HT, bHT = K.sb("HT", [128, 2048])
        HTb, bHTb = K.sb("HTb", [128, 2048], BF16)
        xsrot = Rot(K, "xs", [128, 2048], BF16, 3)
        bsrot = Rot(K, "bs", [128, 512], BF16, 3)
        btrot = Rot(K, "bt", [128, 4, 128], BF16, 3)
        ctrot = Rot(K, "ct", [128, 4, 128], BF16, 3)
        dtrot = Rot(K, "dt", [128, 64], F32, 3)
        ybrot = Rot(K, "yb", [128, 2048], BF16, 3)
        zsrot = Rot(K, "zs", [128, 2048], BF16, 3)
        xtrot = Rot(K, "xt", [128, D], F32, 3)
        smrot = Rot(K, "sm", [128, 8, 32], F32, 3)
        dhlrot = Rot(K, "dhl", [128, 2, 32], BF16, 3)
        wbrot = Rot(K, "wendb", [128, 32], BF16, 3)
        LTrot = Rot(K, "LT", [128, 512], BF16, 3)
        Wrot = Rot(K, "Wt", [128, 512], BF16, 5)
        xqrot = Rot(K, "xq", [128, 2048], BF16, 1)
        cburot = Rot(K, "cbu", [128, 512], BF16, 3)
        xprot2 = None
        bigrot = Rot(K, "big", [128, 2048], BF16, 4)
        t1rot = Rot(K, "t1big", [128, 2048], BF16, 2)
        tmrot = Rot(K, "tm", [128, 512], F32, 2)
        tmbrot = Rot(K, "tmb", [128, 512], BF16, 3)
        ynrot = Rot(K, "yn", [128, 2048], BF16, 2)
        yTrot = Rot(K, "ynT", [128, 2048], BF16, 1)
        xorot = Rot(K, "xo", [128, D], F32, 1)
        strot = Rot(K, "fst", [128, 4], F32, 2)

        def bch(ap, n, m):
            return ap.unsqueeze(2).to_broadcast([128, n, m])

        phrot = Rot(K, "phs", [128, 2048], BF16, 3)

        def pre1(dirn, tt, defer=False):
            Tm = tri[:, 1, :] if dirn == 0 else tri[:, 3, :]
            NEGm = tri[:, 6, :] if dirn == 0 else tri[:, 7, :]
            dc = dirn * 32
            lat = tt >= NTC
            c = {"tt": tt, "lat": lat, "dirn": dirn}
            xs, bxs = xsrot.get()
            K.dma("sp", xs[:], d["XS"][tt * 128:(tt + 1) * 128, :], [], [bxs])
            bs, bbs = bsrot.get()
            K.dma("sp", bs[:], d["BS"][tt * 128:(tt + 1) * 128, :], [], [bbs])
            dt, bdt = dtrot.get()
            K.dma("sp", dt[:], d["DT"][tt * 128:(tt + 1) * 128, :], [], [bdt])
            if lat:
                bt, bbt = btrot.get()
                K.dma("sp", bt[:], d["BT"][:, :, tt * 128:(tt + 1) * 128].rearrange("g p t -> p g t"), [], [bbt])
                ct, bct = ctrot.get()
                K.dma("sp", ct[:], d["CTt"][:, :, tt * 128:(tt + 1) * 128].rearrange("g p t -> p g t"), [], [bct])
                c["ct"] = (ct, bct)
                if dirn == 0:
                    yb, byb = ybrot.get()
                    K.dma("sp", yb[:], d["YB"][tt * 128:(tt + 1) * 128, :], [], [byb])
                    zs, bzs = zsrot.get()
                    K.dma("sp", zs[:], d["ZS"][tt * 128:(tt + 1) * 128, :], [], [bzs])
                    c["fin_in"] = (yb, byb, zs, bzs)
            c["xs"] = (xs, bxs)
            sm, bsm = smrot.get()
            dta, acol, eac, ela, wend, stmp, lndt, ebias = (sm[:, i, :] for i in range(8))
            c["sm"] = (sm, bsm)
            K.tt("dve", dta, dt[:, dc:dc + 32], abc[:, dc:dc + 32], ALU.mult, [bdt, babc], [bsm])
            dhl, bdhl = dhlrot.get()
            c["dhl"] = (dhl, bdhl)
            K.cp("dve", dhl[:, 0, :], dta, [bsm], [bdhl])
            K.tt("dve", dhl[:, 1, :], dta, dhl[:, 0, :], ALU.subtract, [bsm, bdhl], [bdhl])
            K.act(lndt, dt[:, dc:dc + 32], AF.Ln, [bdt], [bsm])

            return _pre1_rest(c, dirn, tt, Tm, dc, lat, xs, bxs, bs, bbs, dt, bdt, sm, bsm, defer,
                              (bt, bbt, ct, bct) if lat else None)

        def _pre1_rest(c, dirn, tt, Tm, dc, lat, xs, bxs, bs, bbs, dt, bdt, sm, bsm, defer, btct):
            dta, acol, eac, ela, wend, stmp, lndt, ebias = (sm[:, i, :] for i in range(8))
            if lat:
                bt, bbt, ct, bct = btct
            st_ = {}

            def part_b():
                pc, bpc = genps.get()
                K.mm(pc[:, 0:32], Tm, dta, True, True, [btri, bsm], [bpc])
                K.mm(pc[:, 32:64], K.ones[:], dta, True, True, [K.b_ones, bsm], [bpc])
                K.cp("dve", acol, pc[:, 0:32], [bpc], [bsm])
                K.tt("dve", ebias, lndt, acol, ALU.subtract, [bsm], [bsm])
                K.act(eac, pc[:, 0:32], AF.Exp, [bpc], [bsm])
                K.act(ela, pc[:, 32:64], AF.Exp, [bpc], [bsm])
                K.tt("dve", stmp, pc[:, 32:64], acol, ALU.subtract, [bpc, bsm], [bsm])
                K.act(stmp, stmp, AF.Exp, [bsm], [bsm])
                wendb, bwendb = wbrot.get()
                K.tt("dve", wendb[:], stmp, dt[:, dc:dc + 32], ALU.mult, [bsm, bdt], [bwendb])
                xq, bxq = xqrot.get()
                K.tt("dve", xq[:].rearrange("p (h e) -> p h e", h=32), xs[:].rearrange("p (h e) -> p h e", h=32),
                     bch(wendb[:], 32, 64), ALU.mult, [bxs, bwendb], [bxq])
                st_["xq"] = (xq, bxq)

            def part_c():
                xq, bxq = st_["xq"]
                phs, bphs = phrot.get()
                c["phs"] = (phs, bphs)
                for g in range(4):
                    ph, bph = genps.get()
                    K.mm(ph[:], bs[:, g * 128:(g + 1) * 128], xq[:, g * 512:(g + 1) * 512], True, True, [bbs, bxq], [bph])
                    K.cp("act", phs[:, g * 512:(g + 1) * 512], ph[:], [bph], [bphs])
                if lat:
                    pcb, bpcb = genps.get()
                    for g in range(4):
                        K.mm(pcb[:, g * 128:(g + 1) * 128], bt[:, g, :], ct[:, g, :], True, True, [bbt, bct], [bpcb])
                    cbu, bcbu = cburot.get()
                    K.tt("dve", cbu[:].rearrange("p (g i) -> p g i", g=4), pcb[:].rearrange("p (g i) -> p g i", g=4),
                         Tm.unsqueeze(1).to_broadcast([128, 4, 128]), ALU.mult, [bpcb, btri], [bcbu])
                    c["cbu"] = (cbu, bcbu)
                    c["yd"] = bigrot.get()

            if defer:
                c["later"] = [part_b, part_c]
            else:
                part_b()
                part_c()
            return c

        def pre2(c, hooks=None):
            hooks = hooks or {}
            if not c["lat"]:
                for k_ in sorted(hooks):
                    hooks[k_]()
                return
            dirn = c["dirn"]
            sm, bsm = c["sm"]
            ebias = sm[:, 7, :]
            dhl, bdhl = c["dhl"]
            xs, bxs = c["xs"]
            cbu, bcbu = c["cbu"]
            yd, byd = c["yd"]
            pend = []

            def issue_yi(p):
                g_, hb_, Wt_, bWt_, pyi_, bpyi_ = p
                for h4 in range(4):
                    h8 = hb_ * 4 + h4
                    hh = g_ * 8 + h8
                    K.mm(pyi_[:, h8 * 64:(h8 + 1) * 64], Wt_[:, h4 * 128:(h4 + 1) * 128],
                         xs[:, hh * 64:(hh + 1) * 64], True, True, [bWt_, bxs], [bpyi_])
                if hb_ == 1:
                    K.cp("act", yd[:, g_ * 512:(g_ + 1) * 512], pyi_[:], [bpyi_], [byd])

            for g in range(4):
                pyi, bpyi = pyirot.get()
                for hb in range(2):
                    if (g * 2 + hb) in hooks:
                        hooks[g * 2 + hb]()
                    pd, bpd = pdrot.get()
                    K.mm(pd[:], K.identb[:], NEGr[:, dirn, :, :].rearrange("p h i -> p (h i)"), True, False,
                         [K.b_identb, bNEGr], [bpd])
                    for h4 in range(4):
                        hh = g * 8 + hb * 4 + h4
                        K.mm(pd[:, h4 * 128:(h4 + 1) * 128], dhl[:, 0, hh:hh + 1].to_broadcast([128, 128]), Tb[:, dirn, :],
                             False, False, [bdhl, bTb], [bpd])
                        K.mm(pd[:, h4 * 128:(h4 + 1) * 128], dhl[:, 1, hh:hh + 1].to_broadcast([128, 128]), Tb[:, dirn, :],
                             False, h4 == 3, [bdhl, bTb], [bpd])
                    LTt, bLT = LTrot.get()
                    for h4 in range(4):
                        hh = g * 8 + hb * 4 + h4
                        K.act(LTt[:, h4 * 128:(h4 + 1) * 128], pd[:, h4 * 128:(h4 + 1) * 128], AF.Exp, [bpd, bsm], [bLT],
                              bias=ebias[:, hh:hh + 1])
                    Wt, bWt = Wrot.get()
                    K.tt("dve", Wt[:].rearrange("p (h i) -> p h i", h=4), LTt[:].rearrange("p (h i) -> p h i", h=4),
                         cbu[:, g * 128:(g + 1) * 128].unsqueeze(1).to_broadcast([128, 4, 128]), ALU.mult,
                         [bLT, bcbu], [bWt])
                    pend.append((g, hb, Wt, bWt, pyi, bpyi))
                    if len(pend) > 2:
                        issue_yi(pend.pop(0))
            while pend:
                issue_yi(pend.pop(0))

        def post_hooks(c):
            dirn = c["dirn"]
            tt = c["tt"]
            sm, bsm = c["sm"]
            eac = sm[:, 2, :]
            ela = sm[:, 3, :]
            phs, bphs = c["phs"]
            H = {}

            def h_mult():
                K.tt("dve", HT[:].rearrange("p (h e) -> p h e", h=32), HT[:].rearrange("p (h e) -> p h e", h=32),
                     bch(ela, 32, 64), ALU.mult, [bHT, bsm], [bHT])

            def h_add():
                K.tt("dve", HT[:], HT[:], phs[:], ALU.add, [bHT, bphs], [bHT])

            def h_cast():
                K.cp("dve", HTb[:], HT[:], [bHT], [bHTb])

            def mk_pye(g):
                def f():
                    ct, bct = c["ct"]
                    yd, byd = c["yd"]
                    pye, bpye = genps.get()
                    K.mm(pye[:], ct[:, g, :], HTb[:, g * 512:(g + 1) * 512], True, True, [bct, bHTb], [bpye])
                    tm, btm = tmbrot.get()
                    K.tt("dve", tm[:].rearrange("p (h e) -> p h e", h=8), pye[:].rearrange("p (h e) -> p h e", h=8),
                         bch(eac[:, g * 8:(g + 1) * 8], 8, 64), ALU.mult, [bpye, bsm], [btm])
                    K.tt("dve", yd[:, g * 512:(g + 1) * 512], yd[:, g * 512:(g + 1) * 512], tm[:], ALU.add,
                         [byd, btm], [byd])
                return f

            def seq(*fs):
                def f():
                    for x in fs:
                        x()
                return f

            if not c["lat"]:
                H[0] = seq(h_mult, h_add, h_cast)
                return H, None
            fin = None
            if dirn == 1:
                def y_dsk():
                    xs, bxs = c["xs"]
                    t1, bt1 = t1rot.get()
                    c["t1"] = (t1, bt1)
                    K.tt("dve", t1[:].rearrange("p (h e) -> p h e", h=32), xs[:].rearrange("p (h e) -> p h e", h=32),
                         bch(dsk[:], 32, 64), ALU.mult, [bxs, bdsk], [bt1])

                def y_store():
                    yd, byd = c["yd"]
                    t1, bt1 = c["t1"]
                    yb, byb = ybrot.get()
                    K.tt("dve", yb[:], t1[:], yd[:], ALU.add, [bt1, byd], [byb])
                    K.dma("pool", d["YB"][tt * 128:(tt + 1) * 128, :], yb[:], [byb], [Buf()])
                H[0] = seq(h_mult, mk_pye(0))
                H[1] = seq(mk_pye(1), h_add)
                H[2] = seq(mk_pye(2), y_dsk)
                H[3] = mk_pye(3)
                H[4] = h_cast
                H[6] = y_store
            else:
                fin = c
                H[0] = seq(h_mult, mk_pye(0))
                H[1] = seq(mk_pye(1), h_add)
                H[2] = mk_pye(2)
                H[3] = mk_pye(3)
                H[4] = h_cast
            return H, fin

        def fin_chain(c):
            xs, bxs = c["xs"]
            yd, byd = c["yd"]
            yb, byb, zs, bzs = c["fin_in"]
            xt, bx = xtrot.get()
            sap, sbf = xsrc(K, L1_SRC[0], c["tt"])
            K.dma("sp", xt[:], sap, [sbf], [bx])
            t1, bt1 = t1rot.get()
            K.tt("dve", t1[:], yd[:], yb[:], ALU.add, [byd, byb], [bt1])
            K.tt("dve", t1[:], t1[:], zs[:], ALU.mult, [bt1, bzs], [bt1])
            yn, byn = ynrot.get()
            st, bst = strot.get()
            K.cx.op("dve", [bt1], [byn, bst], lambda e: e.scalar_tensor_tensor(yn[:], t1[:], 1.0, t1[:], ALU.mult, ALU.mult,
                                                                             accum_out=st[:, 0:1]))
            rstd_op(K, st[:, 2:3], st[:, 1:2], st[:, 0:1], 1.0 / 2048, bst)
            K.ts("dve", yn[:], t1[:], st[:, 2:3], None, ALU.mult, None, [bt1, bst], [byn])
            return {"tt": c["tt"], "yn": (yn, byn), "xt": (xt, bx)}

        def fin_tr(f):
            yn, byn = f["yn"]
            ynT, byT = yTrot.get()
            f["ynT"] = (ynT, byT)
            for half in range(2):
                pt, bpt = genps.get()
                ptb = pt[:].bitcast(BF16)
                for c8 in range(8):
                    k = half * 8 + c8
                    K.tr(ptb[:, c8 * 128:(c8 + 1) * 128], yn[:, k * 128:(k + 1) * 128], K.identb[:], [byn, K.b_identb], [bpt])
                K.cp("act", ynT[:, half * 1024:(half + 1) * 1024], ptb, [bpt], [byT])

        def fin_out(f):
            tt = f["tt"]
            ynT, byT = f["ynT"]
            xt, bx = f["xt"]
            g, bg = K.G[(1, 0, 0)]
            xo, bxo = xorot.get()
            for n in range(2):
                ps, bps = genps.get()
                for k in range(16):
                    K.mm(ps[:], ynT[:, k * 128:(k + 1) * 128], wo[:, k, n * 512:(n + 1) * 512], k == 0, k == 15,
                         [byT, bwo], [bps])
                tm, btm = tmrot.get()
                K.tt("dve", tm[:], ps[:], g[:, n * 512:(n + 1) * 512], ALU.mult, [bps, bg], [btm])
                K.tt("dve", xo[:, n * 512:(n + 1) * 512], tm[:], xt[:, n * 512:(n + 1) * 512], ALU.add, [btm, bx], [bxo])
            dap, dbf = xsrc(K, "X3", tt)
            K.dma("pool", dap, xo[:], [bxo], [dbf])

        def sweep(dirn):
            K.memset("dve", HT[:], 0.0, [bHT])
            K.memset("pool", HTb[:], 0.0, [bHTb])
            order = list(range(NT)) if dirn == 0 else [1, 0] + list(range(NT - 1, NTC - 1, -1))
            p1 = [pre1(dirn, order[0]), pre1(dirn, order[1])]
            cur = p1.pop(0)
            pre2(cur)
            f_tr = None
            f_out = None
            for i in range(len(order) + 2):
                if f_out is not None:
                    fin_out(f_out)
                    f_out = None
                nxt = p1.pop(0) if p1 else None
                hooks, fc = post_hooks(cur) if cur is not None else ({}, None)
                if i + 2 < len(order):
                    holder = {}

                    def chain(prev, fn):
                        def f():
                            if prev is not None:
                                prev()
                            fn()
                        return f

                    def ha(tt2=order[i + 2]):
                        holder["c"] = pre1(dirn, tt2, defer=True)
                        p1.append(holder["c"])
                    hooks[1] = chain(hooks.get(1), ha)
                    hooks[3] = chain(hooks.get(3), lambda: holder["c"]["later"][0]())
                    hooks[6] = chain(hooks.get(6), lambda: holder["c"]["later"][1]())
                if nxt is not None:
                    pre2(nxt, hooks)
                else:
                    for k_ in sorted(hooks):
                        hooks[k_]()
                if f_tr is not None:
                    fin_tr(f_tr)
                    f_out = f_tr
                    f_tr = None
                if fc is not None:
                    f_tr = fin_chain(fc)
                cur = nxt

        sweep(1)
        K.cx.barrier()
        for k in range(16):
            K.ts("dve", wo[:, k, :], wo[:, k, :], nw[:, k:k + 1], None, ALU.mult, None, [bwo, bnw], [bwo])
        sweep(0)


def host_constants():
    C = {}
    C["c_ident"] = np.eye(128, dtype=np.float32)
    nf = 32
    inv = (10000.0 ** (-np.arange(nf, dtype=np.float32) / nf)).astype(np.float32)
    pos = np.arange(T)
    rpos = (pos // 64).astype(np.float32)
    cpos = (pos % 64).astype(np.float32)
    ang_r = (rpos[:, None] * inv[None, :]).astype(np.float32)
    ang_c = (cpos[:, None] * inv[None, :]).astype(np.float32)
    cr, sr, cc, sc = np.cos(ang_r), np.sin(ang_r), np.cos(ang_c), np.sin(ang_c)
    Cq = np.concatenate([cr, cr, cc, cc], axis=1).astype(np.float32)
    Sq = np.concatenate([-sr, sr, -sc, sc], axis=1).astype(np.float32)
    ks = np.float32(128 ** -0.5)
    rope = np.stack([Cq, Sq, Cq * ks, Sq * ks], axis=1)
    C["c_rope"] = np.ascontiguousarray(rope.reshape(NTL, 128, 4, 128)).astype(np.float32)
    j = np.arange(128)[:, None].astype(np.float32)
    i = np.arange(128)[None, :].astype(np.float32)
    tri = np.zeros((128, 8, 128), np.float32)
    tri[:, 0] = np.maximum(i - j, 0)
    tri[:, 1] = (i >= j)
    tri[:, 2] = np.maximum(j - i, 0)
    tri[:, 3] = (j >= i)
    tri[:, 4] = i + 1 + 0 * j
    tri[:, 5] = 128 - i + 0 * j
    tri[:, 6] = np.where(i >= j, 0.0, NEG)
    tri[:, 7] = np.where(j >= i, 0.0, NEG)
    C["c_tri"] = tri
    col = np.zeros((128, 2), np.float32)
    col[:, 0] = 127 - np.arange(128)
    col[:, 1] = np.arange(128)
    C["c_col"] = col
    return C


def na_bias_tiles(rpb):
    kl = (np.arange(128) // 64)[:, None]
    kc = (np.arange(128) % 64)[:, None]
    ql = (np.arange(128) // 64)[None, :]
    qc = (np.arange(128) % 64)[None, :]
    cstart = np.clip(qc - 8, 0, 48)
    colok = (kc >= cstart) & (kc < cstart + 16)
    variants = [(-3, False), (-2, False), (-2, True), (-1, False), (0, False), (1, False), (2, True),
                (2, False), (3, False)]
    out = np.full((128, 8, 9, 128), NEG, np.float32)
    for vi, (dt, interior) in enumerate(variants):
        dr = 2 * dt + kl - ql
        ok = colok & (dr >= -7) & (dr <= 7)
        if interior:
            ok = ok & (dr >= -4) & (dr <= 3)
        dri = np.clip(dr + 7, 0, 14)
        dci = np.clip(kc - qc + 15, 0, 30)
        dri, dci = np.broadcast_arrays(dri, dci)
        for h in range(8):
            g = rpb[h][dri, dci]
            out[:, h, vi, :] = np.where(ok, g, np.float32(NEG))
    return out


def make_in_maps(inputs):
    I = {k: np.asarray(v) for k, v in inputs.items()}
    C = host_constants()
    shared = dict(C)
    shared["c_ctx"] = I["c_ctx"].reshape(8, 128)
    shared["w_mod"] = I["w_mod"]
    shared["b_mod"] = I["b_mod"].reshape(2, 48, 128)
    shared["norm_mix"] = I["norm_mix"].reshape(2, 8, 128)
    shared["norm_mlp"] = I["norm_mlp"].reshape(2, 8, 128)
    def pkn(a, nk):
        return np.ascontiguousarray(a.reshape(nk, 128, a.shape[-1]).transpose(1, 0, 2))
    shared["w_mlp_in"] = np.stack([pkn(I["w_mlp_in"][l], 8) for l in range(2)])
    shared["w_mlp_out"] = np.stack([pkn(I["w_mlp_out"][l], 32) for l in range(2)])
    shared["w_in_even"] = pkn(I["w_in_even"][0], 8)
    shared["w_out_even"] = pkn(I["w_out_even"][0], 8)
    shared["ret_decay_logit"] = I["ret_decay_logit"].reshape(8)
    shared["na_bias"] = na_bias_tiles(I["na_rpb"][0])
    shared["w_in_odd"] = pkn(I["w_in_odd"][0], 8)
    shared["conv_w"] = I["conv_w"][0].reshape(7, 24, 128)
    shared["conv_b"] = I["conv_b"][0].reshape(24, 128)
    shared["dt_bias"] = I["dt_bias"].reshape(64)
    shared["a_log"] = I["a_log"].reshape(64)
    shared["d_skip"] = I["d_skip"].reshape(32)
    shared["ssm_norm"] = I["ssm_norm"].reshape(16, 128)
    shared["w_out_odd"] = pkn(I["w_out_odd"][0], 16)
    shared["norm_final"] = I["norm_final"].reshape(D)
    shared = {k: np.ascontiguousarray(v, dtype=np.float32) for k, v in shared.items()}
    maps = []
    for b in range(8):
        m = dict(shared)
        m["x"] = np.ascontiguousarray(I["x"][b], dtype=np.float32)
        m["ctx"] = np.ascontiguousarray(I["ctx"][b], dtype=np.float32)
        m["c"] = np.ascontiguousarray(I["c"][b].reshape(8, 128), dtype=np.float32)
        maps.append(m)
    return maps


def build(dbg=False):
    K = Kern(dbg)
    setup(K)
    phase_mod(K)
    phase_l0_ret_proj(K)
    phase_l0_na(K)
    phase_l0_ret(K)
    phase_mlp(K, 0, "X1", "X2", list(range(NT)))
    phase_l1_proj(K)
    phase_l1_conv(K)
    phase_l1_scan(K)
    phase_mlp(K, 1, "X3", "Y", list(range(NTC, NT)), final=True)
    K.cx.barrier()
    return K


_CACHE = {}


def kernel(**inputs):
    if "K" not in _CACHE:
        _CACHE["K"] = build()
    K = _CACHE["K"]
    maps = make_in_maps(inputs)
    used = set(K.dram.keys())
    maps = [{k: v for k, v in m.items() if k in used} for m in maps]
    res = run_bass_kernel_spmd(K.nc, maps, core_ids=list(range(8)))
    return np.stack([np.asarray(r["y"], dtype=np.float32) for r in res.results], axis=0)
```
